# Optimizing a Trainium2 kernel written in Bass

```python
import jax
import jax.numpy as jnp
from jax import lax
import numpy as np

D_MODEL = 2048
BATCH = 1
SEQ = 8192
DEPTH = 1
DEC_BATCH = 1
DEC_SEQ = 16384
PAST_LEN = 128

CONV_DIM = D_MODEL
CONV_K = 31
SSM_DIM = 2 * D_MODEL
HEAD_DIM = 64
N_HEADS = SSM_DIM // HEAD_DIM
N_GROUPS = 8
HEADS_PER_GROUP = N_HEADS // N_GROUPS
D_STATE = 128
SSM_CONV_K = 5
XBC_DIM = SSM_DIM + 2 * N_GROUPS * D_STATE
NORM_GROUP = SSM_DIM // N_GROUPS
CHUNK = 128
EPS = 1e-5
OFF_CV = 0
OFF_CG = OFF_CV + CONV_DIM
OFF_CS = OFF_CG + CONV_DIM
OFF_Z = OFF_CS + CONV_DIM
OFF_XBC = OFF_Z + SSM_DIM
OFF_DT = OFF_XBC + XBC_DIM
OFF_GATE = OFF_DT + 2 * N_HEADS
IN_DIM = OFF_GATE + 2 * D_MODEL

kernel_name = "bidir_conformer_conv_mamba2_gated_hybrid"


def rms_norm(x, w):
    xf = x.astype(jnp.float32)
    y = xf * lax.rsqrt(jnp.mean(xf * xf, axis=-1, keepdims=True) + EPS)
    return (y * w.astype(jnp.float32)).astype(x.dtype)


def layer_norm(x, g, b):
    xf = x.astype(jnp.float32)
    mu = jnp.mean(xf, axis=-1, keepdims=True)
    xc = xf - mu
    y = xc * lax.rsqrt(jnp.mean(xc * xc, axis=-1, keepdims=True) + EPS)
    return (y * g.astype(jnp.float32) + b.astype(jnp.float32)).astype(x.dtype)


def depthwise_conv(x, w, b):
    k = w.shape[0]
    y = lax.conv_general_dilated(
        x, w[:, None, :].astype(x.dtype), window_strides=(1,),
        padding=[(k // 2, k // 2)], dimension_numbers=('NWC', 'WIO', 'NWC'),
        feature_group_count=x.shape[-1])
    return y + b.astype(x.dtype)


def ssd_scan(x, dt, a, bm, cm):
    b, s = x.shape[0], x.shape[1]
    nc = s // CHUNK
    xr = x.reshape(b, nc, CHUNK, N_GROUPS, HEADS_PER_GROUP, HEAD_DIM).transpose(1, 0, 2, 3, 4, 5)
    dtr = dt.reshape(b, nc, CHUNK, N_GROUPS, HEADS_PER_GROUP).transpose(1, 0, 2, 3, 4)
    br = bm.reshape(b, nc, CHUNK, N_GROUPS, D_STATE).transpose(1, 0, 2, 3, 4)
    cr = cm.reshape(b, nc, CHUNK, N_GROUPS, D_STATE).transpose(1, 0, 2, 3, 4)
    ag = a.reshape(N_GROUPS, HEADS_PER_GROUP)
    mask = jnp.tril(jnp.ones((CHUNK, CHUNK), dtype=bool))[None, :, :, None, None]

    def step(state, inp):
        xc, dtc, bc, cc = inp
        cs = jnp.cumsum(dtc * ag, axis=1)
        seg = cs[:, :, None] - cs[:, None]
        decay = jnp.exp(jnp.where(mask, seg, -jnp.inf))
        cb = jnp.einsum('blgn,bsgn->blsg', cc, bc)
        y = jnp.einsum('blsg,blsgh,bsgh,bsghp->blghp', cb, decay, dtc, xc)
        y = y + jnp.einsum('blgn,bghpn,blgh->blghp', cc, state, jnp.exp(cs))
        last = cs[:, -1]
        w = jnp.exp(last[:, None] - cs) * dtc
        state = state * jnp.exp(last)[..., None, None] + jnp.einsum('bsgn,bsgh,bsghp->bghpn', bc, w, xc)
        return state, y

    state0 = jnp.zeros((b, N_GROUPS, HEADS_PER_GROUP, HEAD_DIM, D_STATE), jnp.float32)
    _, ys = lax.scan(step, state0, (xr, dtr, br, cr))
    return ys.transpose(1, 0, 2, 3, 4, 5).reshape(b, s, N_HEADS, HEAD_DIM)


def encoder_layer(x, norm_w, w_in, b_gate, dw_w, dw_b, ln_g, ln_b, sconv_w, sconv_b,
                  dt_bias, a_log, d_skip, ssm_norm_w, w_branch, w_out):
    bsz, s = x.shape[0], x.shape[1]
    f32 = jnp.float32
    h = rms_norm(x, norm_w)
    proj = jnp.einsum('bsd,de->bse', h, w_in)
    cv = proj[..., OFF_CV:OFF_CG]
    cg = proj[..., OFF_CG:OFF_CS]
    cgate = proj[..., OFF_CS:OFF_Z]
    z = proj[..., OFF_Z:OFF_XBC]
    xbc = proj[..., OFF_XBC:OFF_DT]
    dt_raw = proj[..., OFF_DT:OFF_GATE]
    gate_raw = proj[..., OFF_GATE:]

    u = cv * jax.nn.sigmoid(cg)
    u = depthwise_conv(u, dw_w, dw_b)
    u = jax.nn.silu(layer_norm(u, ln_g, ln_b))
    y_c = u * jax.nn.silu(cgate)

    xbc = jax.nn.silu(depthwise_conv(xbc, sconv_w, sconv_b))
    xs = xbc[..., :SSM_DIM].astype(f32).reshape(bsz, s, N_HEADS, HEAD_DIM)
    bm = xbc[..., SSM_DIM:SSM_DIM + N_GROUPS * D_STATE].astype(f32).reshape(bsz, s, N_GROUPS, D_STATE)
    cm = xbc[..., SSM_DIM + N_GROUPS * D_STATE:].astype(f32).reshape(bsz, s, N_GROUPS, D_STATE)
    dt = jax.nn.softplus(dt_raw.astype(f32).reshape(bsz, s, 2, N_HEADS) + dt_bias.astype(f32))
    a = -jnp.exp(a_log.astype(f32))
    flip = lambda t: jnp.flip(t, axis=1)
    y_f = ssd_scan(xs, dt[:, :, 0], a[0], bm, cm)
    y_b = flip(ssd_scan(flip(xs), flip(dt[:, :, 1]), a[1], flip(bm), flip(cm)))
    y = y_f + y_b + xs * d_skip.astype(f32)[:, None]
    y = y.reshape(bsz, s, SSM_DIM) * jax.nn.silu(z.astype(f32))
    yg = y.reshape(bsz, s, N_GROUPS, NORM_GROUP)
    yg = yg * lax.rsqrt(jnp.mean(yg * yg, axis=-1, keepdims=True) + EPS)
    y_s = (yg.reshape(bsz, s, SSM_DIM) * ssm_norm_w.astype(f32)).astype(x.dtype)

    o_c = jnp.einsum('bse,ed->bsd', y_c, w_branch[:CONV_DIM])
    o_s = jnp.einsum('bse,ed->bsd', y_s, w_branch[CONV_DIM:])
    g = jax.nn.sigmoid(gate_raw + b_gate).reshape(bsz, s, 2, D_MODEL)
    m = g[..., 0, :] * o_c + g[..., 1, :] * o_s
    return x + jnp.einsum('bsd,de->bse', m, w_out)


def trunk(x, norm_w, w_in, b_gate, dw_w, dw_b, ln_g, ln_b, sconv_w, sconv_b,
          dt_bias, a_log, d_skip, ssm_norm_w, w_branch, w_out, final_norm_w):
    for l in range(DEPTH):
        x = encoder_layer(x, norm_w[l], w_in[l], b_gate[l], dw_w[l], dw_b[l], ln_g[l], ln_b[l],
                          sconv_w[l], sconv_b[l], dt_bias[l], a_log[l], d_skip[l],
                          ssm_norm_w[l], w_branch[l], w_out[l])
    return rms_norm(x, final_norm_w)


def setup_inputs(seed: int = 0) -> dict:
    key = jax.random.key(seed)
    k = jax.random.split(key, 20)
    f32 = jnp.float32
    nrm = lambda kk, shape, scale: jax.random.normal(kk, shape, f32) * scale
    dt0 = jnp.exp(jax.random.uniform(k[9], (DEPTH, 2, N_HEADS), f32) * (np.log(0.1) - np.log(0.001)) + np.log(0.001))
    dt_bias = dt0 + jnp.log(-jnp.expm1(-dt0))
    return {
        "x_prompt": jax.random.normal(k[0], (BATCH, SEQ, D_MODEL), f32),
        "x_sample": jax.random.normal(k[1], (DEC_BATCH, DEC_SEQ, D_MODEL), f32),
        "norm_w": 1.0 + nrm(k[2], (DEPTH, D_MODEL), 0.02),
        "w_in": nrm(k[3], (DEPTH, D_MODEL, IN_DIM), D_MODEL ** -0.5),
        "b_gate": nrm(k[4], (DEPTH, 2 * D_MODEL), 0.1),
        "dw_w": nrm(k[5], (DEPTH, CONV_K, CONV_DIM), CONV_K ** -0.5),
        "dw_b": nrm(k[6], (DEPTH, CONV_DIM), 0.02),
        "ln_g": 1.0 + nrm(k[7], (DEPTH, CONV_DIM), 0.02),
        "ln_b": nrm(k[8], (DEPTH, CONV_DIM), 0.02),
        "sconv_w": nrm(k[10], (DEPTH, SSM_CONV_K, XBC_DIM), SSM_CONV_K ** -0.5),
        "sconv_b": nrm(k[11], (DEPTH, XBC_DIM), 0.02),
        "dt_bias": dt_bias,
        "a_log": jnp.log(jax.random.uniform(k[12], (DEPTH, 2, N_HEADS), f32, 1.0, 16.0)),
        "d_skip": 1.0 + nrm(k[13], (DEPTH, N_HEADS), 0.1),
        "ssm_norm_w": 1.0 + nrm(k[14], (DEPTH, SSM_DIM), 0.02),
        "w_branch": jnp.concatenate([nrm(k[15], (DEPTH, CONV_DIM, D_MODEL), CONV_DIM ** -0.5),
                                     nrm(k[16], (DEPTH, SSM_DIM, D_MODEL), SSM_DIM ** -0.5)], axis=1),
        "w_out": nrm(k[17], (DEPTH, D_MODEL, D_MODEL), D_MODEL ** -0.5),
        "final_norm_w": 1.0 + nrm(k[18], (D_MODEL,), 0.02),
    }


def reference(x_prompt, x_sample, norm_w, w_in, b_gate, dw_w, dw_b, ln_g, ln_b, sconv_w, sconv_b,
              dt_bias, a_log, d_skip, ssm_norm_w, w_branch, w_out, final_norm_w):
    y_prompt = trunk(x_prompt, norm_w, w_in, b_gate, dw_w, dw_b, ln_g, ln_b, sconv_w, sconv_b,
                     dt_bias, a_log, d_skip, ssm_norm_w, w_branch, w_out, final_norm_w)
    y_sample = trunk(x_sample, norm_w, w_in, b_gate, dw_w, dw_b, ln_g, ln_b, sconv_w, sconv_b,
                     dt_bias, a_log, d_skip, ssm_norm_w, w_branch, w_out, final_norm_w)
    return (y_prompt, y_sample)
```

```python
import numpy as np
from contextlib import ExitStack
import concourse.bass as bass
import concourse.mybir as mybir
from concourse.bass_utils import run_bass_kernel_spmd

F32 = mybir.dt.float32
BF16 = mybir.dt.bfloat16
AF = mybir.ActivationFunctionType
ALU = mybir.AluOpType

D = 2048
KD = 16
CONV_K = 31
SSM_DIM = 4096
NH = 64
HD = 64
NG = 8
DS = 128
OFF_CV = 0
OFF_CG = 2048
OFF_CS = 4096
OFF_Z = 6144
OFF_X = 10240
OFF_B = OFF_X + 4096
OFF_C = OFF_B + 1024
OFF_DT = 16384
OFF_GATE = 16512
IN_DIM = 20608
EPS = 1e-5
PAD = 16
NCORES = 8

PP_DWW = 0
PP_DWB = 496
PP_LNG = 512
PP_LNB = 528
PP_SCW = 544
PP_SCB = 784
PP_BG = 832
PP_SNW = 864
NPP = 896
PB_NW = 0
PB_FNW = 2048
PB_DSK = 0
PB_DTB = 64
PB_ALOG = 192
NPB = 4416


class Buf:
    __slots__ = ("w", "r")

    def __init__(self):
        self.w = None
        self.r = {}


class T:
    def __init__(self, t):
        self.t = t
        self.b = Buf()


class Sched:
    def __init__(self):
        self.ops = {n: [] for n in ("pe", "act", "dve", "pool", "sp")}
        self.cnt = {n: 0 for n in self.ops}
        self.seen = {n: {} for n in self.ops}
        self.dslots = {"sp": 8, "pool": 6}
        self.dcnt = {"sp": 0, "pool": 0}
        self.semkeys = list(self.ops.keys())
        for q, n in self.dslots.items():
            for i in range(n):
                self.semkeys.append(f"{q}_d{i}")
        self.semval = {k: 0 for k in self.semkeys}

    def _collect(self, eng, reads, writes):
        deps = {}

        def need(d):
            if d is None:
                return
            k, v = d
            if k == "pe" and eng == "pe":
                return
            if deps.get(k, 0) < v:
                deps[k] = v

        for b in reads:
            need(b.w)
        for b in writes:
            need(b.w)
            for d in b.r.values():
                need(d)
        return deps

    def _waits(self, eng, deps):
        seen = self.seen[eng]
        for k, v in deps.items():
            if seen.get(k, 0) < v:
                self.ops[eng].append(("wait", k, v))
                seen[k] = v

    def _mark(self, eng, d, reads, writes):
        for b in reads:
            o = b.r.get(d[0])
            if o is None or o[1] < d[1]:
                b.r[d[0]] = d
        for b in writes:
            b.w = d
            b.r = {}

    def op(self, eng, name, kw, reads=(), writes=(), inc=True):
        fn = (name, kw)
        reads = [x.b if isinstance(x, T) else x for x in reads]
        writes = [x.b if isinstance(x, T) else x for x in writes]
        self._waits(eng, self._collect(eng, reads, writes))
        if inc:
            self.cnt[eng] += 1
            self.ops[eng].append(("ins", fn, eng, 1))
            d = (eng, self.cnt[eng])
            self.semval[eng] = self.cnt[eng]
        else:
            self.ops[eng].append(("ins", fn, None, 0))
            d = (eng, self.cnt[eng] + 1)
        self._mark(eng, d, reads, writes)

    def dma(self, eng, kw, reads=(), writes=()):
        fn = ("dma_start", kw)
        reads = [x.b if isinstance(x, T) else x for x in reads]
        writes = [x.b if isinstance(x, T) else x for x in writes]
        i = self.dcnt[eng]
        self.dcnt[eng] += 1
        ns = self.dslots[eng]
        key = f"{eng}_d{i % ns}"
        deps = self._collect(eng, reads, writes)
        prev = 16 * (i // ns)
        if prev > 0 and deps.get(key, 0) < prev:
            deps[key] = prev
        self._waits(eng, deps)
        val = 16 * (i // ns + 1)
        self.ops[eng].append(("ins", fn, key, 16))
        self.semval[key] = val
        self._mark(eng, (key, val), reads, writes)

    def barrier(self):
        for eng in self.ops:
            deps = {k: v for k, v in self.semval.items() if v > 0 and k != eng}
            self._waits(eng, deps)

    def final_wait(self, eng):
        deps = {k: v for k, v in self.semval.items() if v > 0 and k != eng}
        self._waits(eng, deps)


def build(cfg):
    NP, NS = cfg["NP"], cfg["NS"]
    BLK1, BLK2, BLK3 = cfg["BLK1"], cfg["BLK2"], cfg["BLK3"]
    LP, LS = NCORES * NP, NCORES * NS
    NB1P, NB1S = LP // BLK1, LS // BLK1
    NCH_OWN = (NP + NS) // 128

    nc = bass.Bass("TRN2", target_bir_lowering=False)
    dram = {}

    def din(name, shape):
        dram[name] = nc.dram_tensor(name, list(shape), F32, kind="ExternalInput").ap()

    NBO_P = NB1P - NP // BLK1
    NBO_S = NB1S - NS // BLK1
    NM = 4 * (NBO_P + NBO_S)
    din("xoth", (NBO_P + NBO_S, BLK1 + 4, D))
    din("xoP", (NP + 2 * PAD, D))
    din("xoS", (NS + 2 * PAD, D))
    din("w_in", (D, IN_DIM))
    din("w_branch", (6144, D))
    din("w_out", (D, D))
    din("pp", (128, NPP))
    din("pb", (1, NPB))
    din("consts", (128, 512))
    din("masks", (128, NM))
    yP = nc.dram_tensor("yP", [NP, D], F32, kind="ExternalOutput").ap()
    yS = nc.dram_tensor("yS", [NS, D], F32, kind="ExternalOutput").ap()
    acc_scr = nc.dram_tensor("acc_scr", [4, 128, SSM_DIM], F32).ap()
    yf_scr = nc.dram_tensor("yf_scr", [NCH_OWN, 128, SSM_DIM], F32).ap()
    scrbuf = {"acc": [Buf() for _ in range(4)], "yf": [Buf() for _ in range(NCH_OWN)]}

    S = Sched()
    top = ExitStack()
    sems = {}
    for k in S.semkeys:
        sems[k] = top.enter_context(nc.semaphore(k))

    uid = [0]

    def sb(es, name, shape, dt=F32):
        uid[0] += 1
        return T(es.enter_context(nc.sbuf_tensor(f"{name}_{uid[0]}", list(shape), dt)))

    pools = {}
    rr = {}

    def mkpsum(es, **spec):
        assert sum(spec.values()) <= 8
        pools.clear()
        for kind, n in spec.items():
            if kind == "tp":
                pools[kind] = [T(es.enter_context(nc.psum_tensor(f"ps{kind}{i}_{uid[0]}", [128, 1024], BF16))) for i in range(n)]
            else:
                pools[kind] = [T(es.enter_context(nc.psum_tensor(f"ps{kind}{i}_{uid[0]}", [128, 512], F32))) for i in range(n)]
            uid[0] += 1
            rr[kind] = 0

    def psum(kind):
        lst = pools[kind]
        t = lst[rr[kind] % len(lst)]
        rr[kind] += 1
        return t

    class Ring:
        def __init__(self, items):
            self.items = items
            self.i = 0

        def next(self):
            t = self.items[self.i % len(self.items)]
            self.i += 1
            return t

    cst = sb(top, "cst", [128, 512])
    cstb = sb(top, "cstb", [128, 512], BF16)
    ppt = sb(top, "ppt", [128, NPP])
    pbt = sb(top, "pbt", [128, NPB - 4096])
    mskt = sb(top, "mskt", [128, NM])
    abc = sb(top, "abc", [128, 128])
    W = Ring([sb(top, f"W{i}", [128, KD, 512], BF16) for i in range(cfg.get("NW", 3))])

    IDF = lambda: cst.t[:, 0:128]
    TRIF = lambda: cst.t[:, 128:256]
    TRIB = lambda: cst.t[:, 256:384]
    ONES = lambda: cst.t[:, 384:512]
    IDB = lambda: cstb.t[:, 0:128]

    def flush(es_phase=None):
        with nc.Block() as blk:
            def rep(eng):
                def body(h):
                    for e in S.ops[eng]:
                        if e[0] == "wait":
                            h.wait_ge(sems[e[1]], e[2])
                        else:
                            ins = getattr(h, e[1][0])(**e[1][1])
                            if e[2] is not None:
                                ins.then_inc(sems[e[2]], e[3])
                    S.ops[eng] = []
                return body
            blk.tensor(rep("pe"))
            blk.scalar(rep("act"))
            blk.vector(rep("dve"))
            blk.gpsimd(rep("pool"))
            blk.sync(rep("sp"))

    S.dma("sp", dict(out=cst.t[:], in_=dram["consts"]), writes=[cst])
    S.dma("sp", dict(out=ppt.t[:], in_=dram["pp"]), writes=[ppt])
    S.dma("sp", dict(out=pbt.t[:], in_=dram["pb"][:, 4096:NPB].partition_broadcast(128)), writes=[pbt])
    S.dma("sp", dict(out=mskt.t[:], in_=dram["masks"]), writes=[mskt])
    S.op("dve", "tensor_copy", dict(out=cstb.t[:], in_=cst.t[:]), reads=[cst], writes=[cstb])
    S.op("act", "activation", dict(out=abc.t[:], in_=pbt.t[:, PB_ALOG:PB_ALOG + 128], func=AF.Exp),
         reads=[pbt], writes=[abc])
    S.op("dve", "tensor_scalar", dict(out=abc.t[:], in0=abc.t[:], scalar1=-1.0, scalar2=None, op0=ALU.mult),
         reads=[abc], writes=[abc])

    def load_w(mat, r0, c0, ncols, coff=0, wt=None):
        if wt is None:
            wt = W.next()
        src = mat[r0:r0 + 2048, c0:c0 + ncols].rearrange("(k p) c -> p k c", p=128)
        S.dma("pool", dict(out=wt.t[:, :, coff:coff + ncols], in_=src), writes=[wt])
        return wt

    def load_norm_T(es, src, nrows, hT, st):
        xt_r, xn_r, sm_r = st["xt"], st["xn"], st["sm"]
        for r0 in range(0, nrows, 128):
            R = min(128, nrows - r0)
            xt = xt_r.next()
            xn = xn_r.next()
            sm = sm_r.next()
            S.dma("sp", dict(out=xt.t[0:R, :], in_=src[r0:r0 + R, :]), writes=[xt])
            S.op("act", "activation", dict(
                out=xn.t[0:R, :], in_=xt.t[0:R, :], func=AF.Square, accum_out=sm.t[0:R, 0:1]),
                reads=[xt], writes=[xn, sm])
            S.op("dve", "tensor_scalar", dict(
                out=sm.t[0:R, 1:2], in0=sm.t[0:R, 0:1], scalar1=1.0 / D, scalar2=EPS, op0=ALU.mult, op1=ALU.add),
                reads=[sm], writes=[sm])
            S.op("act", "activation", dict(out=sm.t[0:R, 2:3], in_=sm.t[0:R, 1:2], func=AF.Sqrt),
                 reads=[sm], writes=[sm])
            S.op("dve", "reciprocal", dict(out=sm.t[0:R, 3:4], in_=sm.t[0:R, 2:3]),
                 reads=[sm], writes=[sm])
            S.op("dve", "scalar_tensor_tensor", dict(
                out=xn.t[0:R, :], in0=xt.t[0:R, :], scalar=sm.t[0:R, 3:4], in1=st["nw"].t[0:R, :],
                op0=ALU.mult, op1=ALU.mult), reads=[xt, sm, st["nw"]], writes=[xn])
            for half in range(2):
                tp = psum("tp")
                for j in range(8):
                    k = half * 8 + j
                    S.op("pe", "transpose", dict(
                        out=tp.t[:, j * 128:j * 128 + R], in_=xn.t[0:R, k * 128:(k + 1) * 128],
                        identity=cstb.t[0:R, 0:R]), reads=[xn, cstb], writes=[tp], inc=(j == 7))
                eng = "act" if half == 0 else "dve"
                src_v = tp.t[:].rearrange("p (j t) -> p j t", j=8)[:, :, 0:R]
                dst_v = hT.t[:, half * 8:half * 8 + 8, r0:r0 + R]
                if eng == "act":
                    S.op("act", "activation", dict(out=dst_v, in_=src_v, func=AF.Copy),
                         reads=[tp], writes=[hT])
                else:
                    S.op("dve", "tensor_copy", dict(out=dst_v, in_=src_v),
                         reads=[tp], writes=[hT])

    def proj_fm(wt, co, hT, t0, n):
        ps = psum("mm")
        for k in range(KD):
            S.op("pe", "matmul", dict(out=ps.t[:, 0:n], lhsT=wt.t[:, k, co:co + 128], rhs=hT.t[:, k, t0:t0 + n],
                start=(k == 0), stop=(k == KD - 1)), reads=[wt, hT], writes=[ps], inc=(k == KD - 1))
        return ps

    def proj_tm(wt, co, ncols, hT, t0, ps, po):
        for k in range(KD):
            S.op("pe", "matmul", dict(out=ps.t[:, po:po + ncols], lhsT=hT.t[:, k, t0:t0 + 128], rhs=wt.t[:, k, co:co + ncols],
                start=(k == 0), stop=(k == KD - 1)), reads=[wt, hT], writes=[ps], inc=(k == KD - 1))

    def subblocks(t0, n):
        out = []
        while n > 0:
            m = min(512, n)
            out.append((t0, m))
            t0 += m
            n -= m
        return out

    def conv5_silu(es_bufs, wt, co, hT, tlo, Tn, pidx, outT, out_off=0):
        pre = es_bufs["pre"].next()
        acc = es_bufs["acc"].next()
        for (t0, n) in subblocks(tlo - 2, Tn + 4):
            ps = proj_fm(wt, co, hT, t0, n)
            o = t0 - (tlo - 2)
            S.op("act", "activation", dict(out=pre.t[:, o:o + n], in_=ps.t[:, 0:n], func=AF.Copy),
                 reads=[ps], writes=[pre])
        wbase = PP_SCW + pidx * 5
        S.op("dve", "tensor_scalar", dict(out=acc.t[:, 0:Tn], in0=pre.t[:, 0:Tn], scalar1=ppt.t[:, wbase:wbase + 1],
                                              scalar2=None, op0=ALU.mult), reads=[pre, ppt], writes=[acc])
        for k in range(1, 5):
            S.op("dve", "scalar_tensor_tensor", dict(
                out=acc.t[:, 0:Tn], in0=pre.t[:, k:k + Tn], scalar=ppt.t[:, wbase + k:wbase + k + 1],
                in1=acc.t[:, 0:Tn], op0=ALU.mult, op1=ALU.add), reads=[pre, ppt, acc], writes=[acc])
        S.op("act", "activation", dict(out=outT.t[:, out_off:out_off + Tn], in_=acc.t[:, 0:Tn], func=AF.Silu,
                                           bias=ppt.t[:, PP_SCB + pidx:PP_SCB + pidx + 1]),
             reads=[acc, ppt], writes=[outT])

    def dt_block(hT, tlo, nch, db, want_T):
        wt = load_w(dram["w_in"], 0, OFF_DT, 128)
        for c4 in range(0, nch, 4):
            n4 = min(4, nch - c4)
            ps = psum("sm")
            for i in range(n4):
                proj_tm(wt, 0, 128, hT, tlo + (c4 + i) * 128, ps, i * 128)
            sl = lambda t: t.t[:, c4:c4 + n4, :]
            psv = ps.t[:, 0:n4 * 128].rearrange("p (c f) -> p c f", c=n4)
            bias_v = pbt.t[:, PB_DTB:PB_DTB + 128].unsqueeze(1).to_broadcast([128, n4, 128])
            a_v = abc.t[:].unsqueeze(1).to_broadcast([128, n4, 128])
            S.op("dve", "tensor_tensor", dict(out=sl(db["A"]), in0=psv, in1=bias_v, op=ALU.add),
                 reads=[ps, pbt], writes=[db["A"]])
            S.op("dve", "scalar_tensor_tensor", dict(out=sl(db["dt"]), in0=sl(db["A"]), scalar=-1.0, in1=sl(db["A"]),
                                                     op0=ALU.mult, op1=ALU.max),
                 reads=[db["A"]], writes=[db["dt"]])
            S.op("act", "activation", dict(out=sl(db["dt"]), in_=sl(db["dt"]), func=AF.Exp, scale=-1.0),
                 reads=[db["dt"]], writes=[db["dt"]])
            S.op("act", "activation", dict(out=sl(db["dt"]), in_=sl(db["dt"]), func=AF.Ln, bias=1.0),
                 reads=[db["dt"]], writes=[db["dt"]])
            S.op("dve", "scalar_tensor_tensor", dict(out=sl(db["dt"]), in0=sl(db["A"]), scalar=0.0, in1=sl(db["dt"]),
                                                         op0=ALU.max, op1=ALU.add),
                 reads=[db["A"], db["dt"]], writes=[db["dt"]])
            S.op("dve", "tensor_tensor", dict(out=sl(db["A"]), in0=sl(db["dt"]), in1=a_v, op=ALU.mult),
                 reads=[db["dt"], abc], writes=[db["A"]])
        for c4 in range(0, nch, 4):
            n4 = min(4, nch - c4)
            ps = psum("sm")
            ps2 = psum("mm")
            for i in range(n4):
                c = c4 + i
                S.op("pe", "matmul", dict(out=ps.t[:, i * 128:i * 128 + 64], lhsT=TRIF(), rhs=db["A"].t[:, c, 0:64],
                                                        start=True, stop=True), reads=[cst, db["A"]], writes=[ps], inc=False)
                S.op("pe", "matmul", dict(out=ps.t[:, i * 128 + 64:i * 128 + 128], lhsT=TRIB(),
                                                        rhs=db["A"].t[:, c, 64:128], start=True, stop=True),
                     reads=[cst, db["A"]], writes=[ps], inc=False)
                S.op("pe", "matmul", dict(out=ps2.t[:, i * 128:i * 128 + 128], lhsT=ONES(), rhs=db["A"].t[:, c, :],
                                                        start=True, stop=True), reads=[cst, db["A"]], writes=[ps2],
                     inc=(i == n4 - 1))
            S.op("act", "activation", dict(
                out=db["cs"].t[:, c4:c4 + n4, :], in_=ps.t[:, 0:n4 * 128].rearrange("p (c f) -> p c f", c=n4), func=AF.Copy),
                reads=[ps], writes=[db["cs"]])
            S.op("dve", "tensor_copy", dict(
                out=db["tot"].t[:, c4:c4 + n4, :], in_=ps2.t[:, 0:n4 * 128].rearrange("p (c f) -> p c f", c=n4)),
                reads=[ps2], writes=[db["tot"]])
            if want_T:
                ps3 = psum("ck")
                for i in range(n4):
                    c = c4 + i
                    S.op("pe", "matmul", dict(out=ps3.t[0:64, i * 128:(i + 1) * 128], lhsT=db["A"].t[:, c, 0:64],
                                                            rhs=TRIF(), start=True, stop=True),
                         reads=[cst, db["A"]], writes=[ps3], inc=False)
                    S.op("pe", "matmul", dict(out=ps3.t[64:128, i * 128:(i + 1) * 128],
                                                            lhsT=db["A"].t[:, c, 64:128], rhs=TRIB(), start=True, stop=True),
                         reads=[cst, db["A"]], writes=[ps3], inc=(i == n4 - 1))
                S.op("act", "activation", dict(
                    out=db["csT"].t[:, c4:c4 + n4, :], in_=ps3.t[:, 0:n4 * 128].rearrange("p (c f) -> p c f", c=n4),
                    func=AF.Copy), reads=[ps3], writes=[db["csT"]])
        if want_T:
            S.op("act", "activation", dict(out=db["ecs"].t[:], in_=db["cs"].t[:], func=AF.Exp),
                 reads=[db["cs"]], writes=[db["ecs"]])

    def tok_major(srcT, col0, nblk, c, dst, dcol0):
        tp = psum("tp")
        for i in range(nblk):
            S.op("pe", "transpose", dict(out=tp.t[:, i * 128:(i + 1) * 128],
                                                  in_=srcT[i].t[:, col0 + c * 128:col0 + (c + 1) * 128], identity=IDB()),
                 reads=[srcT[i], cstb], writes=[tp], inc=(i == nblk - 1))
        S.op("act", "activation", dict(out=dst.t[:, c, dcol0:dcol0 + nblk * 128], in_=tp.t[:, 0:nblk * 128], func=AF.Copy),
             reads=[tp], writes=[dst])

    def interleave(ga, gb, ratio=2):
        if not cfg.get("IL", 1):
            for g_ in (gb, ga):
                if g_ is not None:
                    for _ in g_:
                        pass
            return
        alive_a, alive_b = ga is not None, gb is not None
        while alive_a or alive_b:
            if alive_a:
                try:
                    next(ga)
                except StopIteration:
                    alive_a = False
            for _ in range(ratio):
                if alive_b:
                    try:
                        next(gb)
                    except StopIteration:
                        alive_b = False

    def nw_load(es, st, off):
        st["nw"] = sb(es, "nwbc", [128, D])
        S.dma("sp", dict(out=st["nw"].t[:], in_=dram["pb"][:, off:off + D].partition_broadcast(128)), writes=[st["nw"]])

    def group_set(es, nch, tag, need_C):
        Tn = nch * 128
        d = {"xgT": [sb(es, f"xgT{tag}{i}", [128, Tn], BF16) for i in range(4)],
             "BgT": sb(es, f"BgT{tag}", [128, Tn], BF16),
             "xg_tok": sb(es, f"xg_tok{tag}", [128, nch, 512], BF16),
             "Bg_tok": sb(es, f"Bg_tok{tag}", [128, nch, 128], BF16)}
        if need_C:
            d["CgT"] = sb(es, f"CgT{tag}", [128, Tn], BF16)
        return d

    def conv_gen(cbufs, gs, g, hT, tlo, nch, need_C):
        Tn = nch * 128
        wx_ = load_w(dram["w_in"], 0, OFF_X + g * 512, 512)
        wbc = load_w(dram["w_in"], 0, OFF_B + g * 128, 128)
        if need_C:
            load_w(dram["w_in"], 0, OFF_C + g * 128, 128, coff=128, wt=wbc)
        for i in range(4):
            conv5_silu(cbufs, wx_, i * 128, hT, tlo, Tn, g * 4 + i, gs["xgT"][i])
            yield
        conv5_silu(cbufs, wbc, 0, hT, tlo, Tn, 32 + g, gs["BgT"])
        yield
        if need_C:
            conv5_silu(cbufs, wbc, 128, hT, tlo, Tn, 40 + g, gs["CgT"])
            yield
        for c in range(nch):
            tok_major(gs["xgT"], 0, 4, c, gs["xg_tok"], 0)
            tok_major([gs["BgT"]], 0, 1, c, gs["Bg_tok"], 0)
            if c % 2 == 1:
                yield

    NCH1 = BLK1 // 128
    TH1 = BLK1 + 4
    with ExitStack() as es:
        mkpsum(es, mm=2, tp=2, sm=2, ck=2)
        hT = sb(es, "hT1", [128, KD, TH1], BF16)
        st = {"xt": Ring([sb(es, f"xt{i}", [128, D]) for i in range(2)]),
              "xn": Ring([sb(es, f"xn{i}", [128, D], BF16) for i in range(2)]),
              "sm": Ring([sb(es, f"sm{i}", [128, 8]) for i in range(2)])}
        nw_load(es, st, PB_NW)
        cb = {"pre": Ring([sb(es, f"pre{i}", [128, TH1]) for i in range(2)]),
              "acc": Ring([sb(es, f"cacc{i}", [128, BLK1]) for i in range(2)])}
        db = {k: sb(es, "db_" + k, [128, NCH1, 128]) for k in ("dt", "A", "cs", "tot")}
        Et = sb(es, "Et", [128, NCH1, 128])
        wt_ = sb(es, "wt_", [128, NCH1, 128])
        Dt = sb(es, "Dt", [128, 128])
        aF = sb(es, "aF", [128, 64])
        cB = sb(es, "cB", [128, 64])
        pcum = sb(es, "pcum", [128, 64])
        tmpd = sb(es, "tmpd", [128, 64])
        gsets = [group_set(es, NCH1, f"a{i}", False) for i in range(2)]
        wx = Ring([sb(es, f"wx{i}", [128, 512], BF16) for i in range(4)])
        tmpL = Ring([sb(es, f"tmpL{i}", [128, 512]) for i in range(2)])
        RF = sb(es, "RF", [128, SSM_DIM])
        RB = sb(es, "RB", [128, SSM_DIM])
        v3 = lambda ap: ap.rearrange("p (h d) -> p h d", h=8)

        def tail_gen(gs, g, mfa, cBt, aFt):
            xg_tok, Bg_tok = gs["xg_tok"], gs["Bg_tok"]
            Lf = psum("ck")
            Lb = psum("ck")
            for c in range(NCH1):
                for di, Lps in ((0, Lf), (1, Lb)):
                    wxt = wx.next()
                    S.op("dve", "tensor_tensor", dict(
                        out=v3(wxt.t[:]), in0=v3(xg_tok.t[:, c, :]),
                        in1=wt_.t[:, c, di * 64 + g * 8:di * 64 + g * 8 + 8].unsqueeze(2).to_broadcast([128, 8, 64]),
                        op=ALU.mult), reads=[xg_tok, wt_], writes=[wxt])
                    S.op("pe", "matmul", dict(out=Lps.t[:, :], lhsT=Bg_tok.t[:, c, :], rhs=wxt.t[:],
                                              start=(c == 0), stop=(c == NCH1 - 1)),
                         reads=[Bg_tok, wxt], writes=[Lps], inc=True)
                yield
            gsl = slice(g * 512, (g + 1) * 512)
            S.op("dve", "tensor_tensor", dict(
                out=v3(RF.t[:, gsl]), in0=v3(RF.t[:, gsl]),
                in1=aFt.t[:, g * 8:g * 8 + 8].unsqueeze(2).to_broadcast([128, 8, 64]), op=ALU.mult),
                reads=[RF, aFt], writes=[RF])
            S.op("dve", "scalar_tensor_tensor", dict(
                out=RF.t[:, gsl], in0=Lf.t[:, :], scalar=mfa, in1=RF.t[:, gsl], op0=ALU.mult, op1=ALU.add),
                reads=[Lf, RF, mskt], writes=[RF])
            tl = tmpL.next()
            S.op("dve", "tensor_tensor", dict(
                out=v3(tl.t[:]), in0=v3(Lb.t[:, :]),
                in1=cBt.t[:, g * 8:g * 8 + 8].unsqueeze(2).to_broadcast([128, 8, 64]), op=ALU.mult),
                reads=[Lb, cBt], writes=[tl])
            S.op("dve", "tensor_tensor", dict(out=RB.t[:, gsl], in0=RB.t[:, gsl], in1=tl.t[:], op=ALU.add),
                 reads=[RB, tl], writes=[RB])

        blk_i = 0
        for si, nblk in enumerate((NBO_P, NBO_S) if cfg.get("P1", 1) else ()):
            S.op("dve", "memset", dict(ap=RF.t[:], constant=0.0), writes=[RF])
            S.op("dve", "memset", dict(ap=RB.t[:], constant=0.0), writes=[RB])
            S.op("dve", "memset", dict(ap=pcum.t[:], constant=1.0), writes=[pcum])
            for j in range(nblk):
                mc = 4 * blk_i
                mf = mskt.t[:, mc:mc + 1]
                omf = mskt.t[:, mc + 1:mc + 2]
                mb = mskt.t[:, mc + 2:mc + 3]
                omb = mskt.t[:, mc + 3:mc + 4]
                load_norm_T(es, dram["xoth"][blk_i], TH1, hT, st)
                blk_i += 1
                dt_block(hT, 2, NCH1, db, want_T=False)
                for c in range(NCH1):
                    ps = psum("sm")
                    for c2 in range(c, NCH1):
                        S.op("pe", "matmul", dict(out=ps.t[:, 0:64], lhsT=ONES(), rhs=db["A"].t[:, c2, 0:64],
                                                  start=(c2 == c), stop=(c2 == NCH1 - 1)),
                             reads=[cst, db["A"]], writes=[ps], inc=False)
                    for c2 in range(0, c + 1):
                        S.op("pe", "matmul", dict(out=ps.t[:, 64:128], lhsT=ONES(), rhs=db["A"].t[:, c2, 64:128],
                                                  start=(c2 == 0), stop=(c2 == c)),
                             reads=[cst, db["A"]], writes=[ps], inc=(c2 == c))
                    S.op("act", "activation", dict(out=Et.t[:, c, :], in_=ps.t[:, 0:128], func=AF.Copy),
                         reads=[ps], writes=[Et])
                S.op("dve", "tensor_tensor", dict(out=wt_.t[:], in0=Et.t[:], in1=db["cs"].t[:], op=ALU.subtract),
                     reads=[Et, db["cs"]], writes=[wt_])
                S.op("act", "activation", dict(out=wt_.t[:], in_=wt_.t[:], func=AF.Exp), reads=[wt_], writes=[wt_])
                S.op("dve", "tensor_tensor", dict(out=wt_.t[:], in0=wt_.t[:], in1=db["dt"].t[:], op=ALU.mult),
                     reads=[wt_, db["dt"]], writes=[wt_])
                S.op("act", "activation", dict(out=Dt.t[:, 0:64], in_=Et.t[:, 0, 0:64], func=AF.Exp),
                     reads=[Et], writes=[Dt])
                S.op("act", "activation", dict(out=Dt.t[:, 64:128], in_=Et.t[:, NCH1 - 1, 64:128], func=AF.Exp),
                     reads=[Et], writes=[Dt])
                S.op("dve", "tensor_scalar", dict(out=aF.t[:], in0=Dt.t[:, 0:64], scalar1=mf, scalar2=omf,
                                                  op0=ALU.mult, op1=ALU.add), reads=[Dt, mskt], writes=[aF])
                S.op("dve", "tensor_scalar", dict(out=cB.t[:], in0=pcum.t[:], scalar1=mb, scalar2=None, op0=ALU.mult),
                     reads=[pcum, mskt], writes=[cB])
                S.op("dve", "tensor_scalar", dict(out=tmpd.t[:], in0=Dt.t[:, 64:128], scalar1=mb, scalar2=omb,
                                                  op0=ALU.mult, op1=ALU.add), reads=[Dt, mskt], writes=[tmpd])
                S.op("dve", "tensor_tensor", dict(out=pcum.t[:], in0=pcum.t[:], in1=tmpd.t[:], op=ALU.mult),
                     reads=[pcum, tmpd], writes=[pcum])
                prev = None
                for g in range(NG):
                    gs = gsets[g % 2]
                    interleave(conv_gen(cb, gs, g, hT, 2, NCH1, False), prev, ratio=1)
                    prev = tail_gen(gs, g, mf, cB, aF)
                interleave(None, prev)
            S.dma("sp", dict(out=acc_scr[2 * si], in_=RF.t[:]), reads=[RF], writes=[scrbuf["acc"][2 * si]])
            S.dma("sp", dict(out=acc_scr[2 * si + 1], in_=RB.t[:]), reads=[RB], writes=[scrbuf["acc"][2 * si + 1]])
        S.barrier()
        flush()

    def chunk_gen(bufs, gs, g, di, nch, db, carry, chunk_cb):
        xg_tok, Bg_tok, BgT, CgT = gs["xg_tok"], gs["Bg_tok"], gs["BgT"], gs["CgT"]
        Sbf = bufs["Sbf"]
        S.op("act", "activation", dict(out=Sbf.t[:], in_=carry.t[:], func=AF.Copy), reads=[carry], writes=[Sbf])
        order = range(nch) if di == 0 else range(nch - 1, -1, -1)
        mask = TRIF if di == 0 else TRIB
        hb = di * 64 + g * 8
        v3 = lambda ap: ap.rearrange("p (h d) -> p h d", h=8)
        for c in order:
            cs_ = slice(c * 128, (c + 1) * 128)
            ps = psum("sm")
            S.op("pe", "matmul", dict(out=ps.t[:, 0:128], lhsT=BgT.t[:, cs_], rhs=CgT.t[:, cs_], start=True, stop=True),
                 reads=[BgT, CgT], writes=[ps])
            CBm = bufs["CBm"].next()
            S.op("dve", "tensor_tensor", dict(out=CBm.t[:], in0=ps.t[:, 0:128], in1=mask(), op=ALU.mult),
                 reads=[ps, cst], writes=[CBm])
            ncs = bufs["ncs"].next()
            S.op("dve", "tensor_scalar", dict(out=ncs.t[:, 0:8], in0=db["cs"].t[:, c, hb:hb + 8], scalar1=-1.0, scalar2=None,
                                              op0=ALU.mult), reads=[db["cs"]], writes=[ncs])
            wv = bufs["wv"].next()
            S.op("dve", "tensor_tensor", dict(out=wv.t[:, 0:8], in0=db["tot"].t[:, c, hb:hb + 8],
                                              in1=db["cs"].t[:, c, hb:hb + 8], op=ALU.subtract),
                 reads=[db["tot"], db["cs"]], writes=[wv])
            S.op("act", "activation", dict(out=wv.t[:, 0:8], in_=wv.t[:, 0:8], func=AF.Exp), reads=[wv], writes=[wv])
            S.op("dve", "tensor_tensor", dict(out=wv.t[:, 0:8], in0=wv.t[:, 0:8], in1=db["dt"].t[:, c, hb:hb + 8], op=ALU.mult),
                 reads=[wv, db["dt"]], writes=[wv])
            S.op("act", "activation", dict(out=wv.t[:, 8:16], in_=db["tot"].t[:, c, hb:hb + 8], func=AF.Exp),
                 reads=[db["tot"]], writes=[wv])
            wxt = bufs["wx"].next()
            S.op("dve", "tensor_tensor", dict(
                out=v3(wxt.t[:]), in0=v3(xg_tok.t[:, c, :]), in1=wv.t[:, 0:8].unsqueeze(2).to_broadcast([128, 8, 64]),
                op=ALU.mult), reads=[xg_tok, wv], writes=[wxt])
            psss = []
            for hq in range(2):
                pss = psum("sel")
                psss.append(pss)
                for hh in range(4):
                    hi = hq * 4 + hh
                    S.op("pe", "matmul", dict(out=pss.t[:, hh * 128:(hh + 1) * 128],
                                              lhsT=cst.t[:, hb + hi:hb + hi + 1].to_broadcast([128, 128]),
                                              rhs=db["csT"].t[:, c, :], start=True, stop=True),
                         reads=[cst, db["csT"]], writes=[pss], inc=(hh == 3))
            yield
            Yps = psum("ck")
            for hq in range(2):
                pss = psss[hq]
                for hh in range(4):
                    hi = hq * 4 + hh
                    sg = bufs["seg"].next()
                    S.op("dve", "tensor_scalar", dict(out=sg.t[:], in0=pss.t[:, hh * 128:(hh + 1) * 128],
                                                      scalar1=ncs.t[:, hi:hi + 1], scalar2=0.0, op0=ALU.add, op1=ALU.min),
                         reads=[pss, ncs], writes=[sg])
                    S.op("act", "activation", dict(out=sg.t[:], in_=sg.t[:], func=AF.Exp), reads=[sg], writes=[sg])
                    Mt = bufs["M"].next()
                    S.op("dve", "scalar_tensor_tensor", dict(out=Mt.t[:], in0=sg.t[:], scalar=db["dt"].t[:, c, hb + hi:hb + hi + 1],
                                                             in1=CBm.t[:], op0=ALU.mult, op1=ALU.mult),
                         reads=[sg, db["dt"], CBm], writes=[Mt])
                    S.op("pe", "matmul", dict(out=Yps.t[:, hi * 64:(hi + 1) * 64], lhsT=Mt.t[:],
                                              rhs=xg_tok.t[:, c, hi * 64:(hi + 1) * 64], start=True, stop=True),
                         reads=[Mt, xg_tok], writes=[Yps])
                yield
            CSps = psum("ck")
            S.op("pe", "matmul", dict(out=CSps.t[:, :], lhsT=CgT.t[:, cs_], rhs=Sbf.t[:], start=True, stop=True),
                 reads=[CgT, Sbf], writes=[CSps])
            Lps = psum("sm")
            S.op("pe", "matmul", dict(out=Lps.t[:, :], lhsT=Bg_tok.t[:, c, :], rhs=wxt.t[:], start=True, stop=True),
                 reads=[Bg_tok, wxt], writes=[Lps])
            S.op("dve", "tensor_tensor", dict(
                out=v3(carry.t[:]), in0=v3(carry.t[:]), in1=wv.t[:, 8:16].unsqueeze(2).to_broadcast([128, 8, 64]),
                op=ALU.mult), reads=[carry, wv], writes=[carry])
            S.op("dve", "tensor_tensor", dict(out=carry.t[:], in0=carry.t[:], in1=Lps.t[:, :], op=ALU.add),
                 reads=[carry, Lps], writes=[carry])
            S.op("act", "activation", dict(out=Sbf.t[:], in_=carry.t[:], func=AF.Copy), reads=[carry], writes=[Sbf])
            yield from chunk_cb(c, Yps, CSps, hb)

    def ssd_bufs(es, nch, tag, nbuf=2):
        Tn = nch * 128
        return {
            "cb": {"pre": Ring([sb(es, f"pre{tag}{i}", [128, Tn + 4]) for i in range(nbuf)]),
                   "acc": Ring([sb(es, f"cacc{tag}{i}", [128, Tn]) for i in range(nbuf)])},
            "Sbf": sb(es, f"Sbf{tag}", [128, 512], BF16),
            "CBm": Ring([sb(es, f"CBm{tag}{i}", [128, 128]) for i in range(2)]),
            "ncs": Ring([sb(es, f"ncs{tag}{i}", [128, 8]) for i in range(2)]),
            "seg": Ring([sb(es, f"seg{tag}{i}", [128, 128]) for i in range(4)]),
            "M": Ring([sb(es, f"M{tag}{i}", [128, 128], BF16) for i in range(4)]),
            "wv": Ring([sb(es, f"wv{tag}{i}", [128, 16]) for i in range(2)]),
            "wx": Ring([sb(es, f"wx{tag}{i}", [128, 512], BF16) for i in range(2)]),
        }

    def ssd_block(bufs, gsets, di, hT, tlo, nch, db, carry_g, mk_cb):
        prev = None
        for g in range(NG):
            gs = gsets[g % 2]
            interleave(conv_gen(bufs["cb"], gs, g, hT, tlo, nch, True), prev, ratio=2)
            prev = chunk_gen(bufs, gs, g, di, nch, db, carry_g[g], mk_cb(g, gs))
        interleave(None, prev)

    segs = (("xoP", NP, yP, 0, 0), ("xoS", NS, yS, NP // 128, 1))

    NCH2 = BLK2 // 128
    TH2 = BLK2 + 4
    with ExitStack() as es:
        mkpsum(es, mm=2, tp=1, sm=1, ck=2, sel=2)
        hT = sb(es, "hT2", [128, KD, TH2], BF16)
        st = {"xt": Ring([sb(es, f"xt2{i}", [128, D]) for i in range(2)]),
              "xn": Ring([sb(es, f"xn2{i}", [128, D], BF16) for i in range(2)]),
              "sm": Ring([sb(es, f"sm2{i}", [128, 8]) for i in range(2)])}
        nw_load(es, st, PB_NW)
        db = {k: sb(es, "db2_" + k, [128, NCH2, 128]) for k in ("dt", "A", "cs", "tot", "csT", "ecs")}
        bufs = ssd_bufs(es, NCH2, "p2")
        gsets = [group_set(es, NCH2, f"b{i}", True) for i in range(2)]
        carries = sb(es, "carries2", [128, SSM_DIM])
        carry_g = [T(carries.t[:, g * 512:(g + 1) * 512]) for g in range(NG)]
        yst = Ring([sb(es, f"yst{i}", [128, 512]) for i in range(3)])
        for (xname, ntok, yout, chbase, sidx) in (segs if cfg.get("P2", 1) else ()):
            S.dma("sp", dict(out=carries.t[:], in_=acc_scr[2 * sidx]),
                  reads=[scrbuf["acc"][2 * sidx]], writes=[carries] + carry_g)
            for b0 in range(0, ntok, BLK2):
                row0 = PAD + b0 - 2
                load_norm_T(es, dram[xname][row0:row0 + TH2, :], TH2, hT, st)
                dt_block(hT, 2, NCH2, db, want_T=True)

                def mk_cb(g, gs, b0=b0, chbase=chbase):
                    def cbk(c, Yps, CSps, hb):
                        y = yst.next()
                        S.op("act", "activation", dict(out=y.t[:], in_=Yps.t[:, :], func=AF.Copy), reads=[Yps], writes=[y])
                        for hi in range(8):
                            S.op("dve", "scalar_tensor_tensor", dict(
                                out=y.t[:, hi * 64:(hi + 1) * 64], in0=CSps.t[:, hi * 64:(hi + 1) * 64],
                                scalar=db["ecs"].t[:, c, hb + hi:hb + hi + 1], in1=y.t[:, hi * 64:(hi + 1) * 64],
                                op0=ALU.mult, op1=ALU.add), reads=[CSps, db["ecs"], y], writes=[y])
                        ch = chbase + b0 // 128 + c
                        S.dma("sp", dict(out=yf_scr[ch][:, g * 512:(g + 1) * 512], in_=y.t[:]),
                              reads=[y], writes=[scrbuf["yf"][ch]])
                        yield
                    return cbk
                ssd_block(bufs, gsets, 0, hT, 2, NCH2, db, carry_g, mk_cb)
        S.barrier()
        flush()

    NCH3 = BLK3 // 128
    TH3 = BLK3 + 2 * PAD
    onesb = lambda: cstb.t[:, 384:512]
    with ExitStack() as es:
        hT = sb(es, "hT3", [128, KD, TH3], BF16)
        carries = sb(es, "carries3", [128, SSM_DIM])
        carry_g = [T(carries.t[:, g * 512:(g + 1) * 512]) for g in range(NG)]
        ysT = [sb(es, f"ysT{i}", [128, BLK3], BF16) for i in range(32)]
        ycT = [sb(es, f"ycT{i}", [128, BLK3], BF16) for i in range(16)]

        for (xname, ntok, yout, chbase, sidx) in (segs if cfg.get("P3", 1) else ()):
            S.dma("sp", dict(out=carries.t[:], in_=acc_scr[2 * sidx + 1]),
                  reads=[scrbuf["acc"][2 * sidx + 1]], writes=[carries] + carry_g)
            for b0 in range(ntok - BLK3, -1, -BLK3):
                with ExitStack() as e2:
                    mkpsum(e2, tp=2)
                    st = {"xt": Ring([sb(e2, f"xt3{i}", [128, D]) for i in range(2)]),
                          "xn": Ring([sb(e2, f"xn3{i}", [128, D], BF16) for i in range(2)]),
                          "sm": Ring([sb(e2, f"sm3{i}", [128, 8]) for i in range(2)])}
                    nw_load(e2, st, PB_NW)
                    load_norm_T(e2, dram[xname][b0:b0 + TH3, :], TH3, hT, st)
                    S.barrier()
                    flush()
                with ExitStack() as e2:
                    db = {k: sb(e2, "db3_" + k, [128, NCH3, 128]) for k in ("dt", "A", "cs", "tot", "csT", "ecs")}
                    mkpsum(e2, mm=2, tp=1, sm=1, ck=2, sel=2)
                    bufs = ssd_bufs(e2, NCH3, "p3", nbuf=2)
                    gsets = [group_set(e2, NCH3, f"c{i}", True) for i in range(2)]
                    yb = Ring([sb(e2, f"yb{i}", [128, 512]) for i in range(1)])
                    yfl = Ring([sb(e2, f"yfl{i}", [128, 512]) for i in range(1)])
                    zs = Ring([sb(e2, f"zs{i}", [128, 512]) for i in range(1)])
                    ysn = Ring([sb(e2, f"ysn{i}", [128, 512], BF16) for i in range(2)])
                    sq = sb(e2, "sqj", [128, 512])
                    sm3 = Ring([sb(e2, f"sm3b{i}", [128, 8]) for i in range(2)])
                    dt_block(hT, PAD, NCH3, db, want_T=True)
                    def mk_cb(g, gs, b0=b0, chbase=chbase):
                      wz = load_w(dram["w_in"], 0, OFF_Z + g * 512, 512)

                      def cbk(c, Yps, CSps, hb):
                        if True:
                            y = yb.next()
                            yf = yfl.next()
                            ch = chbase + b0 // 128 + c
                            S.dma("sp", dict(out=yf.t[:], in_=yf_scr[ch][:, g * 512:(g + 1) * 512]),
                                  reads=[scrbuf["yf"][ch]], writes=[yf])
                            S.op("act", "activation", dict(out=y.t[:], in_=Yps.t[:, :], func=AF.Copy), reads=[Yps], writes=[y])
                            for hi in range(8):
                                S.op("dve", "scalar_tensor_tensor", dict(
                                    out=y.t[:, hi * 64:(hi + 1) * 64], in0=CSps.t[:, hi * 64:(hi + 1) * 64],
                                    scalar=db["ecs"].t[:, c, hb + hi:hb + hi + 1], in1=y.t[:, hi * 64:(hi + 1) * 64],
                                    op0=ALU.mult, op1=ALU.add), reads=[CSps, db["ecs"], y], writes=[y])
                            S.op("dve", "tensor_tensor", dict(out=y.t[:], in0=y.t[:], in1=yf.t[:], op=ALU.add),
                                 reads=[y, yf], writes=[y])
                            v3 = lambda ap: ap.rearrange("p (h d) -> p h d", h=8)
                            S.op("dve", "tensor_tensor", dict(
                                out=v3(yf.t[:]), in0=v3(gs["xg_tok"].t[:, c, :]),
                                in1=pbt.t[:, PB_DSK + g * 8:PB_DSK + g * 8 + 8].unsqueeze(2).to_broadcast([128, 8, 64]),
                                op=ALU.mult), reads=[gs["xg_tok"], pbt], writes=[yf])
                            S.op("dve", "tensor_tensor", dict(out=y.t[:], in0=y.t[:], in1=yf.t[:], op=ALU.add),
                                 reads=[y, yf], writes=[y])
                            yield
                            zp = psum("mm")
                            proj_tm(wz, 0, 512, hT, PAD + c * 128, zp, 0)
                            yield
                            z = zs.next()
                            S.op("act", "activation", dict(out=z.t[:], in_=zp.t[:, :], func=AF.Silu), reads=[zp], writes=[z])
                            S.op("dve", "tensor_tensor", dict(out=y.t[:], in0=y.t[:], in1=z.t[:], op=ALU.mult),
                                 reads=[y, z], writes=[y])
                            sm = sm3.next()
                            S.op("act", "activation", dict(out=sq.t[:], in_=y.t[:], func=AF.Square, accum_out=sm.t[:, 0:1]),
                                 reads=[y], writes=[sq, sm])
                            S.op("dve", "tensor_scalar", dict(out=sm.t[:, 1:2], in0=sm.t[:, 0:1], scalar1=1.0 / 512, scalar2=EPS,
                                                              op0=ALU.mult, op1=ALU.add), reads=[sm], writes=[sm])
                            S.op("act", "activation", dict(out=sm.t[:, 2:3], in_=sm.t[:, 1:2], func=AF.Sqrt), reads=[sm], writes=[sm])
                            S.op("dve", "reciprocal", dict(out=sm.t[:, 3:4], in_=sm.t[:, 2:3]), reads=[sm], writes=[sm])
                            yn = ysn.next()
                            S.op("dve", "tensor_scalar", dict(out=yn.t[:], in0=y.t[:], scalar1=sm.t[:, 3:4], scalar2=None,
                                                              op0=ALU.mult), reads=[y, sm], writes=[yn])
                            tp = psum("tp")
                            for i in range(4):
                                S.op("pe", "transpose", dict(out=tp.t[:, i * 128:(i + 1) * 128],
                                                             in_=yn.t[:, i * 128:(i + 1) * 128], identity=IDB()),
                                     reads=[yn, cstb], writes=[tp], inc=(i == 3))
                            for i in range(4):
                                cbi = g * 4 + i
                                S.op("dve", "tensor_scalar", dict(
                                    out=ysT[cbi].t[:, c * 128:(c + 1) * 128], in0=tp.t[:, i * 128:(i + 1) * 128],
                                    scalar1=ppt.t[:, PP_SNW + cbi:PP_SNW + cbi + 1], scalar2=None, op0=ALU.mult),
                                    reads=[tp, ppt], writes=[ysT[cbi]])
                            yield
                      return cbk
                    ssd_block(bufs, gsets, 1, hT, PAD, NCH3, db, carry_g, mk_cb)
                    S.barrier()
                    flush()

                with ExitStack() as e2:
                    mkpsum(e2, mm=4, sm=2)
                    cpre = Ring([sb(e2, f"cpre{i}", [128, TH3]) for i in range(2)])
                    csig = Ring([sb(e2, f"csig{i}", [128, TH3]) for i in range(2)])
                    cacc = Ring([sb(e2, f"cacc3{i}", [128, BLK3]) for i in range(3)])
                    stat = sb(e2, "stat", [128, 2, BLK3])
                    usq = Ring([sb(e2, f"usq{i}", [128, BLK3], BF16) for i in range(2)])
                    gt = Ring([sb(e2, f"gt{i}", [128, BLK3]) for i in range(2)])
                    stp = [psum("sm"), psum("sm")]
                    for cbi in range(16):
                        if cbi % 4 == 0:
                            wcv = load_w(dram["w_in"], 0, OFF_CV + cbi * 128, 512)
                            wcg = load_w(dram["w_in"], 0, OFF_CG + cbi * 128, 512)
                        co = (cbi % 4) * 128
                        pre = cpre.next()
                        sg_ = csig.next()
                        for (t0, n) in subblocks(0, TH3):
                            ps = proj_fm(wcg, co, hT, t0, n)
                            S.op("act", "activation", dict(out=sg_.t[:, t0:t0 + n], in_=ps.t[:, 0:n], func=AF.Sigmoid),
                                 reads=[ps], writes=[sg_])
                            ps2 = proj_fm(wcv, co, hT, t0, n)
                            S.op("dve", "tensor_tensor", dict(out=pre.t[:, t0:t0 + n], in0=ps2.t[:, 0:n],
                                                              in1=sg_.t[:, t0:t0 + n], op=ALU.mult),
                                 reads=[ps2, sg_], writes=[pre])
                        acc = cacc.next()
                        wb0 = PP_DWW + cbi * 31
                        S.op("dve", "tensor_scalar", dict(out=acc.t[:], in0=pre.t[:, 1:1 + BLK3], scalar1=ppt.t[:, wb0:wb0 + 1],
                                                          scalar2=ppt.t[:, PP_DWB + cbi:PP_DWB + cbi + 1], op0=ALU.mult,
                                                          op1=ALU.add), reads=[pre, ppt], writes=[acc])
                        for k in range(1, CONV_K):
                            S.op("dve", "scalar_tensor_tensor", dict(
                                out=acc.t[:], in0=pre.t[:, 1 + k:1 + k + BLK3], scalar=ppt.t[:, wb0 + k:wb0 + k + 1],
                                in1=acc.t[:], op0=ALU.mult, op1=ALU.add), reads=[pre, ppt, acc], writes=[acc])
                        S.op("act", "activation", dict(out=ycT[cbi].t[:], in_=acc.t[:], func=AF.Copy),
                             reads=[acc], writes=[ycT[cbi]])
                        us = usq.next()
                        S.op("act", "activation", dict(out=us.t[:], in_=ycT[cbi].t[:], func=AF.Square),
                             reads=[ycT[cbi]], writes=[us])
                        S.op("pe", "matmul", dict(out=stp[0].t[:, 0:BLK3], lhsT=onesb(), rhs=ycT[cbi].t[:],
                                                  start=(cbi == 0), stop=(cbi == 15)),
                             reads=[cstb, ycT[cbi]], writes=[stp[0]])
                        S.op("pe", "matmul", dict(out=stp[1].t[:, 0:BLK3], lhsT=onesb(), rhs=us.t[:],
                                                  start=(cbi == 0), stop=(cbi == 15)),
                             reads=[cstb, us], writes=[stp[1]])
                    S.op("dve", "tensor_scalar", dict(out=stat.t[:, 0, :], in0=stp[0].t[:, 0:BLK3], scalar1=1.0 / D, scalar2=None,
                                                      op0=ALU.mult), reads=[stp[0]], writes=[stat])
                    S.op("dve", "tensor_scalar", dict(out=stat.t[:, 1, :], in0=stp[1].t[:, 0:BLK3], scalar1=1.0 / D, scalar2=EPS,
                                                      op0=ALU.mult, op1=ALU.add), reads=[stp[1]], writes=[stat])
                    tmpm = cacc.next()
                    S.op("dve", "tensor_tensor", dict(out=tmpm.t[:], in0=stat.t[:, 0, :], in1=stat.t[:, 0, :], op=ALU.mult),
                         reads=[stat], writes=[tmpm])
                    S.op("dve", "tensor_tensor", dict(out=stat.t[:, 1, :], in0=stat.t[:, 1, :], in1=tmpm.t[:], op=ALU.subtract),
                         reads=[stat, tmpm], writes=[stat])
                    S.op("act", "activation", dict(out=stat.t[:, 1, :], in_=stat.t[:, 1, :], func=AF.Sqrt), reads=[stat], writes=[stat])
                    S.op("dve", "reciprocal", dict(out=stat.t[:, 1, :], in_=stat.t[:, 1, :]), reads=[stat], writes=[stat])
                    for cbi in range(16):
                        if cbi % 4 == 0:
                            wcs = load_w(dram["w_in"], 0, OFF_CS + cbi * 128, 512)
                        co = (cbi % 4) * 128
                        a_ = cacc.next()
                        S.op("dve", "tensor_tensor", dict(out=a_.t[:], in0=ycT[cbi].t[:], in1=stat.t[:, 0, :], op=ALU.subtract),
                             reads=[ycT[cbi], stat], writes=[a_])
                        S.op("dve", "tensor_tensor", dict(out=a_.t[:], in0=a_.t[:], in1=stat.t[:, 1, :], op=ALU.mult),
                             reads=[a_, stat], writes=[a_])
                        S.op("act", "activation", dict(out=a_.t[:], in_=a_.t[:], func=AF.Silu,
                                                       scale=ppt.t[:, PP_LNG + cbi:PP_LNG + cbi + 1],
                                                       bias=ppt.t[:, PP_LNB + cbi:PP_LNB + cbi + 1]),
                             reads=[a_, ppt], writes=[a_])
                        for (t0, n) in subblocks(0, BLK3):
                            ps = proj_fm(wcs, co, hT, PAD + t0, n)
                            g_ = gt.next()
                            S.op("act", "activation", dict(out=g_.t[:, 0:n], in_=ps.t[:, 0:n], func=AF.Silu),
                                 reads=[ps], writes=[g_])
                            S.op("dve", "tensor_tensor", dict(out=ycT[cbi].t[:, t0:t0 + n], in0=a_.t[:, t0:t0 + n],
                                                              in1=g_.t[:, 0:n], op=ALU.mult),
                                 reads=[a_, g_], writes=[ycT[cbi]])
                    S.barrier()
                    flush()

                with ExitStack() as e2:
                    mkpsum(e2, mm=6)
                    fnw = sb(e2, "fnw", [128, D])
                    S.dma("sp", dict(out=fnw.t[:], in_=dram["pb"][:, PB_FNW:PB_FNW + D].partition_broadcast(128)), writes=[fnw])
                    mT = [sb(e2, f"mT{i}", [128, BLK3], BF16) for i in range(16)]
                    o1s = [sb(e2, f"o1s{i}", [128, BLK3]) for i in range(4)]
                    gt = Ring([sb(e2, f"gtc{i}", [128, BLK3]) for i in range(2)])
                    ores = Ring([sb(e2, f"ores{i}", [128, D]) for i in range(1)])
                    xres = Ring([sb(e2, f"xres{i}", [128, D]) for i in range(2)])
                    sm3 = Ring([sb(e2, f"sm3c{i}", [128, 8]) for i in range(2)])
                    for dq in range(4):
                        wgc = load_w(dram["w_in"], 0, OFF_GATE + dq * 512, 512)
                        wb0_ = load_w(dram["w_branch"], 0, dq * 512, 512)
                        for dj in range(4):
                            dbi = dq * 4 + dj
                            co = dj * 128
                            for (t0, n) in subblocks(0, BLK3):
                                poc = psum("mm")
                                for k in range(16):
                                    S.op("pe", "matmul", dict(out=poc.t[:, 0:n], lhsT=wb0_.t[:, k, co:co + 128],
                                                              rhs=ycT[k].t[:, t0:t0 + n], start=(k == 0), stop=(k == 15)),
                                         reads=[wb0_, ycT[k]], writes=[poc], inc=(k == 15))
                                pgc = proj_fm(wgc, co, hT, PAD + t0, n)
                                g1 = gt.next()
                                S.op("act", "activation", dict(out=g1.t[:, 0:n], in_=pgc.t[:, 0:n], func=AF.Sigmoid,
                                                               bias=ppt.t[:, PP_BG + dbi:PP_BG + dbi + 1]),
                                     reads=[pgc, ppt], writes=[g1])
                                S.op("dve", "tensor_tensor", dict(out=o1s[dj].t[:, t0:t0 + n], in0=poc.t[:, 0:n],
                                                                  in1=g1.t[:, 0:n], op=ALU.mult),
                                     reads=[poc, g1], writes=[o1s[dj]])
                        wgs = load_w(dram["w_in"], 0, OFF_GATE + 2048 + dq * 512, 512)
                        wb1_ = load_w(dram["w_branch"], 2048, dq * 512, 512)
                        wb2_ = load_w(dram["w_branch"], 4096, dq * 512, 512)
                        wbs = [wb1_, wb2_]
                        for dj in range(4):
                            dbi = dq * 4 + dj
                            co = dj * 128
                            for (t0, n) in subblocks(0, BLK3):
                                pos = psum("mm")
                                for k in range(32):
                                    S.op("pe", "matmul", dict(out=pos.t[:, 0:n], lhsT=wbs[k // 16].t[:, k % 16, co:co + 128],
                                                              rhs=ysT[k].t[:, t0:t0 + n], start=(k == 0), stop=(k == 31)),
                                         reads=[wbs[k // 16], ysT[k]], writes=[pos], inc=(k == 31))
                                pgs = proj_fm(wgs, co, hT, PAD + t0, n)
                                g2 = gt.next()
                                S.op("act", "activation", dict(out=g2.t[:, 0:n], in_=pgs.t[:, 0:n], func=AF.Sigmoid,
                                                               bias=ppt.t[:, PP_BG + 16 + dbi:PP_BG + 16 + dbi + 1]),
                                     reads=[pgs, ppt], writes=[g2])
                                S.op("dve", "tensor_tensor", dict(out=g2.t[:, 0:n], in0=pos.t[:, 0:n], in1=g2.t[:, 0:n],
                                                                  op=ALU.mult), reads=[pos, g2], writes=[g2])
                                S.op("dve", "tensor_tensor", dict(out=mT[dbi].t[:, t0:t0 + n], in0=o1s[dj].t[:, t0:t0 + n],
                                                                  in1=g2.t[:, 0:n], op=ALU.add),
                                     reads=[o1s[dj], g2], writes=[mT[dbi]])
                    wos = None
                    for c in range(NCH3):
                        xr = xres.next()
                        r0 = PAD + b0 + c * 128
                        S.dma("sp", dict(out=xr.t[:], in_=dram[xname][r0:r0 + 128, :]), writes=[xr])
                        orr = ores.next()
                        for eq in range(4):
                            wo = load_w(dram["w_out"], 0, eq * 512, 512)
                            po = psum("mm")
                            for k in range(16):
                                S.op("pe", "matmul", dict(out=po.t[:, :], lhsT=mT[k].t[:, c * 128:(c + 1) * 128],
                                                          rhs=wo.t[:, k, :], start=(k == 0), stop=(k == 15)),
                                     reads=[wo, mT[k]], writes=[po], inc=(k == 15))
                            S.op("dve", "tensor_tensor", dict(out=orr.t[:, eq * 512:(eq + 1) * 512], in0=po.t[:, :],
                                                              in1=xr.t[:, eq * 512:(eq + 1) * 512], op=ALU.add),
                                 reads=[po, xr], writes=[orr])
                        sm = sm3.next()
                        S.op("act", "activation", dict(out=xr.t[:], in_=orr.t[:], func=AF.Square, accum_out=sm.t[:, 0:1]),
                             reads=[orr], writes=[xr, sm])
                        S.op("dve", "tensor_scalar", dict(out=sm.t[:, 1:2], in0=sm.t[:, 0:1], scalar1=1.0 / D, scalar2=EPS,
                                                          op0=ALU.mult, op1=ALU.add), reads=[sm], writes=[sm])
                        S.op("act", "activation", dict(out=sm.t[:, 2:3], in_=sm.t[:, 1:2], func=AF.Sqrt), reads=[sm], writes=[sm])
                        S.op("dve", "reciprocal", dict(out=sm.t[:, 3:4], in_=sm.t[:, 2:3]), reads=[sm], writes=[sm])
                        S.op("dve", "scalar_tensor_tensor", dict(out=orr.t[:], in0=orr.t[:], scalar=sm.t[:, 3:4],
                                                                 in1=fnw.t[:], op0=ALU.mult, op1=ALU.mult),
                             reads=[orr, sm, fnw], writes=[orr])
                        r1 = b0 + c * 128
                        S.dma("sp", dict(out=yout[r1:r1 + 128, :], in_=orr.t[:]), reads=[orr])
                    S.barrier()
                    S.final_wait("sp")
                    flush()
    top.close()
    return nc


CFG_FULL = dict(NP=1024, NS=2048, BLK1=512, BLK2=512, BLK3=512, NW=3)


def _host_inputs(cfg, x_prompt, x_sample, norm_w, w_in, b_gate, dw_w, dw_b, ln_g, ln_b, sconv_w, sconv_b,
                 dt_bias, a_log, d_skip, ssm_norm_w, w_branch, w_out, final_norm_w):
    NP, NS, BLK1 = cfg["NP"], cfg["NS"], cfg["BLK1"]
    f = lambda a: np.ascontiguousarray(np.asarray(a, dtype=np.float32))
    xp = f(x_prompt)[0]
    xs = f(x_sample)[0]
    xP = np.zeros((xp.shape[0] + 2 * PAD, D), np.float32)
    xP[PAD:-PAD] = xp
    xS = np.zeros((xs.shape[0] + 2 * PAD, D), np.float32)
    xS[PAD:-PAD] = xs
    pp = np.zeros((128, NPP), np.float32)
    pp[:, PP_DWW:PP_DWW + 496] = f(dw_w)[0].reshape(31, 16, 128).transpose(2, 1, 0).reshape(128, 496)
    pp[:, PP_DWB:PP_DWB + 16] = f(dw_b)[0].reshape(16, 128).T
    pp[:, PP_LNG:PP_LNG + 16] = f(ln_g)[0].reshape(16, 128).T
    pp[:, PP_LNB:PP_LNB + 16] = f(ln_b)[0].reshape(16, 128).T
    pp[:, PP_SCW:PP_SCW + 240] = f(sconv_w)[0].reshape(5, 48, 128).transpose(2, 1, 0).reshape(128, 240)
    pp[:, PP_SCB:PP_SCB + 48] = f(sconv_b)[0].reshape(48, 128).T
    pp[:, PP_BG:PP_BG + 32] = f(b_gate)[0].reshape(32, 128).T
    pp[:, PP_SNW:PP_SNW + 32] = f(ssm_norm_w)[0].reshape(32, 128).T
    pb = np.zeros((1, NPB), np.float32)
    pb[0, PB_NW:PB_NW + D] = f(norm_w)[0]
    pb[0, PB_FNW:PB_FNW + D] = f(final_norm_w)
    pb[0, 4096 + PB_DSK:4096 + PB_DSK + 64] = f(d_skip)[0]
    pb[0, 4096 + PB_DTB:4096 + PB_DTB + 128] = f(dt_bias)[0].reshape(128)
    pb[0, 4096 + PB_ALOG:4096 + PB_ALOG + 128] = f(a_log)[0].reshape(128)
    consts = np.zeros((128, 512), np.float32)
    i = np.arange(128)
    consts[:, 0:128] = np.eye(128)
    consts[:, 128:256] = (i[:, None] <= i[None, :])
    consts[:, 256:384] = (i[:, None] >= i[None, :])
    consts[:, 384:512] = 1.0
    common = {"w_in": f(w_in)[0], "w_branch": f(w_branch)[0], "w_out": f(w_out)[0],
              "pp": pp, "pb": pb, "consts": consts}
    nb1p, nb1s = NCORES * NP // BLK1, NCORES * NS // BLK1
    in_maps = []
    for k in range(NCORES):
        m = dict(common)
        m["xoP"] = np.ascontiguousarray(xP[k * NP:(k + 1) * NP + 2 * PAD])
        m["xoS"] = np.ascontiguousarray(xS[k * NS:(k + 1) * NS + 2 * PAD])
        rows, mrow = [], []
        for (xpad, nb, nown) in ((xP, nb1p, NP // BLK1), (xS, nb1s, NS // BLK1)):
            for j in range(nb):
                if k * nown <= j < (k + 1) * nown:
                    continue
                r0 = PAD + j * BLK1 - 2
                rows.append(xpad[r0:r0 + BLK1 + 4])
                mf = 1.0 if j < k * nown else 0.0
                mrow += [mf, 1 - mf, 1 - mf, mf]
        m["xoth"] = np.ascontiguousarray(np.stack(rows, axis=0))
        msk = np.ascontiguousarray(np.broadcast_to(np.asarray(mrow, np.float32)[None, :], (128, len(mrow))))
        m["masks"] = msk
        in_maps.append(m)
    return in_maps


_NC_CACHE = {}


def run(cfg, **inputs):
    key = tuple(sorted(cfg.items()))
    if key not in _NC_CACHE:
        _NC_CACHE[key] = build(cfg)
    nc = _NC_CACHE[key]
    in_maps = _host_inputs(cfg, **inputs)
    res = run_bass_kernel_spmd(nc, in_maps, core_ids=list(range(NCORES)))
    yp = np.concatenate([np.asarray(r["yP"], dtype=np.float32) for r in res.results], axis=0)[None]
    ys = np.concatenate([np.asarray(r["yS"], dtype=np.float32) for r in res.results], axis=0)[None]
    return yp, ys


def kernel(**inputs):
    return run(CFG_FULL, **inputs)
```

```python
import numpy as np
from contextlib import ExitStack
import concourse.bass as bass
import concourse.mybir as mybir
from concourse.bass_utils import run_bass_kernel_spmd

F32 = mybir.dt.float32
BF16 = mybir.dt.bfloat16
AF = mybir.ActivationFunctionType
ALU = mybir.AluOpType

D = 2048
KD = 16
CONV_K = 31
SSM_DIM = 4096
NH = 64
HD = 64
NG = 8
DS = 128
OFF_CV = 0
OFF_CG = 2048
OFF_CS = 4096
OFF_Z = 6144
OFF_X = 10240
OFF_B = OFF_X + 4096
OFF_C = OFF_B + 1024
OFF_DT = 16384
OFF_GATE = 16512
IN_DIM = 20608
EPS = 1e-5
PAD = 16
NCORES = 8

PP_DWW = 0
PP_DWB = 496
PP_LNG = 512
PP_LNB = 528
PP_SCW = 544
PP_SCB = 784
PP_BG = 832
PP_SNW = 864
NPP = 896
PB_NW = 0
PB_FNW = 2048
PB_DSK = 0
PB_DTB = 64
PB_ALOG = 192
NPB = 4416


class Buf:
    __slots__ = ("w", "r")

    def __init__(self):
        self.w = None
        self.r = {}


class T:
    def __init__(self, t):
        self.t = t
        self.b = Buf()


class Sched:
    def __init__(self):
        self.ops = {n: [] for n in ("pe", "act", "dve", "pool", "sp")}
        self.cnt = {n: 0 for n in self.ops}
        self.seen = {n: {} for n in self.ops}
        self.dslots = {"sp": 8, "pool": 6}
        self.dcnt = {"sp": 0, "pool": 0}
        self.semkeys = list(self.ops.keys())
        for q, n in self.dslots.items():
            for i in range(n):
                self.semkeys.append(f"{q}_d{i}")
        self.semval = {k: 0 for k in self.semkeys}

    def _collect(self, eng, reads, writes):
        deps = {}

        def need(d):
            if d is None:
                return
            k, v = d
            if k == "pe" and eng == "pe":
                return
            if deps.get(k, 0) < v:
                deps[k] = v

        for b in reads:
            need(b.w)
        for b in writes:
            need(b.w)
            for d in b.r.values():
                need(d)
        return deps

    def _waits(self, eng, deps):
        seen = self.seen[eng]
        for k, v in deps.items():
            if seen.get(k, 0) < v:
                self.ops[eng].append(("wait", k, v))
                seen[k] = v

    def _mark(self, eng, d, reads, writes):
        for b in reads:
            o = b.r.get(d[0])
            if o is None or o[1] < d[1]:
                b.r[d[0]] = d
        for b in writes:
            b.w = d
            b.r = {}

    def op(self, eng, name, kw, reads=(), writes=(), inc=True):
        fn = (name, kw)
        reads = [x.b if isinstance(x, T) else x for x in reads]
        writes = [x.b if isinstance(x, T) else x for x in writes]
        self._waits(eng, self._collect(eng, reads, writes))
        if inc:
            self.cnt[eng] += 1
            self.ops[eng].append(("ins", fn, eng, 1))
            d = (eng, self.cnt[eng])
            self.semval[eng] = self.cnt[eng]
        else:
            self.ops[eng].append(("ins", fn, None, 0))
            d = (eng, self.cnt[eng] + 1)
        self._mark(eng, d, reads, writes)

    def dma(self, eng, kw, reads=(), writes=()):
        fn = ("dma_start", kw)
        reads = [x.b if isinstance(x, T) else x for x in reads]
        writes = [x.b if isinstance(x, T) else x for x in writes]
        i = self.dcnt[eng]
        self.dcnt[eng] += 1
        ns = self.dslots[eng]
        key = f"{eng}_d{i % ns}"
        deps = self._collect(eng, reads, writes)
        prev = 16 * (i // ns)
        if prev > 0 and deps.get(key, 0) < prev:
            deps[key] = prev
        self._waits(eng, deps)
        val = 16 * (i // ns + 1)
        self.ops[eng].append(("ins", fn, key, 16))
        self.semval[key] = val
        self._mark(eng, (key, val), reads, writes)

    def barrier(self):
        for eng in self.ops:
            deps = {k: v for k, v in self.semval.items() if v > 0 and k != eng}
            self._waits(eng, deps)

    def final_wait(self, eng):
        deps = {k: v for k, v in self.semval.items() if v > 0 and k != eng}
        self._waits(eng, deps)


def build(cfg):
    NP, NS = cfg["NP"], cfg["NS"]
    BLK1, BLK2, BLK3 = cfg["BLK1"], cfg["BLK2"], cfg["BLK3"]
    LP, LS = NCORES * NP, NCORES * NS
    NB1P, NB1S = LP // BLK1, LS // BLK1
    NCH_OWN = (NP + NS) // 128

    nc = bass.Bass("TRN2", target_bir_lowering=False)
    dram = {}

    def din(name, shape):
        dram[name] = nc.dram_tensor(name, list(shape), F32, kind="ExternalInput").ap()

    NBO_P = NB1P - NP // BLK1
    NBO_S = NB1S - NS // BLK1
    NM = 4 * (NBO_P + NBO_S)
    din("xoth", (NBO_P + NBO_S, BLK1 + 4, D))
    din("xoP", (NP + 2 * PAD, D))
    din("xoS", (NS + 2 * PAD, D))
    din("w_in", (D, IN_DIM))
    din("w_branch", (6144, D))
    din("w_out", (D, D))
    din("pp", (128, NPP))
    din("pb", (1, NPB))
    din("consts", (128, 512))
    din("masks", (128, NM))
    yP = nc.dram_tensor("yP", [NP, D], F32, kind="ExternalOutput").ap()
    yS = nc.dram_tensor("yS", [NS, D], F32, kind="ExternalOutput").ap()
    acc_scr = nc.dram_tensor("acc_scr", [4, 128, SSM_DIM], F32).ap()
    yf_scr = nc.dram_tensor("yf_scr", [NCH_OWN, 128, SSM_DIM], F32).ap()
    scrbuf = {"acc": [Buf() for _ in range(4)], "yf": [Buf() for _ in range(NCH_OWN)]}

    S = Sched()
    top = ExitStack()
    sems = {}
    for k in S.semkeys:
        sems[k] = top.enter_context(nc.semaphore(k))

    uid = [0]

    def sb(es, name, shape, dt=F32):
        uid[0] += 1
        return T(es.enter_context(nc.sbuf_tensor(f"{name}_{uid[0]}", list(shape), dt)))

    pools = {}
    rr = {}

    def mkpsum(es, **spec):
        assert sum(spec.values()) <= 8
        pools.clear()
        for kind, n in spec.items():
            if kind == "tp":
                pools[kind] = [T(es.enter_context(nc.psum_tensor(f"ps{kind}{i}_{uid[0]}", [128, 1024], BF16))) for i in range(n)]
            else:
                pools[kind] = [T(es.enter_context(nc.psum_tensor(f"ps{kind}{i}_{uid[0]}", [128, 512], F32))) for i in range(n)]
            uid[0] += 1
            rr[kind] = 0

    def psum(kind):
        lst = pools[kind]
        t = lst[rr[kind] % len(lst)]
        rr[kind] += 1
        return t

    class Ring:
        def __init__(self, items):
            self.items = items
            self.i = 0

        def next(self):
            t = self.items[self.i % len(self.items)]
            self.i += 1
            return t

    cst = sb(top, "cst", [128, 512])
    cstb = sb(top, "cstb", [128, 512], BF16)
    ppt = sb(top, "ppt", [128, NPP])
    pbt = sb(top, "pbt", [128, NPB - 4096])
    mskt = sb(top, "mskt", [128, NM])
    abc = sb(top, "abc", [128, 128])
    sch = sb(top, "sch", [128, 288])
    W = Ring([sb(top, f"W{i}", [128, KD, 512], BF16) for i in range(cfg.get("NW", 3))])

    IDF = lambda: cst.t[:, 0:128]
    TRIF = lambda: cst.t[:, 128:256]
    TRIB = lambda: cst.t[:, 256:384]
    ONES = lambda: cst.t[:, 384:512]
    IDB = lambda: cstb.t[:, 0:128]

    def flush(es_phase=None):
        with nc.Block() as blk:
            def rep(eng):
                def body(h):
                    for e in S.ops[eng]:
                        if e[0] == "wait":
                            h.wait_ge(sems[e[1]], e[2])
                        else:
                            ins = getattr(h, e[1][0])(**e[1][1])
                            if e[2] is not None:
                                ins.then_inc(sems[e[2]], e[3])
                    S.ops[eng] = []
                return body
            blk.tensor(rep("pe"))
            blk.scalar(rep("act"))
            blk.vector(rep("dve"))
            blk.gpsimd(rep("pool"))
            blk.sync(rep("sp"))

    S.dma("sp", dict(out=cst.t[:], in_=dram["consts"]), writes=[cst])
    S.dma("sp", dict(out=ppt.t[:], in_=dram["pp"]), writes=[ppt])
    S.dma("sp", dict(out=pbt.t[:], in_=dram["pb"][:, 4096:NPB].partition_broadcast(128)), writes=[pbt])
    S.dma("sp", dict(out=mskt.t[:], in_=dram["masks"]), writes=[mskt])
    S.op("dve", "tensor_copy", dict(out=cstb.t[:], in_=cst.t[:]), reads=[cst], writes=[cstb])
    S.op("act", "activation", dict(out=abc.t[:], in_=pbt.t[:, PB_ALOG:PB_ALOG + 128], func=AF.Exp),
         reads=[pbt], writes=[abc])
    S.op("dve", "tensor_scalar", dict(out=abc.t[:], in0=abc.t[:], scalar1=-1.0, scalar2=None, op0=ALU.mult),
         reads=[abc], writes=[abc])
    S.op("dve", "tensor_scalar", dict(out=sch.t[:], in0=ppt.t[:, PP_SCW:PP_SCW + 288], scalar1=0.5, scalar2=None, op0=ALU.mult),
         reads=[ppt], writes=[sch])

    def load_w(mat, r0, c0, ncols, coff=0, wt=None):
        if wt is None:
            wt = W.next()
        src = mat[r0:r0 + 2048, c0:c0 + ncols].rearrange("(k p) c -> p k c", p=128)
        S.dma("pool", dict(out=wt.t[:, :, coff:coff + ncols], in_=src), writes=[wt])
        return wt

    def load_norm_T(es, src, nrows, hT, st):
        xt_r, xn_r, sm_r = st["xt"], st["xn"], st["sm"]
        for r0 in range(0, nrows, 128):
            R = min(128, nrows - r0)
            xt = xt_r.next()
            xn = xn_r.next()
            sm = sm_r.next()
            S.dma("sp", dict(out=xt.t[0:R, :], in_=src[r0:r0 + R, :]), writes=[xt])
            S.op("act", "activation", dict(
                out=xn.t[0:R, :], in_=xt.t[0:R, :], func=AF.Square, accum_out=sm.t[0:R, 0:1]),
                reads=[xt], writes=[xn, sm])
            S.op("dve", "tensor_scalar", dict(
                out=sm.t[0:R, 1:2], in0=sm.t[0:R, 0:1], scalar1=1.0 / D, scalar2=EPS, op0=ALU.mult, op1=ALU.add),
                reads=[sm], writes=[sm])
            S.op("act", "activation", dict(out=sm.t[0:R, 2:3], in_=sm.t[0:R, 1:2], func=AF.Sqrt),
                 reads=[sm], writes=[sm])
            S.op("dve", "reciprocal", dict(out=sm.t[0:R, 3:4], in_=sm.t[0:R, 2:3]),
                 reads=[sm], writes=[sm])
            S.op("dve", "scalar_tensor_tensor", dict(
                out=xn.t[0:R, :], in0=xt.t[0:R, :], scalar=sm.t[0:R, 3:4], in1=st["nw"].t[0:R, :],
                op0=ALU.mult, op1=ALU.mult), reads=[xt, sm, st["nw"]], writes=[xn])
            for half in range(2):
                tp = psum("tp")
                for j in range(8):
                    k = half * 8 + j
                    S.op("pe", "transpose", dict(
                        out=tp.t[:, j * 128:j * 128 + R], in_=xn.t[0:R, k * 128:(k + 1) * 128],
                        identity=cstb.t[0:R, 0:R]), reads=[xn, cstb], writes=[tp], inc=(j == 7))
                eng = "act" if half == 0 else "dve"
                src_v = tp.t[:].rearrange("p (j t) -> p j t", j=8)[:, :, 0:R]
                dst_v = hT.t[:, half * 8:half * 8 + 8, r0:r0 + R]
                if eng == "act":
                    S.op("act", "activation", dict(out=dst_v, in_=src_v, func=AF.Copy),
                         reads=[tp], writes=[hT])
                else:
                    S.op("dve", "tensor_copy", dict(out=dst_v, in_=src_v),
                         reads=[tp], writes=[hT])

    def proj_fm(wt, co, hT, t0, n):
        ps = psum("mm")
        for k in range(KD):
            S.op("pe", "matmul", dict(out=ps.t[:, 0:n], lhsT=wt.t[:, k, co:co + 128], rhs=hT.t[:, k, t0:t0 + n],
                start=(k == 0), stop=(k == KD - 1)), reads=[wt, hT], writes=[ps], inc=(k == KD - 1))
        return ps

    def proj_tm(wt, co, ncols, hT, t0, ps, po):
        for k in range(KD):
            S.op("pe", "matmul", dict(out=ps.t[:, po:po + ncols], lhsT=hT.t[:, k, t0:t0 + 128], rhs=wt.t[:, k, co:co + ncols],
                start=(k == 0), stop=(k == KD - 1)), reads=[wt, hT], writes=[ps], inc=(k == KD - 1))

    def subblocks(t0, n):
        out = []
        while n > 0:
            m = min(512, n)
            out.append((t0, m))
            t0 += m
            n -= m
        return out

    def conv5_silu(es_bufs, wt, co, hT, tlo, Tn, pidx, outT, out_off=0):
        pre = es_bufs["pre"].next()
        acc = es_bufs["acc"].next()
        for (t0, n) in subblocks(tlo - 2, Tn + 4):
            ps = proj_fm(wt, co, hT, t0, n)
            o = t0 - (tlo - 2)
            S.op("act", "activation", dict(out=pre.t[:, o:o + n], in_=ps.t[:, 0:n], func=AF.Copy),
                 reads=[ps], writes=[pre])
        wbase = pidx * 5
        S.op("dve", "tensor_scalar", dict(out=acc.t[:, 0:Tn], in0=pre.t[:, 0:Tn], scalar1=sch.t[:, wbase:wbase + 1],
                                          scalar2=sch.t[:, 240 + pidx:240 + pidx + 1], op0=ALU.mult, op1=ALU.add),
             reads=[pre, sch], writes=[acc])
        for k in range(1, 5):
            S.op("dve", "scalar_tensor_tensor", dict(
                out=acc.t[:, 0:Tn], in0=pre.t[:, k:k + Tn], scalar=sch.t[:, wbase + k:wbase + k + 1],
                in1=acc.t[:, 0:Tn], op0=ALU.mult, op1=ALU.add), reads=[pre, sch, acc], writes=[acc])
        th = pre
        S.op("act", "activation", dict(out=th.t[:, 0:Tn], in_=acc.t[:, 0:Tn], func=AF.Tanh), reads=[acc], writes=[th])
        S.op("dve", "scalar_tensor_tensor", dict(out=outT.t[:, out_off:out_off + Tn], in0=th.t[:, 0:Tn], scalar=1.0,
                                                 in1=acc.t[:, 0:Tn], op0=ALU.add, op1=ALU.mult),
             reads=[th, acc], writes=[outT])

    def dt_block(hT, tlo, nch, db, want_T):
        wt = load_w(dram["w_in"], 0, OFF_DT, 128)
        for c4 in range(0, nch, 4):
            n4 = min(4, nch - c4)
            ps = psum("sm")
            for i in range(n4):
                proj_tm(wt, 0, 128, hT, tlo + (c4 + i) * 128, ps, i * 128)
            sl = lambda t: t.t[:, c4:c4 + n4, :]
            psv = ps.t[:, 0:n4 * 128].rearrange("p (c f) -> p c f", c=n4)
            bias_v = pbt.t[:, PB_DTB:PB_DTB + 128].unsqueeze(1).to_broadcast([128, n4, 128])
            a_v = abc.t[:].unsqueeze(1).to_broadcast([128, n4, 128])
            S.op("dve", "tensor_tensor", dict(out=sl(db["A"]), in0=psv, in1=bias_v, op=ALU.add),
                 reads=[ps, pbt], writes=[db["A"]])
            S.op("dve", "scalar_tensor_tensor", dict(out=sl(db["dt"]), in0=sl(db["A"]), scalar=-1.0, in1=sl(db["A"]),
                                                     op0=ALU.mult, op1=ALU.max),
                 reads=[db["A"]], writes=[db["dt"]])
            S.op("act", "activation", dict(out=sl(db["dt"]), in_=sl(db["dt"]), func=AF.Exp, scale=-1.0),
                 reads=[db["dt"]], writes=[db["dt"]])
            S.op("act", "activation", dict(out=sl(db["dt"]), in_=sl(db["dt"]), func=AF.Ln, bias=1.0),
                 reads=[db["dt"]], writes=[db["dt"]])
            S.op("dve", "scalar_tensor_tensor", dict(out=sl(db["dt"]), in0=sl(db["A"]), scalar=0.0, in1=sl(db["dt"]),
                                                         op0=ALU.max, op1=ALU.add),
                 reads=[db["A"], db["dt"]], writes=[db["dt"]])
            S.op("dve", "tensor_tensor", dict(out=sl(db["A"]), in0=sl(db["dt"]), in1=a_v, op=ALU.mult),
                 reads=[db["dt"], abc], writes=[db["A"]])
        for c4 in range(0, nch, 4):
            n4 = min(4, nch - c4)
            ps = psum("sm")
            ps2 = psum("mm")
            for i in range(n4):
                c = c4 + i
                S.op("pe", "matmul", dict(out=ps.t[:, i * 128:i * 128 + 64], lhsT=TRIF(), rhs=db["A"].t[:, c, 0:64],
                                                        start=True, stop=True), reads=[cst, db["A"]], writes=[ps], inc=False)
                S.op("pe", "matmul", dict(out=ps.t[:, i * 128 + 64:i * 128 + 128], lhsT=TRIB(),
                                                        rhs=db["A"].t[:, c, 64:128], start=True, stop=True),
                     reads=[cst, db["A"]], writes=[ps], inc=False)
                S.op("pe", "matmul", dict(out=ps2.t[:, i * 128:i * 128 + 128], lhsT=ONES(), rhs=db["A"].t[:, c, :],
                                                        start=True, stop=True), reads=[cst, db["A"]], writes=[ps2],
                     inc=(i == n4 - 1))
            S.op("act", "activation", dict(
                out=db["cs"].t[:, c4:c4 + n4, :], in_=ps.t[:, 0:n4 * 128].rearrange("p (c f) -> p c f", c=n4), func=AF.Copy),
                reads=[ps], writes=[db["cs"]])
            S.op("dve", "tensor_copy", dict(
                out=db["tot"].t[:, c4:c4 + n4, :], in_=ps2.t[:, 0:n4 * 128].rearrange("p (c f) -> p c f", c=n4)),
                reads=[ps2], writes=[db["tot"]])
            if want_T:
                ps3 = psum("ck")
                for i in range(n4):
                    c = c4 + i
                    S.op("pe", "matmul", dict(out=ps3.t[0:64, i * 128:(i + 1) * 128], lhsT=db["A"].t[:, c, 0:64],
                                                            rhs=TRIF(), start=True, stop=True),
                         reads=[cst, db["A"]], writes=[ps3], inc=False)
                    S.op("pe", "matmul", dict(out=ps3.t[64:128, i * 128:(i + 1) * 128],
                                                            lhsT=db["A"].t[:, c, 64:128], rhs=TRIB(), start=True, stop=True),
                         reads=[cst, db["A"]], writes=[ps3], inc=(i == n4 - 1))
                S.op("act", "activation", dict(
                    out=db["csT"].t[:, c4:c4 + n4, :], in_=ps3.t[:, 0:n4 * 128].rearrange("p (c f) -> p c f", c=n4),
                    func=AF.Copy), reads=[ps3], writes=[db["csT"]])
        if want_T:
            S.op("act", "activation", dict(out=db["ecs"].t[:], in_=db["cs"].t[:], func=AF.Exp),
                 reads=[db["cs"]], writes=[db["ecs"]])

    def tok_major(srcT, col0, nblk, c, dst, dcol0):
        tp = psum("tp")
        for i in range(nblk):
            S.op("pe", "transpose", dict(out=tp.t[:, i * 128:(i + 1) * 128],
                                                  in_=srcT[i].t[:, col0 + c * 128:col0 + (c + 1) * 128], identity=IDB()),
                 reads=[srcT[i], cstb], writes=[tp], inc=(i == nblk - 1))
        S.op("act", "activation", dict(out=dst.t[:, c, dcol0:dcol0 + nblk * 128], in_=tp.t[:, 0:nblk * 128], func=AF.Copy),
             reads=[tp], writes=[dst])

    def interleave(ga, gb, ratio=2):
        if not cfg.get("IL", 1):
            for g_ in (gb, ga):
                if g_ is not None:
                    for _ in g_:
                        pass
            return
        alive_a, alive_b = ga is not None, gb is not None
        while alive_a or alive_b:
            if alive_a:
                try:
                    next(ga)
                except StopIteration:
                    alive_a = False
            for _ in range(ratio):
                if alive_b:
                    try:
                        next(gb)
                    except StopIteration:
                        alive_b = False

    def nw_load(es, st, off):
        st["nw"] = sb(es, "nwbc", [128, D])
        S.dma("sp", dict(out=st["nw"].t[:], in_=dram["pb"][:, off:off + D].partition_broadcast(128)), writes=[st["nw"]])

    def group_set(es, nch, tag, need_C):
        Tn = nch * 128
        d = {"xgT": [sb(es, f"xgT{tag}{i}", [128, Tn], BF16) for i in range(4)],
             "BgT": sb(es, f"BgT{tag}", [128, Tn], BF16),
             "xg_tok": sb(es, f"xg_tok{tag}", [128, nch, 512], BF16),
             "Bg_tok": sb(es, f"Bg_tok{tag}", [128, nch, 128], BF16)}
        if need_C:
            d["CgT"] = sb(es, f"CgT{tag}", [128, Tn], BF16)
        return d

    def conv_gen(cbufs, gs, g, hT, tlo, nch, need_C):
        Tn = nch * 128
        wx_ = load_w(dram["w_in"], 0, OFF_X + g * 512, 512)
        wbc = load_w(dram["w_in"], 0, OFF_B + g * 128, 128)
        if need_C:
            load_w(dram["w_in"], 0, OFF_C + g * 128, 128, coff=128, wt=wbc)
        for i in range(4):
            conv5_silu(cbufs, wx_, i * 128, hT, tlo, Tn, g * 4 + i, gs["xgT"][i])
            yield
        conv5_silu(cbufs, wbc, 0, hT, tlo, Tn, 32 + g, gs["BgT"])
        yield
        if need_C:
            conv5_silu(cbufs, wbc, 128, hT, tlo, Tn, 40 + g, gs["CgT"])
            yield
        for c in range(nch):
            tok_major(gs["xgT"], 0, 4, c, gs["xg_tok"], 0)
            tok_major([gs["BgT"]], 0, 1, c, gs["Bg_tok"], 0)
            if c % 2 == 1:
                yield

    NCH1 = BLK1 // 128
    TH1 = BLK1 + 4
    with ExitStack() as es:
        mkpsum(es, mm=3, tp=2, sm=1, ck=2)
        hT = sb(es, "hT1", [128, KD, TH1], BF16)
        st = {"xt": Ring([sb(es, f"xt{i}", [128, D]) for i in range(2)]),
              "xn": Ring([sb(es, f"xn{i}", [128, D], BF16) for i in range(2)]),
              "sm": Ring([sb(es, f"sm{i}", [128, 8]) for i in range(2)])}
        nw_load(es, st, PB_NW)
        cb = {"pre": Ring([sb(es, f"pre{i}", [128, TH1]) for i in range(2)]),
              "acc": Ring([sb(es, f"cacc{i}", [128, BLK1]) for i in range(2)])}
        db = {k: sb(es, "db_" + k, [128, NCH1, 128]) for k in ("dt", "A", "cs", "tot")}
        Et = sb(es, "Et", [128, NCH1, 128])
        wt_ = sb(es, "wt_", [128, NCH1, 128])
        Dt = sb(es, "Dt", [128, 128])
        aF = sb(es, "aF", [128, 64])
        cB = sb(es, "cB", [128, 64])
        pcum = sb(es, "pcum", [128, 64])
        tmpd = sb(es, "tmpd", [128, 64])
        gsets = [group_set(es, NCH1, f"a{i}", False) for i in range(2)]
        wx = Ring([sb(es, f"wx{i}", [128, 512], BF16) for i in range(4)])
        tmpL = Ring([sb(es, f"tmpL{i}", [128, 512]) for i in range(2)])
        RF = sb(es, "RF", [128, SSM_DIM])
        RB = sb(es, "RB", [128, SSM_DIM])
        v3 = lambda ap: ap.rearrange("p (h d) -> p h d", h=8)

        def tail_gen(gs, g, mfa, cBt, aFt):
            xg_tok, Bg_tok = gs["xg_tok"], gs["Bg_tok"]
            Lf = psum("ck")
            Lb = psum("ck")
            pend = None
            for c in range(NCH1 + 1):
                cur = None
                if c < NCH1:
                    cur = []
                    for di in (0, 1):
                        wxt = wx.next()
                        cur.append(wxt)
                        S.op("dve", "tensor_tensor", dict(
                            out=v3(wxt.t[:]), in0=v3(xg_tok.t[:, c, :]),
                            in1=wt_.t[:, c, di * 64 + g * 8:di * 64 + g * 8 + 8].unsqueeze(2).to_broadcast([128, 8, 64]),
                            op=ALU.mult), reads=[xg_tok, wt_], writes=[wxt])
                if pend is not None:
                    pc, pw = pend
                    for di, Lps in ((0, Lf), (1, Lb)):
                        S.op("pe", "matmul", dict(out=Lps.t[:, :], lhsT=Bg_tok.t[:, pc, :], rhs=pw[di].t[:],
                                                  start=(pc == 0), stop=(pc == NCH1 - 1)),
                             reads=[Bg_tok, pw[di]], writes=[Lps], inc=True)
                pend = (c, cur) if cur is not None else None
                yield
            gsl = slice(g * 512, (g + 1) * 512)
            S.op("dve", "tensor_tensor", dict(
                out=v3(RF.t[:, gsl]), in0=v3(RF.t[:, gsl]),
                in1=aFt.t[:, g * 8:g * 8 + 8].unsqueeze(2).to_broadcast([128, 8, 64]), op=ALU.mult),
                reads=[RF, aFt], writes=[RF])
            S.op("dve", "scalar_tensor_tensor", dict(
                out=RF.t[:, gsl], in0=Lf.t[:, :], scalar=mfa, in1=RF.t[:, gsl], op0=ALU.mult, op1=ALU.add),
                reads=[Lf, RF, mskt], writes=[RF])
            tl = tmpL.next()
            S.op("dve", "tensor_tensor", dict(
                out=v3(tl.t[:]), in0=v3(Lb.t[:, :]),
                in1=cBt.t[:, g * 8:g * 8 + 8].unsqueeze(2).to_broadcast([128, 8, 64]), op=ALU.mult),
                reads=[Lb, cBt], writes=[tl])
            S.op("dve", "tensor_tensor", dict(out=RB.t[:, gsl], in0=RB.t[:, gsl], in1=tl.t[:], op=ALU.add),
                 reads=[RB, tl], writes=[RB])

        blk_i = 0
        for si, nblk in enumerate((NBO_P, NBO_S) if cfg.get("P1", 1) else ()):
            S.op("dve", "memset", dict(ap=RF.t[:], constant=0.0), writes=[RF])
            S.op("dve", "memset", dict(ap=RB.t[:], constant=0.0), writes=[RB])
            S.op("dve", "memset", dict(ap=pcum.t[:], constant=1.0), writes=[pcum])
            for j in range(nblk):
                mc = 4 * blk_i
                mf = mskt.t[:, mc:mc + 1]
                omf = mskt.t[:, mc + 1:mc + 2]
                mb = mskt.t[:, mc + 2:mc + 3]
                omb = mskt.t[:, mc + 3:mc + 4]
                load_norm_T(es, dram["xoth"][blk_i], TH1, hT, st)
                blk_i += 1
                dt_block(hT, 2, NCH1, db, want_T=False)
                for c in range(NCH1):
                    ps = psum("sm")
                    for c2 in range(c, NCH1):
                        S.op("pe", "matmul", dict(out=ps.t[:, 0:64], lhsT=ONES(), rhs=db["A"].t[:, c2, 0:64],
                                                  start=(c2 == c), stop=(c2 == NCH1 - 1)),
                             reads=[cst, db["A"]], writes=[ps], inc=False)
                    for c2 in range(0, c + 1):
                        S.op("pe", "matmul", dict(out=ps.t[:, 64:128], lhsT=ONES(), rhs=db["A"].t[:, c2, 64:128],
                                                  start=(c2 == 0), stop=(c2 == c)),
                             reads=[cst, db["A"]], writes=[ps], inc=(c2 == c))
                    S.op("act", "activation", dict(out=Et.t[:, c, :], in_=ps.t[:, 0:128], func=AF.Copy),
                         reads=[ps], writes=[Et])
                S.op("dve", "tensor_tensor", dict(out=wt_.t[:], in0=Et.t[:], in1=db["cs"].t[:], op=ALU.subtract),
                     reads=[Et, db["cs"]], writes=[wt_])
                S.op("act", "activation", dict(out=wt_.t[:], in_=wt_.t[:], func=AF.Exp), reads=[wt_], writes=[wt_])
                S.op("dve", "tensor_tensor", dict(out=wt_.t[:], in0=wt_.t[:], in1=db["dt"].t[:], op=ALU.mult),
                     reads=[wt_, db["dt"]], writes=[wt_])
                S.op("act", "activation", dict(out=Dt.t[:, 0:64], in_=Et.t[:, 0, 0:64], func=AF.Exp),
                     reads=[Et], writes=[Dt])
                S.op("act", "activation", dict(out=Dt.t[:, 64:128], in_=Et.t[:, NCH1 - 1, 64:128], func=AF.Exp),
                     reads=[Et], writes=[Dt])
                S.op("dve", "tensor_scalar", dict(out=aF.t[:], in0=Dt.t[:, 0:64], scalar1=mf, scalar2=omf,
                                                  op0=ALU.mult, op1=ALU.add), reads=[Dt, mskt], writes=[aF])
                S.op("dve", "tensor_scalar", dict(out=cB.t[:], in0=pcum.t[:], scalar1=mb, scalar2=None, op0=ALU.mult),
                     reads=[pcum, mskt], writes=[cB])
                S.op("dve", "tensor_scalar", dict(out=tmpd.t[:], in0=Dt.t[:, 64:128], scalar1=mb, scalar2=omb,
                                                  op0=ALU.mult, op1=ALU.add), reads=[Dt, mskt], writes=[tmpd])
                S.op("dve", "tensor_tensor", dict(out=pcum.t[:], in0=pcum.t[:], in1=tmpd.t[:], op=ALU.mult),
                     reads=[pcum, tmpd], writes=[pcum])
                prev = None
                for g in range(NG):
                    gs = gsets[g % 2]
                    interleave(conv_gen(cb, gs, g, hT, 2, NCH1, False), prev, ratio=1)
                    prev = tail_gen(gs, g, mf, cB, aF)
                interleave(None, prev)
            S.dma("sp", dict(out=acc_scr[2 * si], in_=RF.t[:]), reads=[RF], writes=[scrbuf["acc"][2 * si]])
            S.dma("sp", dict(out=acc_scr[2 * si + 1], in_=RB.t[:]), reads=[RB], writes=[scrbuf["acc"][2 * si + 1]])
        S.barrier()
        flush()

    def chunk_gen(bufs, gs, g, di, nch, db, carry, chunk_cb):
        xg_tok, Bg_tok, BgT, CgT = gs["xg_tok"], gs["Bg_tok"], gs["BgT"], gs["CgT"]
        Sbf = bufs["Sbf"]
        S.op("act", "activation", dict(out=Sbf.t[:], in_=carry.t[:], func=AF.Copy), reads=[carry], writes=[Sbf])
        order = range(nch) if di == 0 else range(nch - 1, -1, -1)
        mask = TRIF if di == 0 else TRIB
        hb = di * 64 + g * 8
        v3 = lambda ap: ap.rearrange("p (h d) -> p h d", h=8)
        for c in order:
            cs_ = slice(c * 128, (c + 1) * 128)
            ps = psum("sm")
            S.op("pe", "matmul", dict(out=ps.t[:, 0:128], lhsT=BgT.t[:, cs_], rhs=CgT.t[:, cs_], start=True, stop=True),
                 reads=[BgT, CgT], writes=[ps])
            psss = []
            for hq in range(2):
                pss = psum("sel")
                psss.append(pss)
                for hh in range(4):
                    hi = hq * 4 + hh
                    S.op("pe", "matmul", dict(out=pss.t[:, hh * 128:(hh + 1) * 128],
                                              lhsT=cst.t[:, hb + hi:hb + hi + 1].to_broadcast([128, 128]),
                                              rhs=db["csT"].t[:, c, :], start=True, stop=True),
                         reads=[cst, db["csT"]], writes=[pss], inc=(hh == 3))
            yield
            CBm = bufs["CBm"].next()
            S.op("dve", "tensor_tensor", dict(out=CBm.t[:], in0=ps.t[:, 0:128], in1=mask(), op=ALU.mult),
                 reads=[ps, cst], writes=[CBm])
            ncs = bufs["ncs"].next()
            S.op("dve", "tensor_scalar", dict(out=ncs.t[:, 0:8], in0=db["cs"].t[:, c, hb:hb + 8], scalar1=-1.0, scalar2=None,
                                              op0=ALU.mult), reads=[db["cs"]], writes=[ncs])
            wv = bufs["wv"].next()
            S.op("dve", "tensor_tensor", dict(out=wv.t[:, 0:8], in0=db["tot"].t[:, c, hb:hb + 8],
                                              in1=db["cs"].t[:, c, hb:hb + 8], op=ALU.subtract),
                 reads=[db["tot"], db["cs"]], writes=[wv])
            S.op("act", "activation", dict(out=wv.t[:, 0:8], in_=wv.t[:, 0:8], func=AF.Exp), reads=[wv], writes=[wv])
            S.op("dve", "tensor_tensor", dict(out=wv.t[:, 0:8], in0=wv.t[:, 0:8], in1=db["dt"].t[:, c, hb:hb + 8], op=ALU.mult),
                 reads=[wv, db["dt"]], writes=[wv])
            S.op("act", "activation", dict(out=wv.t[:, 8:16], in_=db["tot"].t[:, c, hb:hb + 8], func=AF.Exp),
                 reads=[db["tot"]], writes=[wv])
            segs_ = []
            for hi in range(8):
                sg = bufs["seg"].next()
                segs_.append(sg)
                S.op("act", "activation", dict(out=sg.t[:], in_=psss[hi // 4].t[:, (hi % 4) * 128:(hi % 4 + 1) * 128],
                                               func=AF.Exp, bias=ncs.t[:, hi:hi + 1]),
                     reads=[psss[hi // 4], ncs], writes=[sg])
            xdt = bufs["wx"].next()
            S.op("dve", "tensor_tensor", dict(
                out=v3(xdt.t[:]), in0=v3(xg_tok.t[:, c, :]),
                in1=db["dt"].t[:, c, hb:hb + 8].unsqueeze(2).to_broadcast([128, 8, 64]), op=ALU.mult),
                reads=[xg_tok, db["dt"]], writes=[xdt])
            wxt = bufs["wx"].next()
            S.op("dve", "tensor_tensor", dict(
                out=v3(wxt.t[:]), in0=v3(xg_tok.t[:, c, :]), in1=wv.t[:, 0:8].unsqueeze(2).to_broadcast([128, 8, 64]),
                op=ALU.mult), reads=[xg_tok, wv], writes=[wxt])
            Mts = []
            for hi in range(8):
                Mt = bufs["M"].next()
                Mts.append(Mt)
                S.op("dve", "scalar_tensor_tensor", dict(out=Mt.t[:], in0=segs_[hi].t[:], scalar=1.0, in1=CBm.t[:],
                                                         op0=ALU.min, op1=ALU.mult),
                     reads=[segs_[hi], CBm], writes=[Mt])
            yield
            Yps = psum("ck")
            for hi in range(8):
                S.op("pe", "matmul", dict(out=Yps.t[:, hi * 64:(hi + 1) * 64], lhsT=Mts[hi].t[:],
                                          rhs=xdt.t[:, hi * 64:(hi + 1) * 64], start=True, stop=True),
                     reads=[Mts[hi], xdt], writes=[Yps], inc=(hi == 7))
            CSps = psum("ck")
            S.op("pe", "matmul", dict(out=CSps.t[:, :], lhsT=CgT.t[:, cs_], rhs=Sbf.t[:], start=True, stop=True),
                 reads=[CgT, Sbf], writes=[CSps])
            Lps = psum("sm")
            S.op("pe", "matmul", dict(out=Lps.t[:, :], lhsT=Bg_tok.t[:, c, :], rhs=wxt.t[:], start=True, stop=True),
                 reads=[Bg_tok, wxt], writes=[Lps])
            yield
            S.op("dve", "tensor_tensor", dict(
                out=v3(carry.t[:]), in0=v3(carry.t[:]), in1=wv.t[:, 8:16].unsqueeze(2).to_broadcast([128, 8, 64]),
                op=ALU.mult), reads=[carry, wv], writes=[carry])
            S.op("dve", "tensor_tensor", dict(out=carry.t[:], in0=carry.t[:], in1=Lps.t[:, :], op=ALU.add),
                 reads=[carry, Lps], writes=[carry])
            S.op("act", "activation", dict(out=Sbf.t[:], in_=carry.t[:], func=AF.Copy), reads=[carry], writes=[Sbf])
            yield from chunk_cb(c, Yps, CSps, hb)

    def ssd_bufs(es, nch, tag, nbuf=2):
        Tn = nch * 128
        return {
            "cb": {"pre": Ring([sb(es, f"pre{tag}{i}", [128, Tn + 4]) for i in range(nbuf)]),
                   "acc": Ring([sb(es, f"cacc{tag}{i}", [128, Tn]) for i in range(nbuf)])},
            "Sbf": sb(es, f"Sbf{tag}", [128, 512], BF16),
            "CBm": Ring([sb(es, f"CBm{tag}{i}", [128, 128]) for i in range(2)]),
            "ncs": Ring([sb(es, f"ncs{tag}{i}", [128, 8]) for i in range(2)]),
            "seg": Ring([sb(es, f"seg{tag}{i}", [128, 128]) for i in range(8)]),
            "M": Ring([sb(es, f"M{tag}{i}", [128, 128], BF16) for i in range(8)]),
            "wv": Ring([sb(es, f"wv{tag}{i}", [128, 16]) for i in range(2)]),
            "wx": Ring([sb(es, f"wx{tag}{i}", [128, 512], BF16) for i in range(4)]),
        }

    def ssd_block(bufs, gsets, di, hT, tlo, nch, db, carry_g, mk_cb):
        prev = None
        for g in range(NG):
            gs = gsets[g % 2]
            interleave(conv_gen(bufs["cb"], gs, g, hT, tlo, nch, True), prev, ratio=2)
            prev = chunk_gen(bufs, gs, g, di, nch, db, carry_g[g], mk_cb(g, gs))
        interleave(None, prev)

    segs = (("xoP", NP, yP, 0, 0), ("xoS", NS, yS, NP // 128, 1))

    NCH2 = BLK2 // 128
    TH2 = BLK2 + 4
    with ExitStack() as es:
        mkpsum(es, mm=2, tp=1, sm=1, ck=2, sel=2)
        hT = sb(es, "hT2", [128, KD, TH2], BF16)
        st = {"xt": Ring([sb(es, f"xt2{i}", [128, D]) for i in range(2)]),
              "xn": Ring([sb(es, f"xn2{i}", [128, D], BF16) for i in range(2)]),
              "sm": Ring([sb(es, f"sm2{i}", [128, 8]) for i in range(2)])}
        nw_load(es, st, PB_NW)
        db = {k: sb(es, "db2_" + k, [128, NCH2, 128]) for k in ("dt", "A", "cs", "tot", "csT", "ecs")}
        bufs = ssd_bufs(es, NCH2, "p2")
        gsets = [group_set(es, NCH2, f"b{i}", True) for i in range(2)]
        carries = sb(es, "carries2", [128, SSM_DIM])
        carry_g = [T(carries.t[:, g * 512:(g + 1) * 512]) for g in range(NG)]
        yst = Ring([sb(es, f"yst{i}", [128, 512]) for i in range(3)])
        for (xname, ntok, yout, chbase, sidx) in (segs if cfg.get("P2", 1) else ()):
            S.dma("sp", dict(out=carries.t[:], in_=acc_scr[2 * sidx]),
                  reads=[scrbuf["acc"][2 * sidx]], writes=[carries] + carry_g)
            for b0 in range(0, ntok, BLK2):
                row0 = PAD + b0 - 2
                load_norm_T(es, dram[xname][row0:row0 + TH2, :], TH2, hT, st)
                dt_block(hT, 2, NCH2, db, want_T=True)

                def mk_cb(g, gs, b0=b0, chbase=chbase):
                    def cbk(c, Yps, CSps, hb):
                        y = yst.next()
                        S.op("act", "activation", dict(out=y.t[:], in_=Yps.t[:, :], func=AF.Copy), reads=[Yps], writes=[y])
                        for hi in range(8):
                            S.op("dve", "scalar_tensor_tensor", dict(
                                out=y.t[:, hi * 64:(hi + 1) * 64], in0=CSps.t[:, hi * 64:(hi + 1) * 64],
                                scalar=db["ecs"].t[:, c, hb + hi:hb + hi + 1], in1=y.t[:, hi * 64:(hi + 1) * 64],
                                op0=ALU.mult, op1=ALU.add), reads=[CSps, db["ecs"], y], writes=[y])
                        ch = chbase + b0 // 128 + c
                        S.dma("sp", dict(out=yf_scr[ch][:, g * 512:(g + 1) * 512], in_=y.t[:]),
                              reads=[y], writes=[scrbuf["yf"][ch]])
                        yield
                    return cbk
                ssd_block(bufs, gsets, 0, hT, 2, NCH2, db, carry_g, mk_cb)
        S.barrier()
        flush()

    NCH3 = BLK3 // 128
    TH3 = BLK3 + 2 * PAD
    onesb = lambda: cstb.t[:, 384:512]
    with ExitStack() as es:
        hT = sb(es, "hT3", [128, KD, TH3], BF16)
        carries = sb(es, "carries3", [128, SSM_DIM])
        carry_g = [T(carries.t[:, g * 512:(g + 1) * 512]) for g in range(NG)]
        ysT = [sb(es, f"ysT{i}", [128, BLK3], BF16) for i in range(32)]
        ycT = [sb(es, f"ycT{i}", [128, BLK3], BF16) for i in range(16)]

        for (xname, ntok, yout, chbase, sidx) in (segs if cfg.get("P3", 1) else ()):
            S.dma("sp", dict(out=carries.t[:], in_=acc_scr[2 * sidx + 1]),
                  reads=[scrbuf["acc"][2 * sidx + 1]], writes=[carries] + carry_g)
            for b0 in range(ntok - BLK3, -1, -BLK3):
                with ExitStack() as e2:
                    mkpsum(e2, tp=2)
                    st = {"xt": Ring([sb(e2, f"xt3{i}", [128, D]) for i in range(2)]),
                          "xn": Ring([sb(e2, f"xn3{i}", [128, D], BF16) for i in range(2)]),
                          "sm": Ring([sb(e2, f"sm3{i}", [128, 8]) for i in range(2)])}
                    nw_load(e2, st, PB_NW)
                    load_norm_T(e2, dram[xname][b0:b0 + TH3, :], TH3, hT, st)
                    S.barrier()
                    flush()
                with ExitStack() as e2:
                    db = {k: sb(e2, "db3_" + k, [128, NCH3, 128]) for k in ("dt", "A", "cs", "tot", "csT", "ecs")}
                    mkpsum(e2, mm=2, tp=1, sm=1, ck=2, sel=2)
                    bufs = ssd_bufs(e2, NCH3, "p3", nbuf=2)
                    gsets = [group_set(e2, NCH3, f"c{i}", True) for i in range(2)]
                    yb = Ring([sb(e2, f"yb{i}", [128, 512]) for i in range(1)])
                    yfl = Ring([sb(e2, f"yfl{i}", [128, 512]) for i in range(1)])
                    zs = Ring([sb(e2, f"zs{i}", [128, 512]) for i in range(1)])
                    ysn = Ring([sb(e2, f"ysn{i}", [128, 512], BF16) for i in range(2)])
                    sq = sb(e2, "sqj", [128, 512])
                    sm3 = Ring([sb(e2, f"sm3b{i}", [128, 8]) for i in range(2)])
                    dt_block(hT, PAD, NCH3, db, want_T=True)
                    def mk_cb(g, gs, b0=b0, chbase=chbase):
                      wz = load_w(dram["w_in"], 0, OFF_Z + g * 512, 512)

                      def cbk(c, Yps, CSps, hb):
                        if True:
                            y = yb.next()
                            yf = yfl.next()
                            ch = chbase + b0 // 128 + c
                            S.dma("sp", dict(out=yf.t[:], in_=yf_scr[ch][:, g * 512:(g + 1) * 512]),
                                  reads=[scrbuf["yf"][ch]], writes=[yf])
                            S.op("act", "activation", dict(out=y.t[:], in_=Yps.t[:, :], func=AF.Copy), reads=[Yps], writes=[y])
                            for hi in range(8):
                                S.op("dve", "scalar_tensor_tensor", dict(
                                    out=y.t[:, hi * 64:(hi + 1) * 64], in0=CSps.t[:, hi * 64:(hi + 1) * 64],
                                    scalar=db["ecs"].t[:, c, hb + hi:hb + hi + 1], in1=y.t[:, hi * 64:(hi + 1) * 64],
                                    op0=ALU.mult, op1=ALU.add), reads=[CSps, db["ecs"], y], writes=[y])
                            S.op("dve", "tensor_tensor", dict(out=y.t[:], in0=y.t[:], in1=yf.t[:], op=ALU.add),
                                 reads=[y, yf], writes=[y])
                            v3 = lambda ap: ap.rearrange("p (h d) -> p h d", h=8)
                            S.op("dve", "tensor_tensor", dict(
                                out=v3(yf.t[:]), in0=v3(gs["xg_tok"].t[:, c, :]),
                                in1=pbt.t[:, PB_DSK + g * 8:PB_DSK + g * 8 + 8].unsqueeze(2).to_broadcast([128, 8, 64]),
                                op=ALU.mult), reads=[gs["xg_tok"], pbt], writes=[yf])
                            S.op("dve", "tensor_tensor", dict(out=y.t[:], in0=y.t[:], in1=yf.t[:], op=ALU.add),
                                 reads=[y, yf], writes=[y])
                            yield
                            zp = psum("mm")
                            proj_tm(wz, 0, 512, hT, PAD + c * 128, zp, 0)
                            yield
                            z = zs.next()
                            S.op("act", "activation", dict(out=z.t[:], in_=zp.t[:, :], func=AF.Tanh, scale=0.5), reads=[zp], writes=[z])
                            S.op("dve", "scalar_tensor_tensor", dict(out=z.t[:], in0=z.t[:], scalar=1.0, in1=zp.t[:, :],
                                                                     op0=ALU.add, op1=ALU.mult), reads=[z, zp], writes=[z])
                            S.op("dve", "scalar_tensor_tensor", dict(out=y.t[:], in0=y.t[:], scalar=0.5, in1=z.t[:],
                                                                     op0=ALU.mult, op1=ALU.mult), reads=[y, z], writes=[y])
                            sm = sm3.next()
                            S.op("act", "activation", dict(out=sq.t[:], in_=y.t[:], func=AF.Square, accum_out=sm.t[:, 0:1]),
                                 reads=[y], writes=[sq, sm])
                            S.op("dve", "tensor_scalar", dict(out=sm.t[:, 1:2], in0=sm.t[:, 0:1], scalar1=1.0 / 512, scalar2=EPS,
                                                              op0=ALU.mult, op1=ALU.add), reads=[sm], writes=[sm])
                            S.op("act", "activation", dict(out=sm.t[:, 2:3], in_=sm.t[:, 1:2], func=AF.Sqrt), reads=[sm], writes=[sm])
                            S.op("dve", "reciprocal", dict(out=sm.t[:, 3:4], in_=sm.t[:, 2:3]), reads=[sm], writes=[sm])
                            yn = ysn.next()
                            S.op("dve", "tensor_scalar", dict(out=yn.t[:], in0=y.t[:], scalar1=sm.t[:, 3:4], scalar2=None,
                                                              op0=ALU.mult), reads=[y, sm], writes=[yn])
                            tp = psum("tp")
                            for i in range(4):
                                S.op("pe", "transpose", dict(out=tp.t[:, i * 128:(i + 1) * 128],
                                                             in_=yn.t[:, i * 128:(i + 1) * 128], identity=IDB()),
                                     reads=[yn, cstb], writes=[tp], inc=(i == 3))
                            for i in range(4):
                                cbi = g * 4 + i
                                S.op("dve", "tensor_scalar", dict(
                                    out=ysT[cbi].t[:, c * 128:(c + 1) * 128], in0=tp.t[:, i * 128:(i + 1) * 128],
                                    scalar1=ppt.t[:, PP_SNW + cbi:PP_SNW + cbi + 1], scalar2=None, op0=ALU.mult),
                                    reads=[tp, ppt], writes=[ysT[cbi]])
                            yield
                      return cbk
                    ssd_block(bufs, gsets, 1, hT, PAD, NCH3, db, carry_g, mk_cb)
                    S.barrier()
                    flush()

                with ExitStack() as e2:
                    mkpsum(e2, mm=4, sm=2)
                    cpre = Ring([sb(e2, f"cpre{i}", [128, TH3]) for i in range(2)])
                    csig = Ring([sb(e2, f"csig{i}", [128, TH3]) for i in range(2)])
                    cacc = Ring([sb(e2, f"cacc3{i}", [128, BLK3]) for i in range(3)])
                    stat = sb(e2, "stat", [128, 2, BLK3])
                    usq = Ring([sb(e2, f"usq{i}", [128, BLK3], BF16) for i in range(2)])
                    gt = Ring([sb(e2, f"gt{i}", [128, BLK3]) for i in range(2)])
                    stp = [psum("sm"), psum("sm")]
                    for cbi in range(16):
                        if cbi % 4 == 0:
                            wcv = load_w(dram["w_in"], 0, OFF_CV + cbi * 128, 512)
                            wcg = load_w(dram["w_in"], 0, OFF_CG + cbi * 128, 512)
                        co = (cbi % 4) * 128
                        pre = cpre.next()
                        sg_ = csig.next()
                        for (t0, n) in subblocks(0, TH3):
                            ps = proj_fm(wcg, co, hT, t0, n)
                            S.op("act", "activation", dict(out=sg_.t[:, t0:t0 + n], in_=ps.t[:, 0:n], func=AF.Sigmoid),
                                 reads=[ps], writes=[sg_])
                            ps2 = proj_fm(wcv, co, hT, t0, n)
                            S.op("dve", "tensor_tensor", dict(out=pre.t[:, t0:t0 + n], in0=ps2.t[:, 0:n],
                                                              in1=sg_.t[:, t0:t0 + n], op=ALU.mult),
                                 reads=[ps2, sg_], writes=[pre])
                        acc = cacc.next()
                        wb0 = PP_DWW + cbi * 31
                        S.op("dve", "tensor_scalar", dict(out=acc.t[:], in0=pre.t[:, 1:1 + BLK3], scalar1=ppt.t[:, wb0:wb0 + 1],
                                                          scalar2=ppt.t[:, PP_DWB + cbi:PP_DWB + cbi + 1], op0=ALU.mult,
                                                          op1=ALU.add), reads=[pre, ppt], writes=[acc])
                        for k in range(1, CONV_K):
                            S.op("dve", "scalar_tensor_tensor", dict(
                                out=acc.t[:], in0=pre.t[:, 1 + k:1 + k + BLK3], scalar=ppt.t[:, wb0 + k:wb0 + k + 1],
                                in1=acc.t[:], op0=ALU.mult, op1=ALU.add), reads=[pre, ppt, acc], writes=[acc])
                        S.op("act", "activation", dict(out=ycT[cbi].t[:], in_=acc.t[:], func=AF.Copy),
                             reads=[acc], writes=[ycT[cbi]])
                        us = usq.next()
                        S.op("act", "activation", dict(out=us.t[:], in_=ycT[cbi].t[:], func=AF.Square),
                             reads=[ycT[cbi]], writes=[us])
                        S.op("pe", "matmul", dict(out=stp[0].t[:, 0:BLK3], lhsT=onesb(), rhs=ycT[cbi].t[:],
                                                  start=(cbi == 0), stop=(cbi == 15)),
                             reads=[cstb, ycT[cbi]], writes=[stp[0]])
                        S.op("pe", "matmul", dict(out=stp[1].t[:, 0:BLK3], lhsT=onesb(), rhs=us.t[:],
                                                  start=(cbi == 0), stop=(cbi == 15)),
                             reads=[cstb, us], writes=[stp[1]])
                    S.op("dve", "tensor_scalar", dict(out=stat.t[:, 0, :], in0=stp[0].t[:, 0:BLK3], scalar1=1.0 / D, scalar2=None,
                                                      op0=ALU.mult), reads=[stp[0]], writes=[stat])
                    S.op("dve", "tensor_scalar", dict(out=stat.t[:, 1, :], in0=stp[1].t[:, 0:BLK3], scalar1=1.0 / D, scalar2=EPS,
                                                      op0=ALU.mult, op1=ALU.add), reads=[stp[1]], writes=[stat])
                    tmpm = cacc.next()
                    S.op("dve", "tensor_tensor", dict(out=tmpm.t[:], in0=stat.t[:, 0, :], in1=stat.t[:, 0, :], op=ALU.mult),
                         reads=[stat], writes=[tmpm])
                    S.op("dve", "tensor_tensor", dict(out=stat.t[:, 1, :], in0=stat.t[:, 1, :], in1=tmpm.t[:], op=ALU.subtract),
                         reads=[stat, tmpm], writes=[stat])
                    S.op("act", "activation", dict(out=stat.t[:, 1, :], in_=stat.t[:, 1, :], func=AF.Sqrt), reads=[stat], writes=[stat])
                    S.op("dve", "reciprocal", dict(out=stat.t[:, 1, :], in_=stat.t[:, 1, :]), reads=[stat], writes=[stat])
                    for cbi in range(16):
                        if cbi % 4 == 0:
                            wcs = load_w(dram["w_in"], 0, OFF_CS + cbi * 128, 512)
                        co = (cbi % 4) * 128
                        a_ = cacc.next()
                        S.op("dve", "tensor_tensor", dict(out=a_.t[:], in0=ycT[cbi].t[:], in1=stat.t[:, 0, :], op=ALU.subtract),
                             reads=[ycT[cbi], stat], writes=[a_])
                        S.op("dve", "tensor_tensor", dict(out=a_.t[:], in0=a_.t[:], in1=stat.t[:, 1, :], op=ALU.mult),
                             reads=[a_, stat], writes=[a_])
                        S.op("act", "activation", dict(out=a_.t[:], in_=a_.t[:], func=AF.Silu,
                                                       scale=ppt.t[:, PP_LNG + cbi:PP_LNG + cbi + 1],
                                                       bias=ppt.t[:, PP_LNB + cbi:PP_LNB + cbi + 1]),
                             reads=[a_, ppt], writes=[a_])
                        for (t0, n) in subblocks(0, BLK3):
                            ps = proj_fm(wcs, co, hT, PAD + t0, n)
                            g_ = gt.next()
                            S.op("act", "activation", dict(out=g_.t[:, 0:n], in_=ps.t[:, 0:n], func=AF.Silu),
                                 reads=[ps], writes=[g_])
                            S.op("dve", "tensor_tensor", dict(out=ycT[cbi].t[:, t0:t0 + n], in0=a_.t[:, t0:t0 + n],
                                                              in1=g_.t[:, 0:n], op=ALU.mult),
                                 reads=[a_, g_], writes=[ycT[cbi]])
                    S.barrier()
                    flush()

                with ExitStack() as e2:
                    mkpsum(e2, mm=6)
                    fnw = sb(e2, "fnw", [128, D])
                    S.dma("sp", dict(out=fnw.t[:], in_=dram["pb"][:, PB_FNW:PB_FNW + D].partition_broadcast(128)), writes=[fnw])
                    mT = [sb(e2, f"mT{i}", [128, BLK3], BF16) for i in range(16)]
                    o1s = [sb(e2, f"o1s{i}", [128, BLK3]) for i in range(4)]
                    gt = Ring([sb(e2, f"gtc{i}", [128, BLK3]) for i in range(2)])
                    ores = Ring([sb(e2, f"ores{i}", [128, D]) for i in range(1)])
                    xres = Ring([sb(e2, f"xres{i}", [128, D]) for i in range(2)])
                    sm3 = Ring([sb(e2, f"sm3c{i}", [128, 8]) for i in range(2)])
                    for dq in range(4):
                        wgc = load_w(dram["w_in"], 0, OFF_GATE + dq * 512, 512)
                        wb0_ = load_w(dram["w_branch"], 0, dq * 512, 512)
                        for dj in range(4):
                            dbi = dq * 4 + dj
                            co = dj * 128
                            for (t0, n) in subblocks(0, BLK3):
                                poc = psum("mm")
                                for k in range(16):
                                    S.op("pe", "matmul", dict(out=poc.t[:, 0:n], lhsT=wb0_.t[:, k, co:co + 128],
                                                              rhs=ycT[k].t[:, t0:t0 + n], start=(k == 0), stop=(k == 15)),
                                         reads=[wb0_, ycT[k]], writes=[poc], inc=(k == 15))
                                pgc = proj_fm(wgc, co, hT, PAD + t0, n)
                                g1 = gt.next()
                                S.op("act", "activation", dict(out=g1.t[:, 0:n], in_=pgc.t[:, 0:n], func=AF.Sigmoid,
                                                               bias=ppt.t[:, PP_BG + dbi:PP_BG + dbi + 1]),
                                     reads=[pgc, ppt], writes=[g1])
                                S.op("dve", "tensor_tensor", dict(out=o1s[dj].t[:, t0:t0 + n], in0=poc.t[:, 0:n],
                                                                  in1=g1.t[:, 0:n], op=ALU.mult),
                                     reads=[poc, g1], writes=[o1s[dj]])
                        wgs = load_w(dram["w_in"], 0, OFF_GATE + 2048 + dq * 512, 512)
                        wb1_ = load_w(dram["w_branch"], 2048, dq * 512, 512)
                        wb2_ = load_w(dram["w_branch"], 4096, dq * 512, 512)
                        wbs = [wb1_, wb2_]
                        for dj in range(4):
                            dbi = dq * 4 + dj
                            co = dj * 128
                            for (t0, n) in subblocks(0, BLK3):
                                pos = psum("mm")
                                for k in range(32):
                                    S.op("pe", "matmul", dict(out=pos.t[:, 0:n], lhsT=wbs[k // 16].t[:, k % 16, co:co + 128],
                                                              rhs=ysT[k].t[:, t0:t0 + n], start=(k == 0), stop=(k == 31)),
                                         reads=[wbs[k // 16], ysT[k]], writes=[pos], inc=(k == 31))
                                pgs = proj_fm(wgs, co, hT, PAD + t0, n)
                                g2 = gt.next()
                                S.op("act", "activation", dict(out=g2.t[:, 0:n], in_=pgs.t[:, 0:n], func=AF.Sigmoid,
                                                               bias=ppt.t[:, PP_BG + 16 + dbi:PP_BG + 16 + dbi + 1]),
                                     reads=[pgs, ppt], writes=[g2])
                                S.op("dve", "tensor_tensor", dict(out=g2.t[:, 0:n], in0=pos.t[:, 0:n], in1=g2.t[:, 0:n],
                                                                  op=ALU.mult), reads=[pos, g2], writes=[g2])
                                S.op("dve", "tensor_tensor", dict(out=mT[dbi].t[:, t0:t0 + n], in0=o1s[dj].t[:, t0:t0 + n],
                                                                  in1=g2.t[:, 0:n], op=ALU.add),
                                     reads=[o1s[dj], g2], writes=[mT[dbi]])
                    wos = None
                    for c in range(NCH3):
                        xr = xres.next()
                        r0 = PAD + b0 + c * 128
                        S.dma("sp", dict(out=xr.t[:], in_=dram[xname][r0:r0 + 128, :]), writes=[xr])
                        orr = ores.next()
                        for eq in range(4):
                            wo = load_w(dram["w_out"], 0, eq * 512, 512)
                            po = psum("mm")
                            for k in range(16):
                                S.op("pe", "matmul", dict(out=po.t[:, :], lhsT=mT[k].t[:, c * 128:(c + 1) * 128],
                                                          rhs=wo.t[:, k, :], start=(k == 0), stop=(k == 15)),
                                     reads=[wo, mT[k]], writes=[po], inc=(k == 15))
                            S.op("dve", "tensor_tensor", dict(out=orr.t[:, eq * 512:(eq + 1) * 512], in0=po.t[:, :],
                                                              in1=xr.t[:, eq * 512:(eq + 1) * 512], op=ALU.add),
                                 reads=[po, xr], writes=[orr])
                        sm = sm3.next()
                        S.op("act", "activation", dict(out=xr.t[:], in_=orr.t[:], func=AF.Square, accum_out=sm.t[:, 0:1]),
                             reads=[orr], writes=[xr, sm])
                        S.op("dve", "tensor_scalar", dict(out=sm.t[:, 1:2], in0=sm.t[:, 0:1], scalar1=1.0 / D, scalar2=EPS,
                                                          op0=ALU.mult, op1=ALU.add), reads=[sm], writes=[sm])
                        S.op("act", "activation", dict(out=sm.t[:, 2:3], in_=sm.t[:, 1:2], func=AF.Sqrt), reads=[sm], writes=[sm])
                        S.op("dve", "reciprocal", dict(out=sm.t[:, 3:4], in_=sm.t[:, 2:3]), reads=[sm], writes=[sm])
                        S.op("dve", "scalar_tensor_tensor", dict(out=orr.t[:], in0=orr.t[:], scalar=sm.t[:, 3:4],
                                                                 in1=fnw.t[:], op0=ALU.mult, op1=ALU.mult),
                             reads=[orr, sm, fnw], writes=[orr])
                        r1 = b0 + c * 128
                        S.dma("sp", dict(out=yout[r1:r1 + 128, :], in_=orr.t[:]), reads=[orr])
                    S.barrier()
                    S.final_wait("sp")
                    flush()
    top.close()
    return nc


CFG_FULL = dict(NP=1024, NS=2048, BLK1=512, BLK2=512, BLK3=512, NW=3)


def _host_inputs(cfg, x_prompt, x_sample, norm_w, w_in, b_gate, dw_w, dw_b, ln_g, ln_b, sconv_w, sconv_b,
                 dt_bias, a_log, d_skip, ssm_norm_w, w_branch, w_out, final_norm_w):
    NP, NS, BLK1 = cfg["NP"], cfg["NS"], cfg["BLK1"]
    f = lambda a: np.ascontiguousarray(np.asarray(a, dtype=np.float32))
    xp = f(x_prompt)[0]
    xs = f(x_sample)[0]
    xP = np.zeros((xp.shape[0] + 2 * PAD, D), np.float32)
    xP[PAD:-PAD] = xp
    xS = np.zeros((xs.shape[0] + 2 * PAD, D), np.float32)
    xS[PAD:-PAD] = xs
    pp = np.zeros((128, NPP), np.float32)
    pp[:, PP_DWW:PP_DWW + 496] = f(dw_w)[0].reshape(31, 16, 128).transpose(2, 1, 0).reshape(128, 496)
    pp[:, PP_DWB:PP_DWB + 16] = f(dw_b)[0].reshape(16, 128).T
    pp[:, PP_LNG:PP_LNG + 16] = f(ln_g)[0].reshape(16, 128).T
    pp[:, PP_LNB:PP_LNB + 16] = f(ln_b)[0].reshape(16, 128).T
    pp[:, PP_SCW:PP_SCW + 240] = f(sconv_w)[0].reshape(5, 48, 128).transpose(2, 1, 0).reshape(128, 240)
    pp[:, PP_SCB:PP_SCB + 48] = f(sconv_b)[0].reshape(48, 128).T
    pp[:, PP_BG:PP_BG + 32] = f(b_gate)[0].reshape(32, 128).T
    pp[:, PP_SNW:PP_SNW + 32] = f(ssm_norm_w)[0].reshape(32, 128).T
    pb = np.zeros((1, NPB), np.float32)
    pb[0, PB_NW:PB_NW + D] = f(norm_w)[0]
    pb[0, PB_FNW:PB_FNW + D] = f(final_norm_w)
    pb[0, 4096 + PB_DSK:4096 + PB_DSK + 64] = f(d_skip)[0]
    pb[0, 4096 + PB_DTB:4096 + PB_DTB + 128] = f(dt_bias)[0].reshape(128)
    pb[0, 4096 + PB_ALOG:4096 + PB_ALOG + 128] = f(a_log)[0].reshape(128)
    consts = np.zeros((128, 512), np.float32)
    i = np.arange(128)
    consts[:, 0:128] = np.eye(128)
    consts[:, 128:256] = (i[:, None] <= i[None, :])
    consts[:, 256:384] = (i[:, None] >= i[None, :])
    consts[:, 384:512] = 1.0
    common = {"w_in": f(w_in)[0], "w_branch": f(w_branch)[0], "w_out": f(w_out)[0],
              "pp": pp, "pb": pb, "consts": consts}
    nb1p, nb1s = NCORES * NP // BLK1, NCORES * NS // BLK1
    in_maps = []
    for k in range(NCORES):
        m = dict(common)
        m["xoP"] = np.ascontiguousarray(xP[k * NP:(k + 1) * NP + 2 * PAD])
        m["xoS"] = np.ascontiguousarray(xS[k * NS:(k + 1) * NS + 2 * PAD])
        rows, mrow = [], []
        for (xpad, nb, nown) in ((xP, nb1p, NP // BLK1), (xS, nb1s, NS // BLK1)):
            for j in range(nb):
                if k * nown <= j < (k + 1) * nown:
                    continue
                r0 = PAD + j * BLK1 - 2
                rows.append(xpad[r0:r0 + BLK1 + 4])
                mf = 1.0 if j < k * nown else 0.0
                mrow += [mf, 1 - mf, 1 - mf, mf]
        m["xoth"] = np.ascontiguousarray(np.stack(rows, axis=0))
        msk = np.ascontiguousarray(np.broadcast_to(np.asarray(mrow, np.float32)[None, :], (128, len(mrow))))
        m["masks"] = msk
        in_maps.append(m)
    return in_maps


_NC_CACHE = {}


def run(cfg, **inputs):
    key = tuple(sorted(cfg.items()))
    if key not in _NC_CACHE:
        _NC_CACHE[key] = build(cfg)
    nc = _NC_CACHE[key]
    in_maps = _host_inputs(cfg, **inputs)
    res = run_bass_kernel_spmd(nc, in_maps, core_ids=list(range(NCORES)))
    yp = np.concatenate([np.asarray(r["yP"], dtype=np.float32) for r in res.results], axis=0)[None]
    ys = np.concatenate([np.asarray(r["yS"], dtype=np.float32) for r in res.results], axis=0)[None]
    return yp, ys


def kernel(**inputs):
    return run(CFG_FULL, **inputs)
```

```python
import numpy as np
from contextlib import ExitStack
import concourse.bass as bass
import concourse.mybir as mybir
from concourse.bass_utils import run_bass_kernel_spmd

F32 = mybir.dt.float32
BF16 = mybir.dt.bfloat16
AF = mybir.ActivationFunctionType
ALU = mybir.AluOpType

D = 2048
KD = 16
CONV_K = 31
SSM_DIM = 4096
NH = 64
HD = 64
NG = 8
DS = 128
OFF_CV = 0
OFF_CG = 2048
OFF_CS = 4096
OFF_Z = 6144
OFF_X = 10240
OFF_B = OFF_X + 4096
OFF_C = OFF_B + 1024
OFF_DT = 16384
OFF_GATE = 16512
IN_DIM = 20608
EPS = 1e-5
PAD = 16
NCORES = 8

PP_DWW = 0
PP_DWB = 496
PP_LNG = 512
PP_LNB = 528
PP_SCW = 544
PP_SCB = 784
PP_BG = 832
PP_SNW = 864
NPP = 896
PB_NW = 0
PB_FNW = 2048
PB_DSK = 0
PB_DTB = 64
PB_ALOG = 192
NPB = 4416


class Buf:
    __slots__ = ("w", "r")

    def __init__(self):
        self.w = None
        self.r = {}


class T:
    def __init__(self, t):
        self.t = t
        self.b = Buf()


class Sched:
    def __init__(self):
        self.ops = {n: [] for n in ("pe", "act", "dve", "pool", "sp")}
        self.cnt = {n: 0 for n in self.ops}
        self.seen = {n: {} for n in self.ops}
        self.dslots = {"sp": 8, "pool": 6}
        self.dcnt = {"sp": 0, "pool": 0}
        self.semkeys = list(self.ops.keys())
        for q, n in self.dslots.items():
            for i in range(n):
                self.semkeys.append(f"{q}_d{i}")
        self.semval = {k: 0 for k in self.semkeys}

    def _collect(self, eng, reads, writes):
        deps = {}

        def need(d):
            if d is None:
                return
            k, v = d
            if k == "pe" and eng == "pe":
                return
            if deps.get(k, 0) < v:
                deps[k] = v

        for b in reads:
            need(b.w)
        for b in writes:
            need(b.w)
            for d in b.r.values():
                need(d)
        return deps

    def _waits(self, eng, deps):
        seen = self.seen[eng]
        for k, v in deps.items():
            if seen.get(k, 0) < v:
                self.ops[eng].append(("wait", k, v))
                seen[k] = v

    def _mark(self, eng, d, reads, writes):
        for b in reads:
            o = b.r.get(d[0])
            if o is None or o[1] < d[1]:
                b.r[d[0]] = d
        for b in writes:
            b.w = d
            b.r = {}

    def op(self, eng, name, kw, reads=(), writes=(), inc=True):
        fn = (name, kw)
        reads = [x.b if isinstance(x, T) else x for x in reads]
        writes = [x.b if isinstance(x, T) else x for x in writes]
        self._waits(eng, self._collect(eng, reads, writes))
        if inc:
            self.cnt[eng] += 1
            self.ops[eng].append(("ins", fn, eng, 1))
            d = (eng, self.cnt[eng])
            self.semval[eng] = self.cnt[eng]
        else:
            self.ops[eng].append(("ins", fn, None, 0))
            d = (eng, self.cnt[eng] + 1)
        self._mark(eng, d, reads, writes)

    def dma(self, eng, kw, reads=(), writes=()):
        fn = ("dma_start", kw)
        reads = [x.b if isinstance(x, T) else x for x in reads]
        writes = [x.b if isinstance(x, T) else x for x in writes]
        i = self.dcnt[eng]
        self.dcnt[eng] += 1
        ns = self.dslots[eng]
        key = f"{eng}_d{i % ns}"
        deps = self._collect(eng, reads, writes)
        prev = 16 * (i // ns)
        if prev > 0 and deps.get(key, 0) < prev:
            deps[key] = prev
        self._waits(eng, deps)
        val = 16 * (i // ns + 1)
        self.ops[eng].append(("ins", fn, key, 16))
        self.semval[key] = val
        self._mark(eng, (key, val), reads, writes)

    def barrier(self):
        for eng in self.ops:
            deps = {k: v for k, v in self.semval.items() if v > 0 and k != eng}
            self._waits(eng, deps)

    def final_wait(self, eng):
        deps = {k: v for k, v in self.semval.items() if v > 0 and k != eng}
        self._waits(eng, deps)


def build(cfg):
    NP, NS = cfg["NP"], cfg["NS"]
    BLK1, BLK2, BLK3 = cfg["BLK1"], cfg["BLK2"], cfg["BLK3"]
    LP, LS = NCORES * NP, NCORES * NS
    NB1P, NB1S = LP // BLK1, LS // BLK1
    NCH_OWN = (NP + NS) // 128

    nc = bass.Bass("TRN2", target_bir_lowering=False)
    dram = {}

    def din(name, shape):
        dram[name] = nc.dram_tensor(name, list(shape), F32, kind="ExternalInput").ap()

    NBO_P = NB1P - NP // BLK1
    NBO_S = NB1S - NS // BLK1
    NM = 4 * (NBO_P + NBO_S)
    din("xoth", (NBO_P + NBO_S, BLK1 + 4, D))
    din("xoP", (NP + 2 * PAD, D))
    din("xoS", (NS + 2 * PAD, D))
    din("w_in", (D, IN_DIM))
    din("w_branch", (6144, D))
    din("w_out", (D, D))
    din("pp", (128, NPP))
    din("pb", (1, NPB))
    din("consts", (128, 512))
    din("masks", (128, NM))
    yP = nc.dram_tensor("yP", [NP, D], F32, kind="ExternalOutput").ap()
    yS = nc.dram_tensor("yS", [NS, D], F32, kind="ExternalOutput").ap()
    acc_scr = nc.dram_tensor("acc_scr", [4, 128, SSM_DIM], F32).ap()
    yf_scr = nc.dram_tensor("yf_scr", [NCH_OWN, 128, SSM_DIM], F32).ap()
    scrbuf = {"acc": [Buf() for _ in range(4)], "yf": [Buf() for _ in range(NCH_OWN)]}

    S = Sched()
    top = ExitStack()
    sems = {}
    for k in S.semkeys:
        sems[k] = top.enter_context(nc.semaphore(k))

    uid = [0]

    def sb(es, name, shape, dt=F32):
        uid[0] += 1
        return T(es.enter_context(nc.sbuf_tensor(f"{name}_{uid[0]}", list(shape), dt)))

    pools = {}
    rr = {}

    def mkpsum(es, **spec):
        assert sum(spec.values()) <= 8
        pools.clear()
        for kind, n in spec.items():
            if kind == "tp":
                pools[kind] = [T(es.enter_context(nc.psum_tensor(f"ps{kind}{i}_{uid[0]}", [128, 1024], BF16))) for i in range(n)]
            else:
                pools[kind] = [T(es.enter_context(nc.psum_tensor(f"ps{kind}{i}_{uid[0]}", [128, 512], F32))) for i in range(n)]
            uid[0] += 1
            rr[kind] = 0

    def psum(kind):
        lst = pools[kind]
        t = lst[rr[kind] % len(lst)]
        rr[kind] += 1
        return t

    class Ring:
        def __init__(self, items):
            self.items = items
            self.i = 0

        def next(self):
            t = self.items[self.i % len(self.items)]
            self.i += 1
            return t

    cst = sb(top, "cst", [128, 512])
    cstb = sb(top, "cstb", [128, 512], BF16)
    ppt = sb(top, "ppt", [128, NPP])
    pbt = sb(top, "pbt", [128, NPB - 4096])
    mskt = sb(top, "mskt", [128, NM])
    abc = sb(top, "abc", [128, 128])
    sch = sb(top, "sch", [128, 288])
    W = Ring([sb(top, f"W{i}", [128, KD, 512], BF16) for i in range(cfg.get("NW", 3))])

    IDF = lambda: cst.t[:, 0:128]
    TRIF = lambda: cst.t[:, 128:256]
    TRIB = lambda: cst.t[:, 256:384]
    ONES = lambda: cst.t[:, 384:512]
    IDB = lambda: cstb.t[:, 0:128]

    def flush(es_phase=None):
        with nc.Block() as blk:
            def rep(eng):
                def body(h):
                    for e in S.ops[eng]:
                        if e[0] == "wait":
                            h.wait_ge(sems[e[1]], e[2])
                        else:
                            ins = getattr(h, e[1][0])(**e[1][1])
                            if e[2] is not None:
                                ins.then_inc(sems[e[2]], e[3])
                    S.ops[eng] = []
                return body
            blk.tensor(rep("pe"))
            blk.scalar(rep("act"))
            blk.vector(rep("dve"))
            blk.gpsimd(rep("pool"))
            blk.sync(rep("sp"))

    S.dma("sp", dict(out=cst.t[:], in_=dram["consts"]), writes=[cst])
    S.dma("sp", dict(out=ppt.t[:], in_=dram["pp"]), writes=[ppt])
    S.dma("sp", dict(out=pbt.t[:], in_=dram["pb"][:, 4096:NPB].partition_broadcast(128)), writes=[pbt])
    S.dma("sp", dict(out=mskt.t[:], in_=dram["masks"]), writes=[mskt])
    S.op("dve", "tensor_copy", dict(out=cstb.t[:], in_=cst.t[:]), reads=[cst], writes=[cstb])
    S.op("act", "activation", dict(out=abc.t[:], in_=pbt.t[:, PB_ALOG:PB_ALOG + 128], func=AF.Exp),
         reads=[pbt], writes=[abc])
    S.op("dve", "tensor_scalar", dict(out=abc.t[:], in0=abc.t[:], scalar1=-1.0, scalar2=None, op0=ALU.mult),
         reads=[abc], writes=[abc])
    S.op("dve", "tensor_scalar", dict(out=sch.t[:], in0=ppt.t[:, PP_SCW:PP_SCW + 288], scalar1=0.5, scalar2=None, op0=ALU.mult),
         reads=[ppt], writes=[sch])

    def load_w(mat, r0, c0, ncols, coff=0, wt=None):
        if wt is None:
            wt = W.next()
        src = mat[r0:r0 + 2048, c0:c0 + ncols].rearrange("(k p) c -> p k c", p=128)
        S.dma("pool", dict(out=wt.t[:, :, coff:coff + ncols], in_=src), writes=[wt])
        return wt

    def load_norm_T(es, src, nrows, hT, st):
        xt_r, xn_r, sm_r = st["xt"], st["xn"], st["sm"]
        for r0 in range(0, nrows, 128):
            R = min(128, nrows - r0)
            xt = xt_r.next()
            xn = xn_r.next()
            sm = sm_r.next()
            S.dma("sp", dict(out=xt.t[0:R, :], in_=src[r0:r0 + R, :]), writes=[xt])
            S.op("act", "activation", dict(
                out=xn.t[0:R, :], in_=xt.t[0:R, :], func=AF.Square, accum_out=sm.t[0:R, 0:1]),
                reads=[xt], writes=[xn, sm])
            S.op("dve", "tensor_scalar", dict(
                out=sm.t[0:R, 1:2], in0=sm.t[0:R, 0:1], scalar1=1.0 / D, scalar2=EPS, op0=ALU.mult, op1=ALU.add),
                reads=[sm], writes=[sm])
            S.op("act", "activation", dict(out=sm.t[0:R, 2:3], in_=sm.t[0:R, 1:2], func=AF.Sqrt),
                 reads=[sm], writes=[sm])
            S.op("dve", "reciprocal", dict(out=sm.t[0:R, 3:4], in_=sm.t[0:R, 2:3]),
                 reads=[sm], writes=[sm])
            S.op("dve", "scalar_tensor_tensor", dict(
                out=xn.t[0:R, :], in0=xt.t[0:R, :], scalar=sm.t[0:R, 3:4], in1=st["nw"].t[0:R, :],
                op0=ALU.mult, op1=ALU.mult), reads=[xt, sm, st["nw"]], writes=[xn])
            for half in range(2):
                tp = psum("tp")
                for j in range(8):
                    k = half * 8 + j
                    S.op("pe", "transpose", dict(
                        out=tp.t[:, j * 128:j * 128 + R], in_=xn.t[0:R, k * 128:(k + 1) * 128],
                        identity=cstb.t[0:R, 0:R]), reads=[xn, cstb], writes=[tp], inc=(j == 7))
                eng = "act" if half == 0 else "dve"
                src_v = tp.t[:].rearrange("p (j t) -> p j t", j=8)[:, :, 0:R]
                dst_v = hT.t[:, half * 8:half * 8 + 8, r0:r0 + R]
                if eng == "act":
                    S.op("act", "activation", dict(out=dst_v, in_=src_v, func=AF.Copy),
                         reads=[tp], writes=[hT])
                else:
                    S.op("dve", "tensor_copy", dict(out=dst_v, in_=src_v),
                         reads=[tp], writes=[hT])

    def proj_fm(wt, co, hT, t0, n):
        ps = psum("mm")
        for k in range(KD):
            S.op("pe", "matmul", dict(out=ps.t[:, 0:n], lhsT=wt.t[:, k, co:co + 128], rhs=hT.t[:, k, t0:t0 + n],
                start=(k == 0), stop=(k == KD - 1)), reads=[wt, hT], writes=[ps], inc=(k == KD - 1))
        return ps

    def proj_tm(wt, co, ncols, hT, t0, ps, po):
        for k in range(KD):
            S.op("pe", "matmul", dict(out=ps.t[:, po:po + ncols], lhsT=hT.t[:, k, t0:t0 + 128], rhs=wt.t[:, k, co:co + ncols],
                start=(k == 0), stop=(k == KD - 1)), reads=[wt, hT], writes=[ps], inc=(k == KD - 1))

    def subblocks(t0, n):
        k = (n + 511) // 512
        out = []
        for i in range(k):
            a = (n * i) // k
            b = (n * (i + 1)) // k
            out.append((t0 + a, b - a))
        return out

    def conv5_silu(es_bufs, wt, co, hT, tlo, Tn, pidx, outT, out_off=0):
        pre = es_bufs["pre"].next()
        acc = es_bufs["acc"].next()
        for (t0, n) in subblocks(tlo - 2, Tn + 4):
            ps = proj_fm(wt, co, hT, t0, n)
            o = t0 - (tlo - 2)
            S.op("act", "activation", dict(out=pre.t[:, o:o + n], in_=ps.t[:, 0:n], func=AF.Copy),
                 reads=[ps], writes=[pre])
        wbase = pidx * 5
        S.op("dve", "tensor_scalar", dict(out=acc.t[:, 0:Tn], in0=pre.t[:, 0:Tn], scalar1=sch.t[:, wbase:wbase + 1],
                                          scalar2=sch.t[:, 240 + pidx:240 + pidx + 1], op0=ALU.mult, op1=ALU.add),
             reads=[pre, sch], writes=[acc])
        for k in range(1, 5):
            S.op("dve", "scalar_tensor_tensor", dict(
                out=acc.t[:, 0:Tn], in0=pre.t[:, k:k + Tn], scalar=sch.t[:, wbase + k:wbase + k + 1],
                in1=acc.t[:, 0:Tn], op0=ALU.mult, op1=ALU.add), reads=[pre, sch, acc], writes=[acc])
        th = pre
        S.op("act", "activation", dict(out=th.t[:, 0:Tn], in_=acc.t[:, 0:Tn], func=AF.Tanh), reads=[acc], writes=[th])
        S.op("dve", "scalar_tensor_tensor", dict(out=outT.t[:, out_off:out_off + Tn], in0=th.t[:, 0:Tn], scalar=1.0,
                                                 in1=acc.t[:, 0:Tn], op0=ALU.add, op1=ALU.mult),
             reads=[th, acc], writes=[outT])

    def dt_block(hT, tlo, nch, db, want_T):
        wt = load_w(dram["w_in"], 0, OFF_DT, 128)
        for c4 in range(0, nch, 4):
            n4 = min(4, nch - c4)
            ps = psum("sm")
            for i in range(n4):
                proj_tm(wt, 0, 128, hT, tlo + (c4 + i) * 128, ps, i * 128)
            sl = lambda t: t.t[:, c4:c4 + n4, :]
            psv = ps.t[:, 0:n4 * 128].rearrange("p (c f) -> p c f", c=n4)
            bias_v = pbt.t[:, PB_DTB:PB_DTB + 128].unsqueeze(1).to_broadcast([128, n4, 128])
            a_v = abc.t[:].unsqueeze(1).to_broadcast([128, n4, 128])
            S.op("dve", "tensor_tensor", dict(out=sl(db["A"]), in0=psv, in1=bias_v, op=ALU.add),
                 reads=[ps, pbt], writes=[db["A"]])
            S.op("dve", "scalar_tensor_tensor", dict(out=sl(db["dt"]), in0=sl(db["A"]), scalar=-1.0, in1=sl(db["A"]),
                                                     op0=ALU.mult, op1=ALU.max),
                 reads=[db["A"]], writes=[db["dt"]])
            S.op("act", "activation", dict(out=sl(db["dt"]), in_=sl(db["dt"]), func=AF.Exp, scale=-1.0),
                 reads=[db["dt"]], writes=[db["dt"]])
            S.op("act", "activation", dict(out=sl(db["dt"]), in_=sl(db["dt"]), func=AF.Ln, bias=1.0),
                 reads=[db["dt"]], writes=[db["dt"]])
            S.op("dve", "scalar_tensor_tensor", dict(out=sl(db["dt"]), in0=sl(db["A"]), scalar=0.0, in1=sl(db["dt"]),
                                                         op0=ALU.max, op1=ALU.add),
                 reads=[db["A"], db["dt"]], writes=[db["dt"]])
            S.op("dve", "tensor_tensor", dict(out=sl(db["A"]), in0=sl(db["dt"]), in1=a_v, op=ALU.mult),
                 reads=[db["dt"], abc], writes=[db["A"]])
        for c4 in range(0, nch, 4):
            n4 = min(4, nch - c4)
            ps = psum("sm")
            ps2 = psum("mm")
            for i in range(n4):
                c = c4 + i
                S.op("pe", "matmul", dict(out=ps.t[:, i * 128:i * 128 + 64], lhsT=TRIF(), rhs=db["A"].t[:, c, 0:64],
                                                        start=True, stop=True), reads=[cst, db["A"]], writes=[ps], inc=False)
                S.op("pe", "matmul", dict(out=ps.t[:, i * 128 + 64:i * 128 + 128], lhsT=TRIB(),
                                                        rhs=db["A"].t[:, c, 64:128], start=True, stop=True),
                     reads=[cst, db["A"]], writes=[ps], inc=False)
                S.op("pe", "matmul", dict(out=ps2.t[:, i * 128:i * 128 + 128], lhsT=ONES(), rhs=db["A"].t[:, c, :],
                                                        start=True, stop=True), reads=[cst, db["A"]], writes=[ps2],
                     inc=(i == n4 - 1))
            S.op("act", "activation", dict(
                out=db["cs"].t[:, c4:c4 + n4, :], in_=ps.t[:, 0:n4 * 128].rearrange("p (c f) -> p c f", c=n4), func=AF.Copy),
                reads=[ps], writes=[db["cs"]])
            S.op("dve", "tensor_copy", dict(
                out=db["tot"].t[:, c4:c4 + n4, :], in_=ps2.t[:, 0:n4 * 128].rearrange("p (c f) -> p c f", c=n4)),
                reads=[ps2], writes=[db["tot"]])
            if want_T:
                ps3 = psum("ck")
                for i in range(n4):
                    c = c4 + i
                    S.op("pe", "matmul", dict(out=ps3.t[0:64, i * 128:(i + 1) * 128], lhsT=db["A"].t[:, c, 0:64],
                                                            rhs=TRIF(), start=True, stop=True),
                         reads=[cst, db["A"]], writes=[ps3], inc=False)
                    S.op("pe", "matmul", dict(out=ps3.t[64:128, i * 128:(i + 1) * 128],
                                                            lhsT=db["A"].t[:, c, 64:128], rhs=TRIB(), start=True, stop=True),
                         reads=[cst, db["A"]], writes=[ps3], inc=(i == n4 - 1))
                S.op("act", "activation", dict(
                    out=db["csT"].t[:, c4:c4 + n4, :], in_=ps3.t[:, 0:n4 * 128].rearrange("p (c f) -> p c f", c=n4),
                    func=AF.Copy), reads=[ps3], writes=[db["csT"]])
        if want_T:
            S.op("act", "activation", dict(out=db["ecs"].t[:], in_=db["cs"].t[:], func=AF.Exp),
                 reads=[db["cs"]], writes=[db["ecs"]])

    def tok_major(srcT, col0, nblk, c, dst, dcol0):
        tp = psum("tp")
        for i in range(nblk):
            S.op("pe", "transpose", dict(out=tp.t[:, i * 128:(i + 1) * 128],
                                                  in_=srcT[i].t[:, col0 + c * 128:col0 + (c + 1) * 128], identity=IDB()),
                 reads=[srcT[i], cstb], writes=[tp], inc=(i == nblk - 1))
        S.op("act", "activation", dict(out=dst.t[:, c, dcol0:dcol0 + nblk * 128], in_=tp.t[:, 0:nblk * 128], func=AF.Copy),
             reads=[tp], writes=[dst])

    def interleave(ga, gb, ratio=2):
        if not cfg.get("IL", 1):
            for g_ in (gb, ga):
                if g_ is not None:
                    for _ in g_:
                        pass
            return
        alive_a, alive_b = ga is not None, gb is not None
        while alive_a or alive_b:
            if alive_a:
                try:
                    next(ga)
                except StopIteration:
                    alive_a = False
            for _ in range(ratio):
                if alive_b:
                    try:
                        next(gb)
                    except StopIteration:
                        alive_b = False

    def nw_load(es, st, off):
        st["nw"] = sb(es, "nwbc", [128, D])
        S.dma("sp", dict(out=st["nw"].t[:], in_=dram["pb"][:, off:off + D].partition_broadcast(128)), writes=[st["nw"]])

    def group_set(es, nch, tag, need_C, need_xT=True):
        Tn = nch * 128
        pack = sb(es, f"gpack{tag}", [128, 7 * Tn], BF16)
        d = {"pack": pack,
             "xg_tok": T(pack.t[:, 0:4 * Tn].rearrange("p (c f) -> p c f", c=nch)),
             "Bg_tok": T(pack.t[:, 4 * Tn:5 * Tn].rearrange("p (c f) -> p c f", c=nch)),
             "BgT": T(pack.t[:, 5 * Tn:6 * Tn]),
             "CgT": T(pack.t[:, 6 * Tn:7 * Tn])}
        if need_xT:
            d["xgT"] = [sb(es, f"xgT{tag}{i}", [128, Tn], BF16) for i in range(4)]
        return d

    def conv_gen(cbufs, gs, g, hT, tlo, nch, need_C):
        Tn = nch * 128
        wx_ = load_w(dram["w_in"], 0, OFF_X + g * 512, 512)
        wbc = load_w(dram["w_in"], 0, OFF_B + g * 128, 128)
        if need_C:
            load_w(dram["w_in"], 0, OFF_C + g * 128, 128, coff=128, wt=wbc)
        for i in range(4):
            conv5_silu(cbufs, wx_, i * 128, hT, tlo, Tn, g * 4 + i, gs["xgT"][i])
            yield
        conv5_silu(cbufs, wbc, 0, hT, tlo, Tn, 32 + g, gs["BgT"])
        yield
        if need_C:
            conv5_silu(cbufs, wbc, 128, hT, tlo, Tn, 40 + g, gs["CgT"])
            yield
        for c in range(nch):
            tok_major(gs["xgT"], 0, 4, c, gs["xg_tok"], 0)
            tok_major([gs["BgT"]], 0, 1, c, gs["Bg_tok"], 0)
            if c % 2 == 1:
                yield

    NCH1 = BLK1 // 128
    TH1 = BLK1 + 4
    with ExitStack() as es:
        mkpsum(es, mm=3, tp=2, sm=1, ck=2)
        hT = sb(es, "hT1", [128, KD, TH1], BF16)
        st = {"xt": Ring([sb(es, f"xt{i}", [128, D]) for i in range(2)]),
              "xn": Ring([sb(es, f"xn{i}", [128, D], BF16) for i in range(2)]),
              "sm": Ring([sb(es, f"sm{i}", [128, 8]) for i in range(2)])}
        nw_load(es, st, PB_NW)
        cb = {"pre": Ring([sb(es, f"pre{i}", [128, TH1]) for i in range(2)]),
              "acc": Ring([sb(es, f"cacc{i}", [128, BLK1]) for i in range(2)])}
        db = {k: sb(es, "db_" + k, [128, NCH1, 128]) for k in ("dt", "A", "cs", "tot")}
        Et = sb(es, "Et", [128, NCH1, 128])
        wt_ = sb(es, "wt_", [128, NCH1, 128])
        Dt = sb(es, "Dt", [128, 128])
        aF = sb(es, "aF", [128, 64])
        cB = sb(es, "cB", [128, 64])
        pcum = sb(es, "pcum", [128, 64])
        tmpd = sb(es, "tmpd", [128, 64])
        gsets = [group_set(es, NCH1, f"a{i}", False) for i in range(2)]
        wx = Ring([sb(es, f"wx{i}", [128, 512], BF16) for i in range(4)])
        tmpL = Ring([sb(es, f"tmpL{i}", [128, 512]) for i in range(2)])
        RF = sb(es, "RF", [128, SSM_DIM])
        RB = sb(es, "RB", [128, SSM_DIM])
        v3 = lambda ap: ap.rearrange("p (h d) -> p h d", h=8)

        def tail_gen(gs, g, mfa, cBt, aFt):
            xg_tok, Bg_tok = gs["xg_tok"], gs["Bg_tok"]
            Lf = psum("ck")
            Lb = psum("ck")
            pend = None
            for c in range(NCH1 + 1):
                cur = None
                if c < NCH1:
                    cur = []
                    for di in (0, 1):
                        wxt = wx.next()
                        cur.append(wxt)
                        S.op("dve", "tensor_tensor", dict(
                            out=v3(wxt.t[:]), in0=v3(xg_tok.t[:, c, :]),
                            in1=wt_.t[:, c, di * 64 + g * 8:di * 64 + g * 8 + 8].unsqueeze(2).to_broadcast([128, 8, 64]),
                            op=ALU.mult), reads=[xg_tok, wt_], writes=[wxt])
                if pend is not None:
                    pc, pw = pend
                    for di, Lps in ((0, Lf), (1, Lb)):
                        S.op("pe", "matmul", dict(out=Lps.t[:, :], lhsT=Bg_tok.t[:, pc, :], rhs=pw[di].t[:],
                                                  start=(pc == 0), stop=(pc == NCH1 - 1)),
                             reads=[Bg_tok, pw[di]], writes=[Lps], inc=True)
                pend = (c, cur) if cur is not None else None
                yield
            gsl = slice(g * 512, (g + 1) * 512)
            S.op("dve", "tensor_tensor", dict(
                out=v3(RF.t[:, gsl]), in0=v3(RF.t[:, gsl]),
                in1=aFt.t[:, g * 8:g * 8 + 8].unsqueeze(2).to_broadcast([128, 8, 64]), op=ALU.mult),
                reads=[RF, aFt], writes=[RF])
            S.op("dve", "scalar_tensor_tensor", dict(
                out=RF.t[:, gsl], in0=Lf.t[:, :], scalar=mfa, in1=RF.t[:, gsl], op0=ALU.mult, op1=ALU.add),
                reads=[Lf, RF, mskt], writes=[RF])
            tl = tmpL.next()
            S.op("dve", "tensor_tensor", dict(
                out=v3(tl.t[:]), in0=v3(Lb.t[:, :]),
                in1=cBt.t[:, g * 8:g * 8 + 8].unsqueeze(2).to_broadcast([128, 8, 64]), op=ALU.mult),
                reads=[Lb, cBt], writes=[tl])
            S.op("dve", "tensor_tensor", dict(out=RB.t[:, gsl], in0=RB.t[:, gsl], in1=tl.t[:], op=ALU.add),
                 reads=[RB, tl], writes=[RB])

        blk_i = 0
        for si, nblk in enumerate((NBO_P, NBO_S) if cfg.get("P1", 1) else ()):
            S.op("dve", "memset", dict(ap=RF.t[:], constant=0.0), writes=[RF])
            S.op("dve", "memset", dict(ap=RB.t[:], constant=0.0), writes=[RB])
            S.op("dve", "memset", dict(ap=pcum.t[:], constant=1.0), writes=[pcum])
            for j in range(nblk):
                mc = 4 * blk_i
                mf = mskt.t[:, mc:mc + 1]
                omf = mskt.t[:, mc + 1:mc + 2]
                mb = mskt.t[:, mc + 2:mc + 3]
                omb = mskt.t[:, mc + 3:mc + 4]
                load_norm_T(es, dram["xoth"][blk_i], TH1, hT, st)
                blk_i += 1
                dt_block(hT, 2, NCH1, db, want_T=False)
                for c in range(NCH1):
                    ps = psum("sm")
                    for c2 in range(c, NCH1):
                        S.op("pe", "matmul", dict(out=ps.t[:, 0:64], lhsT=ONES(), rhs=db["A"].t[:, c2, 0:64],
                                                  start=(c2 == c), stop=(c2 == NCH1 - 1)),
                             reads=[cst, db["A"]], writes=[ps], inc=False)
                    for c2 in range(0, c + 1):
                        S.op("pe", "matmul", dict(out=ps.t[:, 64:128], lhsT=ONES(), rhs=db["A"].t[:, c2, 64:128],
                                                  start=(c2 == 0), stop=(c2 == c)),
                             reads=[cst, db["A"]], writes=[ps], inc=(c2 == c))
                    S.op("act", "activation", dict(out=Et.t[:, c, :], in_=ps.t[:, 0:128], func=AF.Copy),
                         reads=[ps], writes=[Et])
                S.op("dve", "tensor_tensor", dict(out=wt_.t[:], in0=Et.t[:], in1=db["cs"].t[:], op=ALU.subtract),
                     reads=[Et, db["cs"]], writes=[wt_])
                S.op("act", "activation", dict(out=wt_.t[:], in_=wt_.t[:], func=AF.Exp), reads=[wt_], writes=[wt_])
                S.op("dve", "tensor_tensor", dict(out=wt_.t[:], in0=wt_.t[:], in1=db["dt"].t[:], op=ALU.mult),
                     reads=[wt_, db["dt"]], writes=[wt_])
                S.op("act", "activation", dict(out=Dt.t[:, 0:64], in_=Et.t[:, 0, 0:64], func=AF.Exp),
                     reads=[Et], writes=[Dt])
                S.op("act", "activation", dict(out=Dt.t[:, 64:128], in_=Et.t[:, NCH1 - 1, 64:128], func=AF.Exp),
                     reads=[Et], writes=[Dt])
                S.op("dve", "tensor_scalar", dict(out=aF.t[:], in0=Dt.t[:, 0:64], scalar1=mf, scalar2=omf,
                                                  op0=ALU.mult, op1=ALU.add), reads=[Dt, mskt], writes=[aF])
                S.op("dve", "tensor_scalar", dict(out=cB.t[:], in0=pcum.t[:], scalar1=mb, scalar2=None, op0=ALU.mult),
                     reads=[pcum, mskt], writes=[cB])
                S.op("dve", "tensor_scalar", dict(out=tmpd.t[:], in0=Dt.t[:, 64:128], scalar1=mb, scalar2=omb,
                                                  op0=ALU.mult, op1=ALU.add), reads=[Dt, mskt], writes=[tmpd])
                S.op("dve", "tensor_tensor", dict(out=pcum.t[:], in0=pcum.t[:], in1=tmpd.t[:], op=ALU.mult),
                     reads=[pcum, tmpd], writes=[pcum])
                prev = None
                for g in range(NG):
                    gs = gsets[g % 2]
                    interleave(conv_gen(cb, gs, g, hT, 2, NCH1, False), prev, ratio=1)
                    prev = tail_gen(gs, g, mf, cB, aF)
                interleave(None, prev)
            S.dma("sp", dict(out=acc_scr[2 * si], in_=RF.t[:]), reads=[RF], writes=[scrbuf["acc"][2 * si]])
            S.dma("sp", dict(out=acc_scr[2 * si + 1], in_=RB.t[:]), reads=[RB], writes=[scrbuf["acc"][2 * si + 1]])
        S.barrier()
        flush()

    def chunk_gen(bufs, gs, g, di, nch, db, carry, chunk_cb):
        xg_tok, Bg_tok, BgT, CgT = gs["xg_tok"], gs["Bg_tok"], gs["BgT"], gs["CgT"]
        Sbf = bufs["Sbf"]
        S.op("act", "activation", dict(out=Sbf.t[:], in_=carry.t[:], func=AF.Copy), reads=[carry], writes=[Sbf])
        order = range(nch) if di == 0 else range(nch - 1, -1, -1)
        mask = TRIF if di == 0 else TRIB
        hb = di * 64 + g * 8
        v3 = lambda ap: ap.rearrange("p (h d) -> p h d", h=8)
        for c in order:
            cs_ = slice(c * 128, (c + 1) * 128)
            ps = psum("sm")
            S.op("pe", "matmul", dict(out=ps.t[:, 0:128], lhsT=BgT.t[:, cs_], rhs=CgT.t[:, cs_], start=True, stop=True),
                 reads=[BgT, CgT], writes=[ps])
            psss = []
            for hq in range(2):
                pss = psum("sel")
                psss.append(pss)
                for hh in range(4):
                    hi = hq * 4 + hh
                    S.op("pe", "matmul", dict(out=pss.t[:, hh * 128:(hh + 1) * 128],
                                              lhsT=cst.t[:, hb + hi:hb + hi + 1].to_broadcast([128, 128]),
                                              rhs=db["csT"].t[:, c, :], start=True, stop=True),
                         reads=[cst, db["csT"]], writes=[pss], inc=(hh == 3))
            yield
            CBm = bufs["CBm"].next()
            S.op("dve", "tensor_tensor", dict(out=CBm.t[:], in0=ps.t[:, 0:128], in1=mask(), op=ALU.mult),
                 reads=[ps, cst], writes=[CBm])
            ncs = bufs["ncs"].next()
            S.op("dve", "tensor_scalar", dict(out=ncs.t[:, 0:8], in0=db["cs"].t[:, c, hb:hb + 8], scalar1=-1.0, scalar2=None,
                                              op0=ALU.mult), reads=[db["cs"]], writes=[ncs])
            wv = bufs["wv"].next()
            S.op("dve", "tensor_tensor", dict(out=wv.t[:, 0:8], in0=db["tot"].t[:, c, hb:hb + 8],
                                              in1=db["cs"].t[:, c, hb:hb + 8], op=ALU.subtract),
                 reads=[db["tot"], db["cs"]], writes=[wv])
            S.op("act", "activation", dict(out=wv.t[:, 0:8], in_=wv.t[:, 0:8], func=AF.Exp), reads=[wv], writes=[wv])
            S.op("dve", "tensor_tensor", dict(out=wv.t[:, 0:8], in0=wv.t[:, 0:8], in1=db["dt"].t[:, c, hb:hb + 8], op=ALU.mult),
                 reads=[wv, db["dt"]], writes=[wv])
            S.op("act", "activation", dict(out=wv.t[:, 8:16], in_=db["tot"].t[:, c, hb:hb + 8], func=AF.Exp),
                 reads=[db["tot"]], writes=[wv])
            segs_ = []
            for hi in range(8):
                sg = bufs["seg"].next()
                segs_.append(sg)
                S.op("act", "activation", dict(out=sg.t[:], in_=psss[hi // 4].t[:, (hi % 4) * 128:(hi % 4 + 1) * 128],
                                               func=AF.Exp, bias=ncs.t[:, hi:hi + 1]),
                     reads=[psss[hi // 4], ncs], writes=[sg])
            xdt = bufs["wx"].next()
            S.op("dve", "tensor_tensor", dict(
                out=v3(xdt.t[:]), in0=v3(xg_tok.t[:, c, :]),
                in1=db["dt"].t[:, c, hb:hb + 8].unsqueeze(2).to_broadcast([128, 8, 64]), op=ALU.mult),
                reads=[xg_tok, db["dt"]], writes=[xdt])
            wxt = bufs["wx"].next()
            S.op("dve", "tensor_tensor", dict(
                out=v3(wxt.t[:]), in0=v3(xg_tok.t[:, c, :]), in1=wv.t[:, 0:8].unsqueeze(2).to_broadcast([128, 8, 64]),
                op=ALU.mult), reads=[xg_tok, wv], writes=[wxt])
            Mts = []
            for hi in range(8):
                Mt = bufs["M"].next()
                Mts.append(Mt)
                S.op("dve", "scalar_tensor_tensor", dict(out=Mt.t[:], in0=segs_[hi].t[:], scalar=1.0, in1=CBm.t[:],
                                                         op0=ALU.min, op1=ALU.mult),
                     reads=[segs_[hi], CBm], writes=[Mt])
            yield
            Yps = psum("ck")
            for hi in range(8):
                S.op("pe", "matmul", dict(out=Yps.t[:, hi * 64:(hi + 1) * 64], lhsT=Mts[hi].t[:],
                                          rhs=xdt.t[:, hi * 64:(hi + 1) * 64], start=True, stop=True),
                     reads=[Mts[hi], xdt], writes=[Yps], inc=(hi == 7))
            CSps = psum("ck")
            S.op("pe", "matmul", dict(out=CSps.t[:, :], lhsT=CgT.t[:, cs_], rhs=Sbf.t[:], start=True, stop=True),
                 reads=[CgT, Sbf], writes=[CSps])
            Lps = psum("sm")
            S.op("pe", "matmul", dict(out=Lps.t[:, :], lhsT=Bg_tok.t[:, c, :], rhs=wxt.t[:], start=True, stop=True),
                 reads=[Bg_tok, wxt], writes=[Lps])
            yield
            S.op("dve", "tensor_tensor", dict(
                out=v3(carry.t[:]), in0=v3(carry.t[:]), in1=wv.t[:, 8:16].unsqueeze(2).to_broadcast([128, 8, 64]),
                op=ALU.mult), reads=[carry, wv], writes=[carry])
            S.op("dve", "tensor_tensor", dict(out=carry.t[:], in0=carry.t[:], in1=Lps.t[:, :], op=ALU.add),
                 reads=[carry, Lps], writes=[carry])
            S.op("act", "activation", dict(out=Sbf.t[:], in_=carry.t[:], func=AF.Copy), reads=[carry], writes=[Sbf])
            yield from chunk_cb(c, Yps, CSps, hb)

    def ssd_bufs(es, nch, tag, nbuf=2):
        Tn = nch * 128
        return {
            "cb": {"pre": Ring([sb(es, f"pre{tag}{i}", [128, Tn + 4]) for i in range(nbuf)]),
                   "acc": Ring([sb(es, f"cacc{tag}{i}", [128, Tn]) for i in range(nbuf)])},
            "Sbf": sb(es, f"Sbf{tag}", [128, 512], BF16),
            "CBm": Ring([sb(es, f"CBm{tag}{i}", [128, 128]) for i in range(2)]),
            "ncs": Ring([sb(es, f"ncs{tag}{i}", [128, 8]) for i in range(2)]),
            "seg": Ring([sb(es, f"seg{tag}{i}", [128, 128]) for i in range(8)]),
            "M": Ring([sb(es, f"M{tag}{i}", [128, 128], BF16) for i in range(8)]),
            "wv": Ring([sb(es, f"wv{tag}{i}", [128, 16]) for i in range(2)]),
            "wx": Ring([sb(es, f"wx{tag}{i}", [128, 512], BF16) for i in range(4)]),
        }

    gsp = nc.dram_tensor("gsp", [(NP + NS) // BLK2 * NG, 128, 7 * BLK2], BF16).ap()
    gsp_buf = [Buf() for _ in range((NP + NS) // BLK2 * NG)]
    assert BLK2 == BLK3

    def ssd_block(bufs, gsets, di, hT, tlo, nch, db, carry_g, mk_cb, blk_idx, reload):
        prev = None
        for g in range(NG):
            gs = gsets[g % 2]
            parts = [gs["xg_tok"], gs["Bg_tok"], gs["BgT"], gs["CgT"]]
            idx = blk_idx * NG + g
            if reload:
                S.dma("sp", dict(out=gs["pack"].t[:], in_=gsp[idx]), reads=[gsp_buf[idx]], writes=[gs["pack"]] + parts)
                interleave(None, prev)
            else:
                interleave(conv_gen(bufs["cb"], gs, g, hT, tlo, nch, True), prev, ratio=2)
                S.dma("sp", dict(out=gsp[idx], in_=gs["pack"].t[:]), reads=[gs["pack"]] + parts, writes=[gsp_buf[idx]])
            prev = chunk_gen(bufs, gs, g, di, nch, db, carry_g[g], mk_cb(g, gs))
        interleave(None, prev)

    segs = (("xoP", NP, yP, 0, 0), ("xoS", NS, yS, NP // 128, 1))

    NCH2 = BLK2 // 128
    TH2 = BLK2 + 4
    with ExitStack() as es:
        mkpsum(es, mm=2, tp=1, sm=1, ck=2, sel=2)
        hT = sb(es, "hT2", [128, KD, TH2], BF16)
        st = {"xt": Ring([sb(es, f"xt2{i}", [128, D]) for i in range(2)]),
              "xn": Ring([sb(es, f"xn2{i}", [128, D], BF16) for i in range(2)]),
              "sm": Ring([sb(es, f"sm2{i}", [128, 8]) for i in range(2)])}
        nw_load(es, st, PB_NW)
        db = {k: sb(es, "db2_" + k, [128, NCH2, 128]) for k in ("dt", "A", "cs", "tot", "csT", "ecs")}
        bufs = ssd_bufs(es, NCH2, "p2")
        gsets = [group_set(es, NCH2, f"b{i}", True) for i in range(2)]
        carries = sb(es, "carries2", [128, SSM_DIM])
        carry_g = [T(carries.t[:, g * 512:(g + 1) * 512]) for g in range(NG)]
        yst = Ring([sb(es, f"yst{i}", [128, 512]) for i in range(3)])
        for (xname, ntok, yout, chbase, sidx) in (segs if cfg.get("P2", 1) else ()):
            S.dma("sp", dict(out=carries.t[:], in_=acc_scr[2 * sidx]),
                  reads=[scrbuf["acc"][2 * sidx]], writes=[carries] + carry_g)
            for b0 in range(0, ntok, BLK2):
                row0 = PAD + b0 - 2
                load_norm_T(es, dram[xname][row0:row0 + TH2, :], TH2, hT, st)
                dt_block(hT, 2, NCH2, db, want_T=True)

                def mk_cb(g, gs, b0=b0, chbase=chbase):
                    def cbk(c, Yps, CSps, hb):
                        y = yst.next()
                        S.op("act", "activation", dict(out=y.t[:], in_=Yps.t[:, :], func=AF.Copy), reads=[Yps], writes=[y])
                        for hi in range(8):
                            S.op("dve", "scalar_tensor_tensor", dict(
                                out=y.t[:, hi * 64:(hi + 1) * 64], in0=CSps.t[:, hi * 64:(hi + 1) * 64],
                                scalar=db["ecs"].t[:, c, hb + hi:hb + hi + 1], in1=y.t[:, hi * 64:(hi + 1) * 64],
                                op0=ALU.mult, op1=ALU.add), reads=[CSps, db["ecs"], y], writes=[y])
                        ch = chbase + b0 // 128 + c
                        S.dma("sp", dict(out=yf_scr[ch][:, g * 512:(g + 1) * 512], in_=y.t[:]),
                              reads=[y], writes=[scrbuf["yf"][ch]])
                        yield
                    return cbk
                ssd_block(bufs, gsets, 0, hT, 2, NCH2, db, carry_g, mk_cb, chbase * 128 // BLK2 + b0 // BLK2, False)
        S.barrier()
        flush()

    NCH3 = BLK3 // 128
    TH3 = BLK3 + 2 * PAD
    onesb = lambda: cstb.t[:, 384:512]
    with ExitStack() as es:
        hT = sb(es, "hT3", [128, KD, TH3], BF16)
        carries = sb(es, "carries3", [128, SSM_DIM])
        carry_g = [T(carries.t[:, g * 512:(g + 1) * 512]) for g in range(NG)]
        ysT = [sb(es, f"ysT{i}", [128, BLK3], BF16) for i in range(32)]
        ycT = [sb(es, f"ycT{i}", [128, BLK3], BF16) for i in range(16)]

        for (xname, ntok, yout, chbase, sidx) in (segs if cfg.get("P3", 1) else ()):
            S.dma("sp", dict(out=carries.t[:], in_=acc_scr[2 * sidx + 1]),
                  reads=[scrbuf["acc"][2 * sidx + 1]], writes=[carries] + carry_g)
            for b0 in range(ntok - BLK3, -1, -BLK3):
                with ExitStack() as e2:
                    mkpsum(e2, tp=2)
                    st = {"xt": Ring([sb(e2, f"xt3{i}", [128, D]) for i in range(2)]),
                          "xn": Ring([sb(e2, f"xn3{i}", [128, D], BF16) for i in range(2)]),
                          "sm": Ring([sb(e2, f"sm3{i}", [128, 8]) for i in range(2)])}
                    nw_load(e2, st, PB_NW)
                    load_norm_T(e2, dram[xname][b0:b0 + TH3, :], TH3, hT, st)
                    S.barrier()
                    flush()
                with ExitStack() as e2:
                    db = {k: sb(e2, "db3_" + k, [128, NCH3, 128]) for k in ("dt", "A", "cs", "tot", "csT", "ecs")}
                    mkpsum(e2, mm=2, tp=1, sm=1, ck=2, sel=2)
                    bufs = ssd_bufs(e2, NCH3, "p3", nbuf=2)
                    gsets = [group_set(e2, NCH3, f"c{i}", True, need_xT=False) for i in range(2)]
                    yb = Ring([sb(e2, f"yb{i}", [128, 512]) for i in range(1)])
                    yfl = Ring([sb(e2, f"yfl{i}", [128, 512]) for i in range(1)])
                    zs = Ring([sb(e2, f"zs{i}", [128, 512]) for i in range(1)])
                    ysn = Ring([sb(e2, f"ysn{i}", [128, 512], BF16) for i in range(2)])
                    sq = sb(e2, "sqj", [128, 512])
                    sm3 = Ring([sb(e2, f"sm3b{i}", [128, 8]) for i in range(2)])
                    dt_block(hT, PAD, NCH3, db, want_T=True)
                    def mk_cb(g, gs, b0=b0, chbase=chbase):
                      wz = load_w(dram["w_in"], 0, OFF_Z + g * 512, 512)

                      def cbk(c, Yps, CSps, hb):
                        if True:
                            y = yb.next()
                            yf = yfl.next()
                            ch = chbase + b0 // 128 + c
                            S.dma("sp", dict(out=yf.t[:], in_=yf_scr[ch][:, g * 512:(g + 1) * 512]),
                                  reads=[scrbuf["yf"][ch]], writes=[yf])
                            S.op("act", "activation", dict(out=y.t[:], in_=Yps.t[:, :], func=AF.Copy), reads=[Yps], writes=[y])
                            for hi in range(8):
                                S.op("dve", "scalar_tensor_tensor", dict(
                                    out=y.t[:, hi * 64:(hi + 1) * 64], in0=CSps.t[:, hi * 64:(hi + 1) * 64],
                                    scalar=db["ecs"].t[:, c, hb + hi:hb + hi + 1], in1=y.t[:, hi * 64:(hi + 1) * 64],
                                    op0=ALU.mult, op1=ALU.add), reads=[CSps, db["ecs"], y], writes=[y])
                            S.op("dve", "tensor_tensor", dict(out=y.t[:], in0=y.t[:], in1=yf.t[:], op=ALU.add),
                                 reads=[y, yf], writes=[y])
                            v3 = lambda ap: ap.rearrange("p (h d) -> p h d", h=8)
                            S.op("dve", "tensor_tensor", dict(
                                out=v3(yf.t[:]), in0=v3(gs["xg_tok"].t[:, c, :]),
                                in1=pbt.t[:, PB_DSK + g * 8:PB_DSK + g * 8 + 8].unsqueeze(2).to_broadcast([128, 8, 64]),
                                op=ALU.mult), reads=[gs["xg_tok"], pbt], writes=[yf])
                            S.op("dve", "tensor_tensor", dict(out=y.t[:], in0=y.t[:], in1=yf.t[:], op=ALU.add),
                                 reads=[y, yf], writes=[y])
                            yield
                            zp = psum("mm")
                            proj_tm(wz, 0, 512, hT, PAD + c * 128, zp, 0)
                            yield
                            z = zs.next()
                            S.op("act", "activation", dict(out=z.t[:], in_=zp.t[:, :], func=AF.Tanh, scale=0.5), reads=[zp], writes=[z])
                            S.op("dve", "scalar_tensor_tensor", dict(out=z.t[:], in0=z.t[:], scalar=1.0, in1=zp.t[:, :],
                                                                     op0=ALU.add, op1=ALU.mult), reads=[z, zp], writes=[z])
                            S.op("dve", "scalar_tensor_tensor", dict(out=y.t[:], in0=y.t[:], scalar=0.5, in1=z.t[:],
                                                                     op0=ALU.mult, op1=ALU.mult), reads=[y, z], writes=[y])
                            sm = sm3.next()
                            S.op("act", "activation", dict(out=sq.t[:], in_=y.t[:], func=AF.Square, accum_out=sm.t[:, 0:1]),
                                 reads=[y], writes=[sq, sm])
                            S.op("dve", "tensor_scalar", dict(out=sm.t[:, 1:2], in0=sm.t[:, 0:1], scalar1=1.0 / 512, scalar2=EPS,
                                                              op0=ALU.mult, op1=ALU.add), reads=[sm], writes=[sm])
                            S.op("act", "activation", dict(out=sm.t[:, 2:3], in_=sm.t[:, 1:2], func=AF.Sqrt), reads=[sm], writes=[sm])
                            S.op("dve", "reciprocal", dict(out=sm.t[:, 3:4], in_=sm.t[:, 2:3]), reads=[sm], writes=[sm])
                            yn = ysn.next()
                            S.op("dve", "tensor_scalar", dict(out=yn.t[:], in0=y.t[:], scalar1=sm.t[:, 3:4], scalar2=None,
                                                              op0=ALU.mult), reads=[y, sm], writes=[yn])
                            tp = psum("tp")
                            for i in range(4):
                                S.op("pe", "transpose", dict(out=tp.t[:, i * 128:(i + 1) * 128],
                                                             in_=yn.t[:, i * 128:(i + 1) * 128], identity=IDB()),
                                     reads=[yn, cstb], writes=[tp], inc=(i == 3))
                            for i in range(4):
                                cbi = g * 4 + i
                                S.op("dve", "tensor_scalar", dict(
                                    out=ysT[cbi].t[:, c * 128:(c + 1) * 128], in0=tp.t[:, i * 128:(i + 1) * 128],
                                    scalar1=ppt.t[:, PP_SNW + cbi:PP_SNW + cbi + 1], scalar2=None, op0=ALU.mult),
                                    reads=[tp, ppt], writes=[ysT[cbi]])
                            yield
                      return cbk
                    ssd_block(bufs, gsets, 1, hT, PAD, NCH3, db, carry_g, mk_cb, chbase * 128 // BLK3 + b0 // BLK3, True)
                    S.barrier()
                    flush()

                with ExitStack() as e2:
                    mkpsum(e2, mm=4, sm=2)
                    cpre = Ring([sb(e2, f"cpre{i}", [128, TH3]) for i in range(2)])
                    csig = Ring([sb(e2, f"csig{i}", [128, TH3]) for i in range(2)])
                    cacc = Ring([sb(e2, f"cacc3{i}", [128, BLK3]) for i in range(3)])
                    stat = sb(e2, "stat", [128, 2, BLK3])
                    usq = Ring([sb(e2, f"usq{i}", [128, BLK3], BF16) for i in range(2)])
                    gt = Ring([sb(e2, f"gt{i}", [128, BLK3]) for i in range(2)])
                    stp = [psum("sm"), psum("sm")]
                    for cbi in range(16):
                        if cbi % 4 == 0:
                            wcv = load_w(dram["w_in"], 0, OFF_CV + cbi * 128, 512)
                            wcg = load_w(dram["w_in"], 0, OFF_CG + cbi * 128, 512)
                        co = (cbi % 4) * 128
                        pre = cpre.next()
                        sg_ = csig.next()
                        for (t0, n) in subblocks(0, TH3):
                            ps = proj_fm(wcg, co, hT, t0, n)
                            S.op("act", "activation", dict(out=sg_.t[:, t0:t0 + n], in_=ps.t[:, 0:n], func=AF.Sigmoid),
                                 reads=[ps], writes=[sg_])
                            ps2 = proj_fm(wcv, co, hT, t0, n)
                            S.op("dve", "tensor_tensor", dict(out=pre.t[:, t0:t0 + n], in0=ps2.t[:, 0:n],
                                                              in1=sg_.t[:, t0:t0 + n], op=ALU.mult),
                                 reads=[ps2, sg_], writes=[pre])
                        acc = cacc.next()
                        wb0 = PP_DWW + cbi * 31
                        S.op("dve", "tensor_scalar", dict(out=acc.t[:], in0=pre.t[:, 1:1 + BLK3], scalar1=ppt.t[:, wb0:wb0 + 1],
                                                          scalar2=ppt.t[:, PP_DWB + cbi:PP_DWB + cbi + 1], op0=ALU.mult,
                                                          op1=ALU.add), reads=[pre, ppt], writes=[acc])
                        for k in range(1, CONV_K):
                            S.op("dve", "scalar_tensor_tensor", dict(
                                out=acc.t[:], in0=pre.t[:, 1 + k:1 + k + BLK3], scalar=ppt.t[:, wb0 + k:wb0 + k + 1],
                                in1=acc.t[:], op0=ALU.mult, op1=ALU.add), reads=[pre, ppt, acc], writes=[acc])
                        S.op("act", "activation", dict(out=ycT[cbi].t[:], in_=acc.t[:], func=AF.Copy),
                             reads=[acc], writes=[ycT[cbi]])
                        us = usq.next()
                        S.op("act", "activation", dict(out=us.t[:], in_=ycT[cbi].t[:], func=AF.Square),
                             reads=[ycT[cbi]], writes=[us])
                        S.op("pe", "matmul", dict(out=stp[0].t[:, 0:BLK3], lhsT=onesb(), rhs=ycT[cbi].t[:],
                                                  start=(cbi == 0), stop=(cbi == 15)),
                             reads=[cstb, ycT[cbi]], writes=[stp[0]])
                        S.op("pe", "matmul", dict(out=stp[1].t[:, 0:BLK3], lhsT=onesb(), rhs=us.t[:],
                                                  start=(cbi == 0), stop=(cbi == 15)),
                             reads=[cstb, us], writes=[stp[1]])
                    S.op("dve", "tensor_scalar", dict(out=stat.t[:, 0, :], in0=stp[0].t[:, 0:BLK3], scalar1=1.0 / D, scalar2=None,
                                                      op0=ALU.mult), reads=[stp[0]], writes=[stat])
                    S.op("dve", "tensor_scalar", dict(out=stat.t[:, 1, :], in0=stp[1].t[:, 0:BLK3], scalar1=1.0 / D, scalar2=EPS,
                                                      op0=ALU.mult, op1=ALU.add), reads=[stp[1]], writes=[stat])
                    tmpm = cacc.next()
                    S.op("dve", "tensor_tensor", dict(out=tmpm.t[:], in0=stat.t[:, 0, :], in1=stat.t[:, 0, :], op=ALU.mult),
                         reads=[stat], writes=[tmpm])
                    S.op("dve", "tensor_tensor", dict(out=stat.t[:, 1, :], in0=stat.t[:, 1, :], in1=tmpm.t[:], op=ALU.subtract),
                         reads=[stat, tmpm], writes=[stat])
                    S.op("act", "activation", dict(out=stat.t[:, 1, :], in_=stat.t[:, 1, :], func=AF.Sqrt), reads=[stat], writes=[stat])
                    S.op("dve", "reciprocal", dict(out=stat.t[:, 1, :], in_=stat.t[:, 1, :]), reads=[stat], writes=[stat])
                    for cbi in range(16):
                        if cbi % 4 == 0:
                            wcs = load_w(dram["w_in"], 0, OFF_CS + cbi * 128, 512)
                        co = (cbi % 4) * 128
                        a_ = cacc.next()
                        S.op("dve", "tensor_tensor", dict(out=a_.t[:], in0=ycT[cbi].t[:], in1=stat.t[:, 0, :], op=ALU.subtract),
                             reads=[ycT[cbi], stat], writes=[a_])
                        S.op("dve", "tensor_tensor", dict(out=a_.t[:], in0=a_.t[:], in1=stat.t[:, 1, :], op=ALU.mult),
                             reads=[a_, stat], writes=[a_])
                        S.op("act", "activation", dict(out=a_.t[:], in_=a_.t[:], func=AF.Silu,
                                                       scale=ppt.t[:, PP_LNG + cbi:PP_LNG + cbi + 1],
                                                       bias=ppt.t[:, PP_LNB + cbi:PP_LNB + cbi + 1]),
                             reads=[a_, ppt], writes=[a_])
                        for (t0, n) in subblocks(0, BLK3):
                            ps = proj_fm(wcs, co, hT, PAD + t0, n)
                            g_ = gt.next()
                            S.op("act", "activation", dict(out=g_.t[:, 0:n], in_=ps.t[:, 0:n], func=AF.Silu),
                                 reads=[ps], writes=[g_])
                            S.op("dve", "tensor_tensor", dict(out=ycT[cbi].t[:, t0:t0 + n], in0=a_.t[:, t0:t0 + n],
                                                              in1=g_.t[:, 0:n], op=ALU.mult),
                                 reads=[a_, g_], writes=[ycT[cbi]])
                    S.barrier()
                    flush()

                with ExitStack() as e2:
                    mkpsum(e2, mm=6)
                    fnw = sb(e2, "fnw", [128, D])
                    S.dma("sp", dict(out=fnw.t[:], in_=dram["pb"][:, PB_FNW:PB_FNW + D].partition_broadcast(128)), writes=[fnw])
                    mT = [sb(e2, f"mT{i}", [128, BLK3], BF16) for i in range(16)]
                    o1s = [sb(e2, f"o1s{i}", [128, BLK3]) for i in range(4)]
                    gt = Ring([sb(e2, f"gtc{i}", [128, BLK3]) for i in range(2)])
                    ores = Ring([sb(e2, f"ores{i}", [128, D]) for i in range(1)])
                    xres = Ring([sb(e2, f"xres{i}", [128, D]) for i in range(2)])
                    sm3 = Ring([sb(e2, f"sm3c{i}", [128, 8]) for i in range(2)])
                    for dq in range(4):
                        wgc = load_w(dram["w_in"], 0, OFF_GATE + dq * 512, 512)
                        wb0_ = load_w(dram["w_branch"], 0, dq * 512, 512)
                        for dj in range(4):
                            dbi = dq * 4 + dj
                            co = dj * 128
                            for (t0, n) in subblocks(0, BLK3):
                                poc = psum("mm")
                                for k in range(16):
                                    S.op("pe", "matmul", dict(out=poc.t[:, 0:n], lhsT=wb0_.t[:, k, co:co + 128],
                                                              rhs=ycT[k].t[:, t0:t0 + n], start=(k == 0), stop=(k == 15)),
                                         reads=[wb0_, ycT[k]], writes=[poc], inc=(k == 15))
                                pgc = proj_fm(wgc, co, hT, PAD + t0, n)
                                g1 = gt.next()
                                S.op("act", "activation", dict(out=g1.t[:, 0:n], in_=pgc.t[:, 0:n], func=AF.Sigmoid,
                                                               bias=ppt.t[:, PP_BG + dbi:PP_BG + dbi + 1]),
                                     reads=[pgc, ppt], writes=[g1])
                                S.op("dve", "tensor_tensor", dict(out=o1s[dj].t[:, t0:t0 + n], in0=poc.t[:, 0:n],
                                                                  in1=g1.t[:, 0:n], op=ALU.mult),
                                     reads=[poc, g1], writes=[o1s[dj]])
                        wgs = load_w(dram["w_in"], 0, OFF_GATE + 2048 + dq * 512, 512)
                        wb1_ = load_w(dram["w_branch"], 2048, dq * 512, 512)
                        wb2_ = load_w(dram["w_branch"], 4096, dq * 512, 512)
                        wbs = [wb1_, wb2_]
                        for dj in range(4):
                            dbi = dq * 4 + dj
                            co = dj * 128
                            for (t0, n) in subblocks(0, BLK3):
                                pos = psum("mm")
                                for k in range(32):
                                    S.op("pe", "matmul", dict(out=pos.t[:, 0:n], lhsT=wbs[k // 16].t[:, k % 16, co:co + 128],
                                                              rhs=ysT[k].t[:, t0:t0 + n], start=(k == 0), stop=(k == 31)),
                                         reads=[wbs[k // 16], ysT[k]], writes=[pos], inc=(k == 31))
                                pgs = proj_fm(wgs, co, hT, PAD + t0, n)
                                g2 = gt.next()
                                S.op("act", "activation", dict(out=g2.t[:, 0:n], in_=pgs.t[:, 0:n], func=AF.Sigmoid,
                                                               bias=ppt.t[:, PP_BG + 16 + dbi:PP_BG + 16 + dbi + 1]),
                                     reads=[pgs, ppt], writes=[g2])
                                S.op("dve", "tensor_tensor", dict(out=g2.t[:, 0:n], in0=pos.t[:, 0:n], in1=g2.t[:, 0:n],
                                                                  op=ALU.mult), reads=[pos, g2], writes=[g2])
                                S.op("dve", "tensor_tensor", dict(out=mT[dbi].t[:, t0:t0 + n], in0=o1s[dj].t[:, t0:t0 + n],
                                                                  in1=g2.t[:, 0:n], op=ALU.add),
                                     reads=[o1s[dj], g2], writes=[mT[dbi]])
                    wos = None
                    for c in range(NCH3):
                        xr = xres.next()
                        r0 = PAD + b0 + c * 128
                        S.dma("sp", dict(out=xr.t[:], in_=dram[xname][r0:r0 + 128, :]), writes=[xr])
                        orr = ores.next()
                        for eq in range(4):
                            wo = load_w(dram["w_out"], 0, eq * 512, 512)
                            po = psum("mm")
                            for k in range(16):
                                S.op("pe", "matmul", dict(out=po.t[:, :], lhsT=mT[k].t[:, c * 128:(c + 1) * 128],
                                                          rhs=wo.t[:, k, :], start=(k == 0), stop=(k == 15)),
                                     reads=[wo, mT[k]], writes=[po], inc=(k == 15))
                            S.op("dve", "tensor_tensor", dict(out=orr.t[:, eq * 512:(eq + 1) * 512], in0=po.t[:, :],
                                                              in1=xr.t[:, eq * 512:(eq + 1) * 512], op=ALU.add),
                                 reads=[po, xr], writes=[orr])
                        sm = sm3.next()
                        S.op("act", "activation", dict(out=xr.t[:], in_=orr.t[:], func=AF.Square, accum_out=sm.t[:, 0:1]),
                             reads=[orr], writes=[xr, sm])
                        S.op("dve", "tensor_scalar", dict(out=sm.t[:, 1:2], in0=sm.t[:, 0:1], scalar1=1.0 / D, scalar2=EPS,
                                                          op0=ALU.mult, op1=ALU.add), reads=[sm], writes=[sm])
                        S.op("act", "activation", dict(out=sm.t[:, 2:3], in_=sm.t[:, 1:2], func=AF.Sqrt), reads=[sm], writes=[sm])
                        S.op("dve", "reciprocal", dict(out=sm.t[:, 3:4], in_=sm.t[:, 2:3]), reads=[sm], writes=[sm])
                        S.op("dve", "scalar_tensor_tensor", dict(out=orr.t[:], in0=orr.t[:], scalar=sm.t[:, 3:4],
                                                                 in1=fnw.t[:], op0=ALU.mult, op1=ALU.mult),
                             reads=[orr, sm, fnw], writes=[orr])
                        r1 = b0 + c * 128
                        S.dma("sp", dict(out=yout[r1:r1 + 128, :], in_=orr.t[:]), reads=[orr])
                    S.barrier()
                    S.final_wait("sp")
                    flush()
    top.close()
    return nc


CFG_FULL = dict(NP=1024, NS=2048, BLK1=512, BLK2=512, BLK3=512, NW=3)


def _host_inputs(cfg, x_prompt, x_sample, norm_w, w_in, b_gate, dw_w, dw_b, ln_g, ln_b, sconv_w, sconv_b,
                 dt_bias, a_log, d_skip, ssm_norm_w, w_branch, w_out, final_norm_w):
    NP, NS, BLK1 = cfg["NP"], cfg["NS"], cfg["BLK1"]
    f = lambda a: np.ascontiguousarray(np.asarray(a, dtype=np.float32))
    xp = f(x_prompt)[0]
    xs = f(x_sample)[0]
    xP = np.zeros((xp.shape[0] + 2 * PAD, D), np.float32)
    xP[PAD:-PAD] = xp
    xS = np.zeros((xs.shape[0] + 2 * PAD, D), np.float32)
    xS[PAD:-PAD] = xs
    pp = np.zeros((128, NPP), np.float32)
    pp[:, PP_DWW:PP_DWW + 496] = f(dw_w)[0].reshape(31, 16, 128).transpose(2, 1, 0).reshape(128, 496)
    pp[:, PP_DWB:PP_DWB + 16] = f(dw_b)[0].reshape(16, 128).T
    pp[:, PP_LNG:PP_LNG + 16] = f(ln_g)[0].reshape(16, 128).T
    pp[:, PP_LNB:PP_LNB + 16] = f(ln_b)[0].reshape(16, 128).T
    pp[:, PP_SCW:PP_SCW + 240] = f(sconv_w)[0].reshape(5, 48, 128).transpose(2, 1, 0).reshape(128, 240)
    pp[:, PP_SCB:PP_SCB + 48] = f(sconv_b)[0].reshape(48, 128).T
    pp[:, PP_BG:PP_BG + 32] = f(b_gate)[0].reshape(32, 128).T
    pp[:, PP_SNW:PP_SNW + 32] = f(ssm_norm_w)[0].reshape(32, 128).T
    pb = np.zeros((1, NPB), np.float32)
    pb[0, PB_NW:PB_NW + D] = f(norm_w)[0]
    pb[0, PB_FNW:PB_FNW + D] = f(final_norm_w)
    pb[0, 4096 + PB_DSK:4096 + PB_DSK + 64] = f(d_skip)[0]
    pb[0, 4096 + PB_DTB:4096 + PB_DTB + 128] = f(dt_bias)[0].reshape(128)
    pb[0, 4096 + PB_ALOG:4096 + PB_ALOG + 128] = f(a_log)[0].reshape(128)
    consts = np.zeros((128, 512), np.float32)
    i = np.arange(128)
    consts[:, 0:128] = np.eye(128)
    consts[:, 128:256] = (i[:, None] <= i[None, :])
    consts[:, 256:384] = (i[:, None] >= i[None, :])
    consts[:, 384:512] = 1.0
    common = {"w_in": f(w_in)[0], "w_branch": f(w_branch)[0], "w_out": f(w_out)[0],
              "pp": pp, "pb": pb, "consts": consts}
    nb1p, nb1s = NCORES * NP // BLK1, NCORES * NS // BLK1
    in_maps = []
    for k in range(NCORES):
        m = dict(common)
        m["xoP"] = np.ascontiguousarray(xP[k * NP:(k + 1) * NP + 2 * PAD])
        m["xoS"] = np.ascontiguousarray(xS[k * NS:(k + 1) * NS + 2 * PAD])
        rows, mrow = [], []
        for (xpad, nb, nown) in ((xP, nb1p, NP // BLK1), (xS, nb1s, NS // BLK1)):
            for j in range(nb):
                if k * nown <= j < (k + 1) * nown:
                    continue
                r0 = PAD + j * BLK1 - 2
                rows.append(xpad[r0:r0 + BLK1 + 4])
                mf = 1.0 if j < k * nown else 0.0
                mrow += [mf, 1 - mf, 1 - mf, mf]
        m["xoth"] = np.ascontiguousarray(np.stack(rows, axis=0))
        msk = np.ascontiguousarray(np.broadcast_to(np.asarray(mrow, np.float32)[None, :], (128, len(mrow))))
        m["masks"] = msk
        in_maps.append(m)
    return in_maps


_NC_CACHE = {}


def run(cfg, **inputs):
    key = tuple(sorted(cfg.items()))
    if key not in _NC_CACHE:
        _NC_CACHE[key] = build(cfg)
    nc = _NC_CACHE[key]
    in_maps = _host_inputs(cfg, **inputs)
    res = run_bass_kernel_spmd(nc, in_maps, core_ids=list(range(NCORES)))
    yp = np.concatenate([np.asarray(r["yP"], dtype=np.float32) for r in res.results], axis=0)[None]
    ys = np.concatenate([np.asarray(r["yS"], dtype=np.float32) for r in res.results], axis=0)[None]
    return yp, ys


def kernel(**inputs):
    return run(CFG_FULL, **inputs)
```

```python
import numpy as np
from contextlib import ExitStack
import concourse.bass as bass
import concourse.mybir as mybir
from concourse.bass_utils import run_bass_kernel_spmd

F32 = mybir.dt.float32
BF16 = mybir.dt.bfloat16
AF = mybir.ActivationFunctionType
ALU = mybir.AluOpType

D = 2048
KD = 16
CONV_K = 31
SSM_DIM = 4096
NH = 64
HD = 64
NG = 8
DS = 128
OFF_CV = 0
OFF_CG = 2048
OFF_CS = 4096
OFF_Z = 6144
OFF_X = 10240
OFF_B = OFF_X + 4096
OFF_C = OFF_B + 1024
OFF_DT = 16384
OFF_GATE = 16512
IN_DIM = 20608
EPS = 1e-5
PAD = 16
NCORES = 8

PP_DWW = 0
PP_DWB = 496
PP_LNG = 512
PP_LNB = 528
PP_SCW = 544
PP_SCB = 784
PP_BG = 832
PP_SNW = 864
NPP = 896
PB_NW = 0
PB_FNW = 2048
PB_DSK = 0
PB_DTB = 64
PB_ALOG = 192
NPB = 4416


class Buf:
    __slots__ = ("w", "r")

    def __init__(self):
        self.w = None
        self.r = {}


class T:
    def __init__(self, t):
        self.t = t
        self.b = Buf()


class Sched:
    def __init__(self):
        self.ops = {n: [] for n in ("pe", "act", "dve", "pool", "sp")}
        self.cnt = {n: 0 for n in self.ops}
        self.seen = {n: {} for n in self.ops}
        self.dslots = {"sp": 8, "pool": 6}
        self.dcnt = {"sp": 0, "pool": 0}
        self.semkeys = list(self.ops.keys())
        for q, n in self.dslots.items():
            for i in range(n):
                self.semkeys.append(f"{q}_d{i}")
        self.semval = {k: 0 for k in self.semkeys}

    def _collect(self, eng, reads, writes):
        deps = {}

        def need(d):
            if d is None:
                return
            k, v = d
            if k == "pe" and eng == "pe":
                return
            if deps.get(k, 0) < v:
                deps[k] = v

        for b in reads:
            need(b.w)
        for b in writes:
            need(b.w)
            for d in b.r.values():
                need(d)
        return deps

    def _waits(self, eng, deps):
        seen = self.seen[eng]
        for k, v in deps.items():
            if seen.get(k, 0) < v:
                self.ops[eng].append(("wait", k, v))
                seen[k] = v

    def _mark(self, eng, d, reads, writes):
        for b in reads:
            o = b.r.get(d[0])
            if o is None or o[1] < d[1]:
                b.r[d[0]] = d
        for b in writes:
            b.w = d
            b.r = {}

    def op(self, eng, name, kw, reads=(), writes=(), inc=True):
        fn = (name, kw)
        reads = [x.b if isinstance(x, T) else x for x in reads]
        writes = [x.b if isinstance(x, T) else x for x in writes]
        self._waits(eng, self._collect(eng, reads, writes))
        if inc:
            self.cnt[eng] += 1
            self.ops[eng].append(("ins", fn, eng, 1))
            d = (eng, self.cnt[eng])
            self.semval[eng] = self.cnt[eng]
        else:
            self.ops[eng].append(("ins", fn, None, 0))
            d = (eng, self.cnt[eng] + 1)
        self._mark(eng, d, reads, writes)

    def dma(self, eng, kw, reads=(), writes=()):
        fn = ("dma_start", kw)
        reads = [x.b if isinstance(x, T) else x for x in reads]
        writes = [x.b if isinstance(x, T) else x for x in writes]
        i = self.dcnt[eng]
        self.dcnt[eng] += 1
        ns = self.dslots[eng]
        key = f"{eng}_d{i % ns}"
        deps = self._collect(eng, reads, writes)
        prev = 16 * (i // ns)
        if prev > 0 and deps.get(key, 0) < prev:
            deps[key] = prev
        self._waits(eng, deps)
        val = 16 * (i // ns + 1)
        self.ops[eng].append(("ins", fn, key, 16))
        self.semval[key] = val
        self._mark(eng, (key, val), reads, writes)

    def barrier(self):
        for eng in self.ops:
            deps = {k: v for k, v in self.semval.items() if v > 0 and k != eng}
            self._waits(eng, deps)

    def final_wait(self, eng):
        deps = {k: v for k, v in self.semval.items() if v > 0 and k != eng}
        self._waits(eng, deps)


def build(cfg):
    NP, NS = cfg["NP"], cfg["NS"]
    BLK1, BLK2, BLK3 = cfg["BLK1"], cfg["BLK2"], cfg["BLK3"]
    LP, LS = NCORES * NP, NCORES * NS
    NB1P, NB1S = LP // BLK1, LS // BLK1
    NCH_OWN = (NP + NS) // 128

    nc = bass.Bass("TRN2", target_bir_lowering=False)
    dram = {}

    def din(name, shape):
        dram[name] = nc.dram_tensor(name, list(shape), F32, kind="ExternalInput").ap()

    NBO_P = NB1P - NP // BLK1
    NBO_S = NB1S - NS // BLK1
    NM = 4 * (NBO_P + NBO_S)
    din("xoth", (NBO_P + NBO_S, BLK1 + 4, D))
    din("xoP", (NP + 2 * PAD, D))
    din("xoS", (NS + 2 * PAD, D))
    din("w_in", (D, IN_DIM))
    din("w_branch", (6144, D))
    din("w_out", (D, D))
    din("pp", (128, NPP))
    din("pb", (1, NPB))
    din("consts", (128, 512))
    din("masks", (128, NM))
    yP = nc.dram_tensor("yP", [NP, D], F32, kind="ExternalOutput").ap()
    yS = nc.dram_tensor("yS", [NS, D], F32, kind="ExternalOutput").ap()
    acc_scr = nc.dram_tensor("acc_scr", [4, 128, SSM_DIM], F32).ap()
    yf_scr = nc.dram_tensor("yf_scr", [NCH_OWN, 128, SSM_DIM], F32).ap()
    scrbuf = {"acc": [Buf() for _ in range(4)], "yf": [Buf() for _ in range(NCH_OWN)]}

    S = Sched()
    top = ExitStack()
    sems = {}
    for k in S.semkeys:
        sems[k] = top.enter_context(nc.semaphore(k))

    uid = [0]

    def sb(es, name, shape, dt=F32):
        uid[0] += 1
        return T(es.enter_context(nc.sbuf_tensor(f"{name}_{uid[0]}", list(shape), dt)))

    pools = {}
    rr = {}

    def mkpsum(es, **spec):
        assert sum(spec.values()) <= 8
        pools.clear()
        for kind, n in spec.items():
            if kind == "tp":
                pools[kind] = [T(es.enter_context(nc.psum_tensor(f"ps{kind}{i}_{uid[0]}", [128, 1024], BF16))) for i in range(n)]
            else:
                pools[kind] = [T(es.enter_context(nc.psum_tensor(f"ps{kind}{i}_{uid[0]}", [128, 512], F32))) for i in range(n)]
            uid[0] += 1
            rr[kind] = 0

    def psum(kind):
        lst = pools[kind]
        t = lst[rr[kind] % len(lst)]
        rr[kind] += 1
        return t

    class Ring:
        def __init__(self, items):
            self.items = items
            self.i = 0

        def next(self):
            t = self.items[self.i % len(self.items)]
            self.i += 1
            return t

    cst = sb(top, "cst", [128, 512])
    cstb = sb(top, "cstb", [128, 512], BF16)
    ppt = sb(top, "ppt", [128, NPP])
    pbt = sb(top, "pbt", [128, NPB - 4096])
    mskt = sb(top, "mskt", [128, NM])
    abc = sb(top, "abc", [128, 128])
    sch = sb(top, "sch", [128, 288])
    W = Ring([sb(top, f"W{i}", [128, KD, 512], BF16) for i in range(cfg.get("NW", 3))])

    IDF = lambda: cst.t[:, 0:128]
    TRIF = lambda: cst.t[:, 128:256]
    TRIB = lambda: cst.t[:, 256:384]
    ONES = lambda: cst.t[:, 384:512]
    IDB = lambda: cstb.t[:, 0:128]

    def flush(es_phase=None):
        with nc.Block() as blk:
            def rep(eng):
                def body(h):
                    for e in S.ops[eng]:
                        if e[0] == "wait":
                            h.wait_ge(sems[e[1]], e[2])
                        else:
                            ins = getattr(h, e[1][0])(**e[1][1])
                            if e[2] is not None:
                                ins.then_inc(sems[e[2]], e[3])
                    S.ops[eng] = []
                return body
            blk.tensor(rep("pe"))
            blk.scalar(rep("act"))
            blk.vector(rep("dve"))
            blk.gpsimd(rep("pool"))
            blk.sync(rep("sp"))

    S.dma("sp", dict(out=cst.t[:], in_=dram["consts"]), writes=[cst])
    S.dma("sp", dict(out=ppt.t[:], in_=dram["pp"]), writes=[ppt])
    S.dma("sp", dict(out=pbt.t[:], in_=dram["pb"][:, 4096:NPB].partition_broadcast(128)), writes=[pbt])
    S.dma("sp", dict(out=mskt.t[:], in_=dram["masks"]), writes=[mskt])
    S.op("dve", "tensor_copy", dict(out=cstb.t[:], in_=cst.t[:]), reads=[cst], writes=[cstb])
    S.op("act", "activation", dict(out=abc.t[:], in_=pbt.t[:, PB_ALOG:PB_ALOG + 128], func=AF.Exp),
         reads=[pbt], writes=[abc])
    S.op("dve", "tensor_scalar", dict(out=abc.t[:], in0=abc.t[:], scalar1=-1.0, scalar2=None, op0=ALU.mult),
         reads=[abc], writes=[abc])
    S.op("dve", "tensor_scalar", dict(out=sch.t[:], in0=ppt.t[:, PP_SCW:PP_SCW + 288], scalar1=0.5, scalar2=None, op0=ALU.mult),
         reads=[ppt], writes=[sch])

    def load_w(mat, r0, c0, ncols, coff=0, wt=None):
        if wt is None:
            wt = W.next()
        src = mat[r0:r0 + 2048, c0:c0 + ncols].rearrange("(k p) c -> p k c", p=128)
        S.dma("pool", dict(out=wt.t[:, :, coff:coff + ncols], in_=src), writes=[wt])
        return wt

    def load_norm_T(es, src, nrows, hT, st):
        xt_r, xn_r, sm_r = st["xt"], st["xn"], st["sm"]
        for r0 in range(0, nrows, 128):
            R = min(128, nrows - r0)
            xt = xt_r.next()
            xn = xn_r.next()
            sm = sm_r.next()
            S.dma("sp", dict(out=xt.t[0:R, :], in_=src[r0:r0 + R, :]), writes=[xt])
            S.op("act", "activation", dict(
                out=xn.t[0:R, :], in_=xt.t[0:R, :], func=AF.Square, accum_out=sm.t[0:R, 0:1]),
                reads=[xt], writes=[xn, sm])
            S.op("dve", "tensor_scalar", dict(
                out=sm.t[0:R, 1:2], in0=sm.t[0:R, 0:1], scalar1=1.0 / D, scalar2=EPS, op0=ALU.mult, op1=ALU.add),
                reads=[sm], writes=[sm])
            S.op("act", "activation", dict(out=sm.t[0:R, 2:3], in_=sm.t[0:R, 1:2], func=AF.Sqrt),
                 reads=[sm], writes=[sm])
            S.op("dve", "reciprocal", dict(out=sm.t[0:R, 3:4], in_=sm.t[0:R, 2:3]),
                 reads=[sm], writes=[sm])
            S.op("dve", "scalar_tensor_tensor", dict(
                out=xn.t[0:R, :], in0=xt.t[0:R, :], scalar=sm.t[0:R, 3:4], in1=st["nw"].t[0:R, :],
                op0=ALU.mult, op1=ALU.mult), reads=[xt, sm, st["nw"]], writes=[xn])
            for half in range(2):
                tp = psum("tp")
                for j in range(8):
                    k = half * 8 + j
                    S.op("pe", "transpose", dict(
                        out=tp.t[:, j * 128:j * 128 + R], in_=xn.t[0:R, k * 128:(k + 1) * 128],
                        identity=cstb.t[0:R, 0:R]), reads=[xn, cstb], writes=[tp], inc=(j == 7))
                eng = "act" if half == 0 else "dve"
                src_v = tp.t[:].rearrange("p (j t) -> p j t", j=8)[:, :, 0:R]
                dst_v = hT.t[:, half * 8:half * 8 + 8, r0:r0 + R]
                if eng == "act":
                    S.op("act", "activation", dict(out=dst_v, in_=src_v, func=AF.Copy),
                         reads=[tp], writes=[hT])
                else:
                    S.op("dve", "tensor_copy", dict(out=dst_v, in_=src_v),
                         reads=[tp], writes=[hT])

    def proj_fm(wt, co, hT, t0, n):
        ps = psum("mm")
        for k in range(KD):
            S.op("pe", "matmul", dict(out=ps.t[:, 0:n], lhsT=wt.t[:, k, co:co + 128], rhs=hT.t[:, k, t0:t0 + n],
                start=(k == 0), stop=(k == KD - 1)), reads=[wt, hT], writes=[ps], inc=(k == KD - 1))
        return ps

    def proj_tm(wt, co, ncols, hT, t0, ps, po):
        for k in range(KD):
            S.op("pe", "matmul", dict(out=ps.t[:, po:po + ncols], lhsT=hT.t[:, k, t0:t0 + 128], rhs=wt.t[:, k, co:co + ncols],
                start=(k == 0), stop=(k == KD - 1)), reads=[wt, hT], writes=[ps], inc=(k == KD - 1))

    def subblocks(t0, n):
        k = (n + 511) // 512
        out = []
        for i in range(k):
            a = (n * i) // k
            b = (n * (i + 1)) // k
            out.append((t0 + a, b - a))
        return out

    def conv5_silu(es_bufs, wt, co, hT, tlo, Tn, pidx, outT, out_off=0):
        pre = es_bufs["pre"].next()
        acc = es_bufs["acc"].next()
        for (t0, n) in subblocks(tlo - 2, Tn + 4):
            ps = proj_fm(wt, co, hT, t0, n)
            o = t0 - (tlo - 2)
            S.op("act", "activation", dict(out=pre.t[:, o:o + n], in_=ps.t[:, 0:n], func=AF.Copy),
                 reads=[ps], writes=[pre])
        wbase = pidx * 5
        S.op("dve", "tensor_scalar", dict(out=acc.t[:, 0:Tn], in0=pre.t[:, 0:Tn], scalar1=sch.t[:, wbase:wbase + 1],
                                          scalar2=sch.t[:, 240 + pidx:240 + pidx + 1], op0=ALU.mult, op1=ALU.add),
             reads=[pre, sch], writes=[acc])
        for k in range(1, 5):
            S.op("dve", "scalar_tensor_tensor", dict(
                out=acc.t[:, 0:Tn], in0=pre.t[:, k:k + Tn], scalar=sch.t[:, wbase + k:wbase + k + 1],
                in1=acc.t[:, 0:Tn], op0=ALU.mult, op1=ALU.add), reads=[pre, sch, acc], writes=[acc])
        th = pre
        S.op("act", "activation", dict(out=th.t[:, 0:Tn], in_=acc.t[:, 0:Tn], func=AF.Tanh), reads=[acc], writes=[th])
        S.op("dve", "scalar_tensor_tensor", dict(out=outT.t[:, out_off:out_off + Tn], in0=th.t[:, 0:Tn], scalar=1.0,
                                                 in1=acc.t[:, 0:Tn], op0=ALU.add, op1=ALU.mult),
             reads=[th, acc], writes=[outT])

    def dt_block(hT, tlo, nch, db, want_T):
        wt = load_w(dram["w_in"], 0, OFF_DT, 128)
        for c4 in range(0, nch, 4):
            n4 = min(4, nch - c4)
            ps = psum("sm")
            for i in range(n4):
                proj_tm(wt, 0, 128, hT, tlo + (c4 + i) * 128, ps, i * 128)
            sl = lambda t: t.t[:, c4:c4 + n4, :]
            psv = ps.t[:, 0:n4 * 128].rearrange("p (c f) -> p c f", c=n4)
            bias_v = pbt.t[:, PB_DTB:PB_DTB + 128].unsqueeze(1).to_broadcast([128, n4, 128])
            a_v = abc.t[:].unsqueeze(1).to_broadcast([128, n4, 128])
            S.op("dve", "tensor_tensor", dict(out=sl(db["A"]), in0=psv, in1=bias_v, op=ALU.add),
                 reads=[ps, pbt], writes=[db["A"]])
            S.op("dve", "scalar_tensor_tensor", dict(out=sl(db["dt"]), in0=sl(db["A"]), scalar=-1.0, in1=sl(db["A"]),
                                                     op0=ALU.mult, op1=ALU.max),
                 reads=[db["A"]], writes=[db["dt"]])
            S.op("act", "activation", dict(out=sl(db["dt"]), in_=sl(db["dt"]), func=AF.Exp, scale=-1.0),
                 reads=[db["dt"]], writes=[db["dt"]])
            S.op("act", "activation", dict(out=sl(db["dt"]), in_=sl(db["dt"]), func=AF.Ln, bias=1.0),
                 reads=[db["dt"]], writes=[db["dt"]])
            S.op("dve", "scalar_tensor_tensor", dict(out=sl(db["dt"]), in0=sl(db["A"]), scalar=0.0, in1=sl(db["dt"]),
                                                         op0=ALU.max, op1=ALU.add),
                 reads=[db["A"], db["dt"]], writes=[db["dt"]])
            S.op("dve", "tensor_tensor", dict(out=sl(db["A"]), in0=sl(db["dt"]), in1=a_v, op=ALU.mult),
                 reads=[db["dt"], abc], writes=[db["A"]])
        for c4 in range(0, nch, 4):
            n4 = min(4, nch - c4)
            ps = psum("sm")
            ps2 = psum("mm")
            for i in range(n4):
                c = c4 + i
                S.op("pe", "matmul", dict(out=ps.t[:, i * 128:i * 128 + 64], lhsT=TRIF(), rhs=db["A"].t[:, c, 0:64],
                                                        start=True, stop=True), reads=[cst, db["A"]], writes=[ps], inc=False)
                S.op("pe", "matmul", dict(out=ps.t[:, i * 128 + 64:i * 128 + 128], lhsT=TRIB(),
                                                        rhs=db["A"].t[:, c, 64:128], start=True, stop=True),
                     reads=[cst, db["A"]], writes=[ps], inc=False)
                S.op("pe", "matmul", dict(out=ps2.t[:, i * 128:i * 128 + 128], lhsT=ONES(), rhs=db["A"].t[:, c, :],
                                                        start=True, stop=True), reads=[cst, db["A"]], writes=[ps2],
                     inc=(i == n4 - 1))
            S.op("act", "activation", dict(
                out=db["cs"].t[:, c4:c4 + n4, :], in_=ps.t[:, 0:n4 * 128].rearrange("p (c f) -> p c f", c=n4), func=AF.Copy),
                reads=[ps], writes=[db["cs"]])
            S.op("dve", "tensor_copy", dict(
                out=db["tot"].t[:, c4:c4 + n4, :], in_=ps2.t[:, 0:n4 * 128].rearrange("p (c f) -> p c f", c=n4)),
                reads=[ps2], writes=[db["tot"]])
            if want_T:
                ps3 = psum("ck")
                for i in range(n4):
                    c = c4 + i
                    S.op("pe", "matmul", dict(out=ps3.t[0:64, i * 128:(i + 1) * 128], lhsT=db["A"].t[:, c, 0:64],
                                                            rhs=TRIF(), start=True, stop=True),
                         reads=[cst, db["A"]], writes=[ps3], inc=False)
                    S.op("pe", "matmul", dict(out=ps3.t[64:128, i * 128:(i + 1) * 128],
                                                            lhsT=db["A"].t[:, c, 64:128], rhs=TRIB(), start=True, stop=True),
                         reads=[cst, db["A"]], writes=[ps3], inc=(i == n4 - 1))
                S.op("act", "activation", dict(
                    out=db["csT"].t[:, c4:c4 + n4, :], in_=ps3.t[:, 0:n4 * 128].rearrange("p (c f) -> p c f", c=n4),
                    func=AF.Copy), reads=[ps3], writes=[db["csT"]])
        if want_T:
            S.op("act", "activation", dict(out=db["ecs"].t[:], in_=db["cs"].t[:], func=AF.Exp),
                 reads=[db["cs"]], writes=[db["ecs"]])

    def tok_major(srcT, col0, nblk, c, dst, dcol0):
        tp = psum("tp")
        for i in range(nblk):
            S.op("pe", "transpose", dict(out=tp.t[:, i * 128:(i + 1) * 128],
                                                  in_=srcT[i].t[:, col0 + c * 128:col0 + (c + 1) * 128], identity=IDB()),
                 reads=[srcT[i], cstb], writes=[tp], inc=(i == nblk - 1))
        S.op("act", "activation", dict(out=dst.t[:, c, dcol0:dcol0 + nblk * 128], in_=tp.t[:, 0:nblk * 128], func=AF.Copy),
             reads=[tp], writes=[dst])

    def interleave(ga, gb, ratio=2):
        if not cfg.get("IL", 1):
            for g_ in (gb, ga):
                if g_ is not None:
                    for _ in g_:
                        pass
            return
        alive_a, alive_b = ga is not None, gb is not None
        while alive_a or alive_b:
            if alive_a:
                try:
                    next(ga)
                except StopIteration:
                    alive_a = False
            for _ in range(ratio):
                if alive_b:
                    try:
                        next(gb)
                    except StopIteration:
                        alive_b = False

    def nw_load(es, st, off):
        st["nw"] = sb(es, "nwbc", [128, D])
        S.dma("sp", dict(out=st["nw"].t[:], in_=dram["pb"][:, off:off + D].partition_broadcast(128)), writes=[st["nw"]])

    def group_set(es, nch, tag, need_C, need_xT=True):
        Tn = nch * 128
        pack = sb(es, f"gpack{tag}", [128, 7 * Tn], BF16)
        d = {"pack": pack,
             "xg_tok": T(pack.t[:, 0:4 * Tn].rearrange("p (c f) -> p c f", c=nch)),
             "Bg_tok": T(pack.t[:, 4 * Tn:5 * Tn].rearrange("p (c f) -> p c f", c=nch)),
             "BgT": T(pack.t[:, 5 * Tn:6 * Tn]),
             "CgT": T(pack.t[:, 6 * Tn:7 * Tn])}
        if need_xT:
            d["xgT"] = [sb(es, f"xgT{tag}{i}", [128, Tn], BF16) for i in range(4)]
        return d

    def conv_gen(cbufs, gs, g, hT, tlo, nch, need_C):
        Tn = nch * 128
        wx_ = load_w(dram["w_in"], 0, OFF_X + g * 512, 512)
        wbc = load_w(dram["w_in"], 0, OFF_B + g * 128, 128)
        if need_C:
            load_w(dram["w_in"], 0, OFF_C + g * 128, 128, coff=128, wt=wbc)
        for i in range(4):
            conv5_silu(cbufs, wx_, i * 128, hT, tlo, Tn, g * 4 + i, gs["xgT"][i])
            yield
        conv5_silu(cbufs, wbc, 0, hT, tlo, Tn, 32 + g, gs["BgT"])
        yield
        if need_C:
            conv5_silu(cbufs, wbc, 128, hT, tlo, Tn, 40 + g, gs["CgT"])
            yield
        for c in range(nch):
            tok_major(gs["xgT"], 0, 4, c, gs["xg_tok"], 0)
            tok_major([gs["BgT"]], 0, 1, c, gs["Bg_tok"], 0)
            if c % 2 == 1:
                yield

    NCH1 = BLK1 // 128
    TH1 = BLK1 + 4
    with ExitStack() as es:
        mkpsum(es, mm=3, tp=2, sm=1, ck=2)
        hT = sb(es, "hT1", [128, KD, TH1], BF16)
        st = {"xt": Ring([sb(es, f"xt{i}", [128, D]) for i in range(2)]),
              "xn": Ring([sb(es, f"xn{i}", [128, D], BF16) for i in range(2)]),
              "sm": Ring([sb(es, f"sm{i}", [128, 8]) for i in range(2)])}
        nw_load(es, st, PB_NW)
        cb = {"pre": Ring([sb(es, f"pre{i}", [128, TH1]) for i in range(2)]),
              "acc": Ring([sb(es, f"cacc{i}", [128, BLK1]) for i in range(2)])}
        db = {k: sb(es, "db_" + k, [128, NCH1, 128]) for k in ("dt", "A", "cs", "tot")}
        Et = sb(es, "Et", [128, NCH1, 128])
        wt_ = sb(es, "wt_", [128, NCH1, 128])
        Dt = sb(es, "Dt", [128, 128])
        aF = sb(es, "aF", [128, 64])
        cB = sb(es, "cB", [128, 64])
        pcum = sb(es, "pcum", [128, 64])
        tmpd = sb(es, "tmpd", [128, 64])
        gsets = [group_set(es, NCH1, f"a{i}", False) for i in range(2)]
        wx = Ring([sb(es, f"wx{i}", [128, 512], BF16) for i in range(4)])
        tmpL = Ring([sb(es, f"tmpL{i}", [128, 512]) for i in range(2)])
        RF = sb(es, "RF", [128, SSM_DIM])
        RB = sb(es, "RB", [128, SSM_DIM])
        v3 = lambda ap: ap.rearrange("p (h d) -> p h d", h=8)

        def tail_gen(gs, g, mfa, cBt, aFt):
            xg_tok, Bg_tok = gs["xg_tok"], gs["Bg_tok"]
            Lf = psum("ck")
            Lb = psum("ck")
            pend = None
            for c in range(NCH1 + 1):
                cur = None
                if c < NCH1:
                    cur = []
                    for di in (0, 1):
                        wxt = wx.next()
                        cur.append(wxt)
                        S.op("dve", "tensor_tensor", dict(
                            out=v3(wxt.t[:]), in0=v3(xg_tok.t[:, c, :]),
                            in1=wt_.t[:, c, di * 64 + g * 8:di * 64 + g * 8 + 8].unsqueeze(2).to_broadcast([128, 8, 64]),
                            op=ALU.mult), reads=[xg_tok, wt_], writes=[wxt])
                if pend is not None:
                    pc, pw = pend
                    for di, Lps in ((0, Lf), (1, Lb)):
                        S.op("pe", "matmul", dict(out=Lps.t[:, :], lhsT=Bg_tok.t[:, pc, :], rhs=pw[di].t[:],
                                                  start=(pc == 0), stop=(pc == NCH1 - 1)),
                             reads=[Bg_tok, pw[di]], writes=[Lps], inc=True)
                pend = (c, cur) if cur is not None else None
                yield
            gsl = slice(g * 512, (g + 1) * 512)
            S.op("dve", "tensor_tensor", dict(
                out=v3(RF.t[:, gsl]), in0=v3(RF.t[:, gsl]),
                in1=aFt.t[:, g * 8:g * 8 + 8].unsqueeze(2).to_broadcast([128, 8, 64]), op=ALU.mult),
                reads=[RF, aFt], writes=[RF])
            S.op("dve", "scalar_tensor_tensor", dict(
                out=RF.t[:, gsl], in0=Lf.t[:, :], scalar=mfa, in1=RF.t[:, gsl], op0=ALU.mult, op1=ALU.add),
                reads=[Lf, RF, mskt], writes=[RF])
            tl = tmpL.next()
            S.op("dve", "tensor_tensor", dict(
                out=v3(tl.t[:]), in0=v3(Lb.t[:, :]),
                in1=cBt.t[:, g * 8:g * 8 + 8].unsqueeze(2).to_broadcast([128, 8, 64]), op=ALU.mult),
                reads=[Lb, cBt], writes=[tl])
            S.op("dve", "tensor_tensor", dict(out=RB.t[:, gsl], in0=RB.t[:, gsl], in1=tl.t[:], op=ALU.add),
                 reads=[RB, tl], writes=[RB])

        blk_i = 0
        for si, nblk in enumerate((NBO_P, NBO_S) if cfg.get("P1", 1) else ()):
            S.op("dve", "memset", dict(ap=RF.t[:], constant=0.0), writes=[RF])
            S.op("dve", "memset", dict(ap=RB.t[:], constant=0.0), writes=[RB])
            S.op("dve", "memset", dict(ap=pcum.t[:], constant=1.0), writes=[pcum])
            for j in range(nblk):
                mc = 4 * blk_i
                mf = mskt.t[:, mc:mc + 1]
                omf = mskt.t[:, mc + 1:mc + 2]
                mb = mskt.t[:, mc + 2:mc + 3]
                omb = mskt.t[:, mc + 3:mc + 4]
                load_norm_T(es, dram["xoth"][blk_i], TH1, hT, st)
                blk_i += 1
                dt_block(hT, 2, NCH1, db, want_T=False)
                for c in range(NCH1):
                    ps = psum("sm")
                    for c2 in range(c, NCH1):
                        S.op("pe", "matmul", dict(out=ps.t[:, 0:64], lhsT=ONES(), rhs=db["A"].t[:, c2, 0:64],
                                                  start=(c2 == c), stop=(c2 == NCH1 - 1)),
                             reads=[cst, db["A"]], writes=[ps], inc=False)
                    for c2 in range(0, c + 1):
                        S.op("pe", "matmul", dict(out=ps.t[:, 64:128], lhsT=ONES(), rhs=db["A"].t[:, c2, 64:128],
                                                  start=(c2 == 0), stop=(c2 == c)),
                             reads=[cst, db["A"]], writes=[ps], inc=(c2 == c))
                    S.op("act", "activation", dict(out=Et.t[:, c, :], in_=ps.t[:, 0:128], func=AF.Copy),
                         reads=[ps], writes=[Et])
                S.op("dve", "tensor_tensor", dict(out=wt_.t[:], in0=Et.t[:], in1=db["cs"].t[:], op=ALU.subtract),
                     reads=[Et, db["cs"]], writes=[wt_])
                S.op("act", "activation", dict(out=wt_.t[:], in_=wt_.t[:], func=AF.Exp), reads=[wt_], writes=[wt_])
                S.op("dve", "tensor_tensor", dict(out=wt_.t[:], in0=wt_.t[:], in1=db["dt"].t[:], op=ALU.mult),
                     reads=[wt_, db["dt"]], writes=[wt_])
                S.op("act", "activation", dict(out=Dt.t[:, 0:64], in_=Et.t[:, 0, 0:64], func=AF.Exp),
                     reads=[Et], writes=[Dt])
                S.op("act", "activation", dict(out=Dt.t[:, 64:128], in_=Et.t[:, NCH1 - 1, 64:128], func=AF.Exp),
                     reads=[Et], writes=[Dt])
                S.op("dve", "tensor_scalar", dict(out=aF.t[:], in0=Dt.t[:, 0:64], scalar1=mf, scalar2=omf,
                                                  op0=ALU.mult, op1=ALU.add), reads=[Dt, mskt], writes=[aF])
                S.op("dve", "tensor_scalar", dict(out=cB.t[:], in0=pcum.t[:], scalar1=mb, scalar2=None, op0=ALU.mult),
                     reads=[pcum, mskt], writes=[cB])
                S.op("dve", "tensor_scalar", dict(out=tmpd.t[:], in0=Dt.t[:, 64:128], scalar1=mb, scalar2=omb,
                                                  op0=ALU.mult, op1=ALU.add), reads=[Dt, mskt], writes=[tmpd])
                S.op("dve", "tensor_tensor", dict(out=pcum.t[:], in0=pcum.t[:], in1=tmpd.t[:], op=ALU.mult),
                     reads=[pcum, tmpd], writes=[pcum])
                prev = None
                for g in range(NG):
                    gs = gsets[g % 2]
                    interleave(conv_gen(cb, gs, g, hT, 2, NCH1, False), prev, ratio=1)
                    prev = tail_gen(gs, g, mf, cB, aF)
                interleave(None, prev)
            S.dma("sp", dict(out=acc_scr[2 * si], in_=RF.t[:]), reads=[RF], writes=[scrbuf["acc"][2 * si]])
            S.dma("sp", dict(out=acc_scr[2 * si + 1], in_=RB.t[:]), reads=[RB], writes=[scrbuf["acc"][2 * si + 1]])
        S.barrier()
        flush()

    def chunk_gen(bufs, gs, g, di, nch, db, carry, chunk_cb):
        xg_tok, Bg_tok, BgT, CgT = gs["xg_tok"], gs["Bg_tok"], gs["BgT"], gs["CgT"]
        Sbf = bufs["Sbf"]
        S.op("act", "activation", dict(out=Sbf.t[:], in_=carry.t[:], func=AF.Copy), reads=[carry], writes=[Sbf])
        order = range(nch) if di == 0 else range(nch - 1, -1, -1)
        mask = TRIF if di == 0 else TRIB
        hb = di * 64 + g * 8
        v3 = lambda ap: ap.rearrange("p (h d) -> p h d", h=8)
        for c in order:
            cs_ = slice(c * 128, (c + 1) * 128)
            ps = psum("sm")
            S.op("pe", "matmul", dict(out=ps.t[:, 0:128], lhsT=BgT.t[:, cs_], rhs=CgT.t[:, cs_], start=True, stop=True),
                 reads=[BgT, CgT], writes=[ps])
            psss = []
            for hq in range(2):
                pss = psum("sel")
                psss.append(pss)
                for hh in range(4):
                    hi = hq * 4 + hh
                    S.op("pe", "matmul", dict(out=pss.t[:, hh * 128:(hh + 1) * 128],
                                              lhsT=cst.t[:, hb + hi:hb + hi + 1].to_broadcast([128, 128]),
                                              rhs=db["csT"].t[:, c, :], start=True, stop=True),
                         reads=[cst, db["csT"]], writes=[pss], inc=(hh == 3))
            yield
            CBm = bufs["CBm"].next()
            S.op("dve", "tensor_tensor", dict(out=CBm.t[:], in0=ps.t[:, 0:128], in1=mask(), op=ALU.mult),
                 reads=[ps, cst], writes=[CBm])
            ncs = bufs["ncs"].next()
            S.op("dve", "tensor_scalar", dict(out=ncs.t[:, 0:8], in0=db["cs"].t[:, c, hb:hb + 8], scalar1=-1.0, scalar2=None,
                                              op0=ALU.mult), reads=[db["cs"]], writes=[ncs])
            wv = bufs["wv"].next()
            S.op("dve", "tensor_tensor", dict(out=wv.t[:, 0:8], in0=db["tot"].t[:, c, hb:hb + 8],
                                              in1=db["cs"].t[:, c, hb:hb + 8], op=ALU.subtract),
                 reads=[db["tot"], db["cs"]], writes=[wv])
            S.op("act", "activation", dict(out=wv.t[:, 0:8], in_=wv.t[:, 0:8], func=AF.Exp), reads=[wv], writes=[wv])
            S.op("dve", "tensor_tensor", dict(out=wv.t[:, 0:8], in0=wv.t[:, 0:8], in1=db["dt"].t[:, c, hb:hb + 8], op=ALU.mult),
                 reads=[wv, db["dt"]], writes=[wv])
            S.op("act", "activation", dict(out=wv.t[:, 8:16], in_=db["tot"].t[:, c, hb:hb + 8], func=AF.Exp),
                 reads=[db["tot"]], writes=[wv])
            segs_ = []
            for hi in range(8):
                sg = bufs["seg"].next()
                segs_.append(sg)
                S.op("act", "activation", dict(out=sg.t[:], in_=psss[hi // 4].t[:, (hi % 4) * 128:(hi % 4 + 1) * 128],
                                               func=AF.Exp, bias=ncs.t[:, hi:hi + 1]),
                     reads=[psss[hi // 4], ncs], writes=[sg])
            xdt = bufs["wx"].next()
            S.op("dve", "tensor_tensor", dict(
                out=v3(xdt.t[:]), in0=v3(xg_tok.t[:, c, :]),
                in1=db["dt"].t[:, c, hb:hb + 8].unsqueeze(2).to_broadcast([128, 8, 64]), op=ALU.mult),
                reads=[xg_tok, db["dt"]], writes=[xdt])
            wxt = bufs["wx"].next()
            S.op("dve", "tensor_tensor", dict(
                out=v3(wxt.t[:]), in0=v3(xg_tok.t[:, c, :]), in1=wv.t[:, 0:8].unsqueeze(2).to_broadcast([128, 8, 64]),
                op=ALU.mult), reads=[xg_tok, wv], writes=[wxt])
            Mts = []
            for hi in range(8):
                Mt = bufs["M"].next()
                Mts.append(Mt)
                S.op("dve", "scalar_tensor_tensor", dict(out=Mt.t[:], in0=segs_[hi].t[:], scalar=1.0, in1=CBm.t[:],
                                                         op0=ALU.min, op1=ALU.mult),
                     reads=[segs_[hi], CBm], writes=[Mt])
            yield
            Yps = psum("ck")
            for hi in range(8):
                S.op("pe", "matmul", dict(out=Yps.t[:, hi * 64:(hi + 1) * 64], lhsT=Mts[hi].t[:],
                                          rhs=xdt.t[:, hi * 64:(hi + 1) * 64], start=True, stop=True),
                     reads=[Mts[hi], xdt], writes=[Yps], inc=(hi == 7))
            CSps = psum("ck")
            S.op("pe", "matmul", dict(out=CSps.t[:, :], lhsT=CgT.t[:, cs_], rhs=Sbf.t[:], start=True, stop=True),
                 reads=[CgT, Sbf], writes=[CSps])
            Lps = psum("sm")
            S.op("pe", "matmul", dict(out=Lps.t[:, :], lhsT=Bg_tok.t[:, c, :], rhs=wxt.t[:], start=True, stop=True),
                 reads=[Bg_tok, wxt], writes=[Lps])
            yield
            S.op("dve", "tensor_tensor", dict(
                out=v3(carry.t[:]), in0=v3(carry.t[:]), in1=wv.t[:, 8:16].unsqueeze(2).to_broadcast([128, 8, 64]),
                op=ALU.mult), reads=[carry, wv], writes=[carry])
            S.op("dve", "tensor_tensor", dict(out=carry.t[:], in0=carry.t[:], in1=Lps.t[:, :], op=ALU.add),
                 reads=[carry, Lps], writes=[carry])
            S.op("act", "activation", dict(out=Sbf.t[:], in_=carry.t[:], func=AF.Copy), reads=[carry], writes=[Sbf])
            yield from chunk_cb(c, Yps, CSps, hb)

    def ssd_bufs(es, nch, tag, nbuf=2):
        Tn = nch * 128
        return {
            "cb": {"pre": Ring([sb(es, f"pre{tag}{i}", [128, Tn + 4]) for i in range(nbuf)]),
                   "acc": Ring([sb(es, f"cacc{tag}{i}", [128, Tn]) for i in range(nbuf)])},
            "Sbf": sb(es, f"Sbf{tag}", [128, 512], BF16),
            "CBm": Ring([sb(es, f"CBm{tag}{i}", [128, 128]) for i in range(2)]),
            "ncs": Ring([sb(es, f"ncs{tag}{i}", [128, 8]) for i in range(2)]),
            "seg": Ring([sb(es, f"seg{tag}{i}", [128, 128]) for i in range(8)]),
            "M": Ring([sb(es, f"M{tag}{i}", [128, 128], BF16) for i in range(8)]),
            "wv": Ring([sb(es, f"wv{tag}{i}", [128, 16]) for i in range(2)]),
            "wx": Ring([sb(es, f"wx{tag}{i}", [128, 512], BF16) for i in range(4)]),
        }

    gsp = nc.dram_tensor("gsp", [(NP + NS) // BLK2 * NG, 128, 7 * BLK2], BF16).ap()
    gsp_buf = [Buf() for _ in range((NP + NS) // BLK2 * NG)]
    assert BLK2 == BLK3

    def ssd_block(bufs, gsets, di, hT, tlo, nch, db, carry_g, mk_cb, blk_idx, reload):
        prev = None
        for g in range(NG):
            gs = gsets[g % 2]
            parts = [gs["xg_tok"], gs["Bg_tok"], gs["BgT"], gs["CgT"]]
            idx = blk_idx * NG + g
            if reload:
                S.dma("sp", dict(out=gs["pack"].t[:], in_=gsp[idx]), reads=[gsp_buf[idx]], writes=[gs["pack"]] + parts)
                interleave(None, prev)
            else:
                interleave(conv_gen(bufs["cb"], gs, g, hT, tlo, nch, True), prev, ratio=2)
                S.dma("sp", dict(out=gsp[idx], in_=gs["pack"].t[:]), reads=[gs["pack"]] + parts, writes=[gsp_buf[idx]])
            prev = chunk_gen(bufs, gs, g, di, nch, db, carry_g[g], mk_cb(g, gs))
        interleave(None, prev)

    segs = (("xoP", NP, yP, 0, 0), ("xoS", NS, yS, NP // 128, 1))

    NCH2 = BLK2 // 128
    TH2 = BLK2 + 4
    with ExitStack() as es:
        mkpsum(es, mm=2, tp=1, sm=1, ck=2, sel=2)
        hT = sb(es, "hT2", [128, KD, TH2], BF16)
        st = {"xt": Ring([sb(es, f"xt2{i}", [128, D]) for i in range(2)]),
              "xn": Ring([sb(es, f"xn2{i}", [128, D], BF16) for i in range(2)]),
              "sm": Ring([sb(es, f"sm2{i}", [128, 8]) for i in range(2)])}
        nw_load(es, st, PB_NW)
        db = {k: sb(es, "db2_" + k, [128, NCH2, 128]) for k in ("dt", "A", "cs", "tot", "csT", "ecs")}
        bufs = ssd_bufs(es, NCH2, "p2")
        gsets = [group_set(es, NCH2, f"b{i}", True) for i in range(2)]
        carries = sb(es, "carries2", [128, SSM_DIM])
        carry_g = [T(carries.t[:, g * 512:(g + 1) * 512]) for g in range(NG)]
        yst = Ring([sb(es, f"yst{i}", [128, 512]) for i in range(3)])
        for (xname, ntok, yout, chbase, sidx) in (segs if cfg.get("P2", 1) else ()):
            S.dma("sp", dict(out=carries.t[:], in_=acc_scr[2 * sidx]),
                  reads=[scrbuf["acc"][2 * sidx]], writes=[carries] + carry_g)
            for b0 in range(0, ntok, BLK2):
                row0 = PAD + b0 - 2
                load_norm_T(es, dram[xname][row0:row0 + TH2, :], TH2, hT, st)
                dt_block(hT, 2, NCH2, db, want_T=True)

                def mk_cb(g, gs, b0=b0, chbase=chbase):
                    def cbk(c, Yps, CSps, hb):
                        y = yst.next()
                        S.op("act", "activation", dict(out=y.t[:], in_=Yps.t[:, :], func=AF.Copy), reads=[Yps], writes=[y])
                        for hi in range(8):
                            S.op("dve", "scalar_tensor_tensor", dict(
                                out=y.t[:, hi * 64:(hi + 1) * 64], in0=CSps.t[:, hi * 64:(hi + 1) * 64],
                                scalar=db["ecs"].t[:, c, hb + hi:hb + hi + 1], in1=y.t[:, hi * 64:(hi + 1) * 64],
                                op0=ALU.mult, op1=ALU.add), reads=[CSps, db["ecs"], y], writes=[y])
                        ch = chbase + b0 // 128 + c
                        S.dma("sp", dict(out=yf_scr[ch][:, g * 512:(g + 1) * 512], in_=y.t[:]),
                              reads=[y], writes=[scrbuf["yf"][ch]])
                        yield
                    return cbk
                ssd_block(bufs, gsets, 0, hT, 2, NCH2, db, carry_g, mk_cb, chbase * 128 // BLK2 + b0 // BLK2, False)
        S.barrier()
        flush()

    NCH3 = BLK3 // 128
    TH3 = BLK3 + 2 * PAD
    onesb = lambda: cstb.t[:, 384:512]
    with ExitStack() as es:
        hT = sb(es, "hT3", [128, KD, TH3], BF16)
        carries = sb(es, "carries3", [128, SSM_DIM])
        carry_g = [T(carries.t[:, g * 512:(g + 1) * 512]) for g in range(NG)]
        ysT = [sb(es, f"ysT{i}", [128, BLK3], BF16) for i in range(32)]
        ycT = [sb(es, f"ycT{i}", [128, BLK3], BF16) for i in range(16)]

        for (xname, ntok, yout, chbase, sidx) in (segs if cfg.get("P3", 1) else ()):
            S.dma("sp", dict(out=carries.t[:], in_=acc_scr[2 * sidx + 1]),
                  reads=[scrbuf["acc"][2 * sidx + 1]], writes=[carries] + carry_g)
            for b0 in range(ntok - BLK3, -1, -BLK3):
                with ExitStack() as e2:
                    mkpsum(e2, tp=2)
                    st = {"xt": Ring([sb(e2, f"xt3{i}", [128, D]) for i in range(2)]),
                          "xn": Ring([sb(e2, f"xn3{i}", [128, D], BF16) for i in range(2)]),
                          "sm": Ring([sb(e2, f"sm3{i}", [128, 8]) for i in range(2)])}
                    nw_load(e2, st, PB_NW)
                    load_norm_T(e2, dram[xname][b0:b0 + TH3, :], TH3, hT, st)
                    S.barrier()
                    flush()
                with ExitStack() as e2:
                    db = {k: sb(e2, "db3_" + k, [128, NCH3, 128]) for k in ("dt", "A", "cs", "tot", "csT", "ecs")}
                    mkpsum(e2, mm=2, tp=1, sm=1, ck=2, sel=2)
                    bufs = ssd_bufs(e2, NCH3, "p3", nbuf=2)
                    gsets = [group_set(e2, NCH3, f"c{i}", True, need_xT=False) for i in range(2)]
                    yb = Ring([sb(e2, f"yb{i}", [128, 512]) for i in range(1)])
                    yfl = Ring([sb(e2, f"yfl{i}", [128, 512]) for i in range(1)])
                    zs = Ring([sb(e2, f"zs{i}", [128, 512]) for i in range(1)])
                    ysn = Ring([sb(e2, f"ysn{i}", [128, 512], BF16) for i in range(2)])
                    sq = sb(e2, "sqj", [128, 512])
                    sm3 = Ring([sb(e2, f"sm3b{i}", [128, 8]) for i in range(2)])
                    dt_block(hT, PAD, NCH3, db, want_T=True)
                    def mk_cb(g, gs, b0=b0, chbase=chbase):
                      wz = load_w(dram["w_in"], 0, OFF_Z + g * 512, 512)

                      def cbk(c, Yps, CSps, hb):
                        if True:
                            y = yb.next()
                            yf = yfl.next()
                            ch = chbase + b0 // 128 + c
                            S.dma("sp", dict(out=yf.t[:], in_=yf_scr[ch][:, g * 512:(g + 1) * 512]),
                                  reads=[scrbuf["yf"][ch]], writes=[yf])
                            S.op("act", "activation", dict(out=y.t[:], in_=Yps.t[:, :], func=AF.Copy), reads=[Yps], writes=[y])
                            for hi in range(8):
                                S.op("dve", "scalar_tensor_tensor", dict(
                                    out=y.t[:, hi * 64:(hi + 1) * 64], in0=CSps.t[:, hi * 64:(hi + 1) * 64],
                                    scalar=db["ecs"].t[:, c, hb + hi:hb + hi + 1], in1=y.t[:, hi * 64:(hi + 1) * 64],
                                    op0=ALU.mult, op1=ALU.add), reads=[CSps, db["ecs"], y], writes=[y])
                            S.op("dve", "tensor_tensor", dict(out=y.t[:], in0=y.t[:], in1=yf.t[:], op=ALU.add),
                                 reads=[y, yf], writes=[y])
                            v3 = lambda ap: ap.rearrange("p (h d) -> p h d", h=8)
                            S.op("dve", "tensor_tensor", dict(
                                out=v3(yf.t[:]), in0=v3(gs["xg_tok"].t[:, c, :]),
                                in1=pbt.t[:, PB_DSK + g * 8:PB_DSK + g * 8 + 8].unsqueeze(2).to_broadcast([128, 8, 64]),
                                op=ALU.mult), reads=[gs["xg_tok"], pbt], writes=[yf])
                            S.op("dve", "tensor_tensor", dict(out=y.t[:], in0=y.t[:], in1=yf.t[:], op=ALU.add),
                                 reads=[y, yf], writes=[y])
                            yield
                            zp = psum("mm")
                            proj_tm(wz, 0, 512, hT, PAD + c * 128, zp, 0)
                            yield
                            z = zs.next()
                            S.op("act", "activation", dict(out=z.t[:], in_=zp.t[:, :], func=AF.Tanh, scale=0.5), reads=[zp], writes=[z])
                            S.op("dve", "scalar_tensor_tensor", dict(out=z.t[:], in0=z.t[:], scalar=1.0, in1=zp.t[:, :],
                                                                     op0=ALU.add, op1=ALU.mult), reads=[z, zp], writes=[z])
                            S.op("dve", "scalar_tensor_tensor", dict(out=y.t[:], in0=y.t[:], scalar=0.5, in1=z.t[:],
                                                                     op0=ALU.mult, op1=ALU.mult), reads=[y, z], writes=[y])
                            sm = sm3.next()
                            S.op("act", "activation", dict(out=sq.t[:], in_=y.t[:], func=AF.Square, accum_out=sm.t[:, 0:1]),
                                 reads=[y], writes=[sq, sm])
                            S.op("dve", "tensor_scalar", dict(out=sm.t[:, 1:2], in0=sm.t[:, 0:1], scalar1=1.0 / 512, scalar2=EPS,
                                                              op0=ALU.mult, op1=ALU.add), reads=[sm], writes=[sm])
                            S.op("act", "activation", dict(out=sm.t[:, 2:3], in_=sm.t[:, 1:2], func=AF.Sqrt), reads=[sm], writes=[sm])
                            S.op("dve", "reciprocal", dict(out=sm.t[:, 3:4], in_=sm.t[:, 2:3]), reads=[sm], writes=[sm])
                            yn = ysn.next()
                            S.op("dve", "tensor_scalar", dict(out=yn.t[:], in0=y.t[:], scalar1=sm.t[:, 3:4], scalar2=None,
                                                              op0=ALU.mult), reads=[y, sm], writes=[yn])
                            tp = psum("tp")
                            for i in range(4):
                                S.op("pe", "transpose", dict(out=tp.t[:, i * 128:(i + 1) * 128],
                                                             in_=yn.t[:, i * 128:(i + 1) * 128], identity=IDB()),
                                     reads=[yn, cstb], writes=[tp], inc=(i == 3))
                            for i in range(4):
                                cbi = g * 4 + i
                                S.op("dve", "tensor_scalar", dict(
                                    out=ysT[cbi].t[:, c * 128:(c + 1) * 128], in0=tp.t[:, i * 128:(i + 1) * 128],
                                    scalar1=ppt.t[:, PP_SNW + cbi:PP_SNW + cbi + 1], scalar2=None, op0=ALU.mult),
                                    reads=[tp, ppt], writes=[ysT[cbi]])
                            yield
                      return cbk
                    ssd_block(bufs, gsets, 1, hT, PAD, NCH3, db, carry_g, mk_cb, chbase * 128 // BLK3 + b0 // BLK3, True)
                    S.barrier()
                    flush()

                with ExitStack() as e2:
                    mkpsum(e2, mm=4, sm=2)
                    cpre = Ring([sb(e2, f"cpre{i}", [128, TH3]) for i in range(2)])
                    csig = Ring([sb(e2, f"csig{i}", [128, TH3]) for i in range(2)])
                    cacc = Ring([sb(e2, f"cacc3{i}", [128, BLK3]) for i in range(3)])
                    cacc2 = Ring([sb(e2, f"cacc4{i}", [128, BLK3]) for i in range(2)])
                    stat = sb(e2, "stat", [128, 2, BLK3])
                    usq = Ring([sb(e2, f"usq{i}", [128, BLK3], BF16) for i in range(2)])
                    gt = Ring([sb(e2, f"gt{i}", [128, BLK3]) for i in range(2)])
                    stp = [psum("sm"), psum("sm")]
                    for cbi in range(16):
                        if cbi % 4 == 0:
                            wcv = load_w(dram["w_in"], 0, OFF_CV + cbi * 128, 512)
                            wcg = load_w(dram["w_in"], 0, OFF_CG + cbi * 128, 512)
                        co = (cbi % 4) * 128
                        pre = cpre.next()
                        sg_ = csig.next()
                        for (t0, n) in subblocks(0, TH3):
                            ps = proj_fm(wcg, co, hT, t0, n)
                            S.op("act", "activation", dict(out=sg_.t[:, t0:t0 + n], in_=ps.t[:, 0:n], func=AF.Sigmoid),
                                 reads=[ps], writes=[sg_])
                            ps2 = proj_fm(wcv, co, hT, t0, n)
                            S.op("dve", "tensor_tensor", dict(out=pre.t[:, t0:t0 + n], in0=ps2.t[:, 0:n],
                                                              in1=sg_.t[:, t0:t0 + n], op=ALU.mult),
                                 reads=[ps2, sg_], writes=[pre])
                        acc = cacc.next()
                        acc2 = cacc2.next()
                        wb0 = PP_DWW + cbi * 31
                        S.op("dve", "tensor_scalar", dict(out=acc.t[:], in0=pre.t[:, 1:1 + BLK3], scalar1=ppt.t[:, wb0:wb0 + 1],
                                                          scalar2=ppt.t[:, PP_DWB + cbi:PP_DWB + cbi + 1], op0=ALU.mult,
                                                          op1=ALU.add), reads=[pre, ppt], writes=[acc])
                        S.op("dve", "tensor_scalar", dict(out=acc2.t[:], in0=pre.t[:, 2:2 + BLK3], scalar1=ppt.t[:, wb0 + 1:wb0 + 2],
                                                          scalar2=None, op0=ALU.mult), reads=[pre, ppt], writes=[acc2])
                        for k in range(2, CONV_K):
                            tg = acc if k % 2 == 0 else acc2
                            S.op("dve", "scalar_tensor_tensor", dict(
                                out=tg.t[:], in0=pre.t[:, 1 + k:1 + k + BLK3], scalar=ppt.t[:, wb0 + k:wb0 + k + 1],
                                in1=tg.t[:], op0=ALU.mult, op1=ALU.add), reads=[pre, ppt, tg], writes=[tg])
                        S.op("dve", "tensor_tensor", dict(out=acc.t[:], in0=acc.t[:], in1=acc2.t[:], op=ALU.add),
                             reads=[acc, acc2], writes=[acc])
                        S.op("act", "activation", dict(out=ycT[cbi].t[:], in_=acc.t[:], func=AF.Copy),
                             reads=[acc], writes=[ycT[cbi]])
                        us = usq.next()
                        S.op("act", "activation", dict(out=us.t[:], in_=ycT[cbi].t[:], func=AF.Square),
                             reads=[ycT[cbi]], writes=[us])
                        S.op("pe", "matmul", dict(out=stp[0].t[:, 0:BLK3], lhsT=onesb(), rhs=ycT[cbi].t[:],
                                                  start=(cbi == 0), stop=(cbi == 15)),
                             reads=[cstb, ycT[cbi]], writes=[stp[0]])
                        S.op("pe", "matmul", dict(out=stp[1].t[:, 0:BLK3], lhsT=onesb(), rhs=us.t[:],
                                                  start=(cbi == 0), stop=(cbi == 15)),
                             reads=[cstb, us], writes=[stp[1]])
                    S.op("dve", "tensor_scalar", dict(out=stat.t[:, 0, :], in0=stp[0].t[:, 0:BLK3], scalar1=1.0 / D, scalar2=None,
                                                      op0=ALU.mult), reads=[stp[0]], writes=[stat])
                    S.op("dve", "tensor_scalar", dict(out=stat.t[:, 1, :], in0=stp[1].t[:, 0:BLK3], scalar1=1.0 / D, scalar2=EPS,
                                                      op0=ALU.mult, op1=ALU.add), reads=[stp[1]], writes=[stat])
                    tmpm = cacc.next()
                    S.op("dve", "tensor_tensor", dict(out=tmpm.t[:], in0=stat.t[:, 0, :], in1=stat.t[:, 0, :], op=ALU.mult),
                         reads=[stat], writes=[tmpm])
                    S.op("dve", "tensor_tensor", dict(out=stat.t[:, 1, :], in0=stat.t[:, 1, :], in1=tmpm.t[:], op=ALU.subtract),
                         reads=[stat, tmpm], writes=[stat])
                    S.op("act", "activation", dict(out=stat.t[:, 1, :], in_=stat.t[:, 1, :], func=AF.Sqrt), reads=[stat], writes=[stat])
                    S.op("dve", "reciprocal", dict(out=stat.t[:, 1, :], in_=stat.t[:, 1, :]), reads=[stat], writes=[stat])
                    for cbi in range(16):
                        if cbi % 4 == 0:
                            wcs = load_w(dram["w_in"], 0, OFF_CS + cbi * 128, 512)
                        co = (cbi % 4) * 128
                        a_ = cacc.next()
                        S.op("dve", "tensor_tensor", dict(out=a_.t[:], in0=ycT[cbi].t[:], in1=stat.t[:, 0, :], op=ALU.subtract),
                             reads=[ycT[cbi], stat], writes=[a_])
                        S.op("dve", "tensor_tensor", dict(out=a_.t[:], in0=a_.t[:], in1=stat.t[:, 1, :], op=ALU.mult),
                             reads=[a_, stat], writes=[a_])
                        S.op("act", "activation", dict(out=a_.t[:], in_=a_.t[:], func=AF.Silu,
                                                       scale=ppt.t[:, PP_LNG + cbi:PP_LNG + cbi + 1],
                                                       bias=ppt.t[:, PP_LNB + cbi:PP_LNB + cbi + 1]),
                             reads=[a_, ppt], writes=[a_])
                        for (t0, n) in subblocks(0, BLK3):
                            ps = proj_fm(wcs, co, hT, PAD + t0, n)
                            g_ = gt.next()
                            S.op("act", "activation", dict(out=g_.t[:, 0:n], in_=ps.t[:, 0:n], func=AF.Silu),
                                 reads=[ps], writes=[g_])
                            S.op("dve", "tensor_tensor", dict(out=ycT[cbi].t[:, t0:t0 + n], in0=a_.t[:, t0:t0 + n],
                                                              in1=g_.t[:, 0:n], op=ALU.mult),
                                 reads=[a_, g_], writes=[ycT[cbi]])
                    S.barrier()
                    flush()

                with ExitStack() as e2:
                    mkpsum(e2, mm=6)
                    fnw = sb(e2, "fnw", [128, D])
                    S.dma("sp", dict(out=fnw.t[:], in_=dram["pb"][:, PB_FNW:PB_FNW + D].partition_broadcast(128)), writes=[fnw])
                    mT = [sb(e2, f"mT{i}", [128, BLK3], BF16) for i in range(16)]
                    o1s = [sb(e2, f"o1s{i}", [128, BLK3]) for i in range(4)]
                    gt = Ring([sb(e2, f"gtc{i}", [128, BLK3]) for i in range(2)])
                    ores = Ring([sb(e2, f"ores{i}", [128, D]) for i in range(1)])
                    xres = Ring([sb(e2, f"xres{i}", [128, D]) for i in range(2)])
                    sm3 = Ring([sb(e2, f"sm3c{i}", [128, 8]) for i in range(2)])
                    for dq in range(4):
                        wgc = load_w(dram["w_in"], 0, OFF_GATE + dq * 512, 512)
                        wb0_ = load_w(dram["w_branch"], 0, dq * 512, 512)
                        for dj in range(4):
                            dbi = dq * 4 + dj
                            co = dj * 128
                            for (t0, n) in subblocks(0, BLK3):
                                poc = psum("mm")
                                for k in range(16):
                                    S.op("pe", "matmul", dict(out=poc.t[:, 0:n], lhsT=wb0_.t[:, k, co:co + 128],
                                                              rhs=ycT[k].t[:, t0:t0 + n], start=(k == 0), stop=(k == 15)),
                                         reads=[wb0_, ycT[k]], writes=[poc], inc=(k == 15))
                                pgc = proj_fm(wgc, co, hT, PAD + t0, n)
                                g1 = gt.next()
                                S.op("act", "activation", dict(out=g1.t[:, 0:n], in_=pgc.t[:, 0:n], func=AF.Sigmoid,
                                                               bias=ppt.t[:, PP_BG + dbi:PP_BG + dbi + 1]),
                                     reads=[pgc, ppt], writes=[g1])
                                S.op("dve", "tensor_tensor", dict(out=o1s[dj].t[:, t0:t0 + n], in0=poc.t[:, 0:n],
                                                                  in1=g1.t[:, 0:n], op=ALU.mult),
                                     reads=[poc, g1], writes=[o1s[dj]])
                        wgs = load_w(dram["w_in"], 0, OFF_GATE + 2048 + dq * 512, 512)
                        wb1_ = load_w(dram["w_branch"], 2048, dq * 512, 512)
                        wb2_ = load_w(dram["w_branch"], 4096, dq * 512, 512)
                        wbs = [wb1_, wb2_]
                        for dj in range(4):
                            dbi = dq * 4 + dj
                            co = dj * 128
                            for (t0, n) in subblocks(0, BLK3):
                                pos = psum("mm")
                                for k in range(32):
                                    S.op("pe", "matmul", dict(out=pos.t[:, 0:n], lhsT=wbs[k // 16].t[:, k % 16, co:co + 128],
                                                              rhs=ysT[k].t[:, t0:t0 + n], start=(k == 0), stop=(k == 31)),
                                         reads=[wbs[k // 16], ysT[k]], writes=[pos], inc=(k == 31))
                                pgs = proj_fm(wgs, co, hT, PAD + t0, n)
                                g2 = gt.next()
                                S.op("act", "activation", dict(out=g2.t[:, 0:n], in_=pgs.t[:, 0:n], func=AF.Sigmoid,
                                                               bias=ppt.t[:, PP_BG + 16 + dbi:PP_BG + 16 + dbi + 1]),
                                     reads=[pgs, ppt], writes=[g2])
                                S.op("dve", "tensor_tensor", dict(out=g2.t[:, 0:n], in0=pos.t[:, 0:n], in1=g2.t[:, 0:n],
                                                                  op=ALU.mult), reads=[pos, g2], writes=[g2])
                                S.op("dve", "tensor_tensor", dict(out=mT[dbi].t[:, t0:t0 + n], in0=o1s[dj].t[:, t0:t0 + n],
                                                                  in1=g2.t[:, 0:n], op=ALU.add),
                                     reads=[o1s[dj], g2], writes=[mT[dbi]])
                    wos = None
                    for c in range(NCH3):
                        xr = xres.next()
                        r0 = PAD + b0 + c * 128
                        S.dma("sp", dict(out=xr.t[:], in_=dram[xname][r0:r0 + 128, :]), writes=[xr])
                        orr = ores.next()
                        for eq in range(4):
                            wo = load_w(dram["w_out"], 0, eq * 512, 512)
                            po = psum("mm")
                            for k in range(16):
                                S.op("pe", "matmul", dict(out=po.t[:, :], lhsT=mT[k].t[:, c * 128:(c + 1) * 128],
                                                          rhs=wo.t[:, k, :], start=(k == 0), stop=(k == 15)),
                                     reads=[wo, mT[k]], writes=[po], inc=(k == 15))
                            S.op("dve", "tensor_tensor", dict(out=orr.t[:, eq * 512:(eq + 1) * 512], in0=po.t[:, :],
                                                              in1=xr.t[:, eq * 512:(eq + 1) * 512], op=ALU.add),
                                 reads=[po, xr], writes=[orr])
                        sm = sm3.next()
                        S.op("act", "activation", dict(out=xr.t[:], in_=orr.t[:], func=AF.Square, accum_out=sm.t[:, 0:1]),
                             reads=[orr], writes=[xr, sm])
                        S.op("dve", "tensor_scalar", dict(out=sm.t[:, 1:2], in0=sm.t[:, 0:1], scalar1=1.0 / D, scalar2=EPS,
                                                          op0=ALU.mult, op1=ALU.add), reads=[sm], writes=[sm])
                        S.op("act", "activation", dict(out=sm.t[:, 2:3], in_=sm.t[:, 1:2], func=AF.Sqrt), reads=[sm], writes=[sm])
                        S.op("dve", "reciprocal", dict(out=sm.t[:, 3:4], in_=sm.t[:, 2:3]), reads=[sm], writes=[sm])
                        S.op("dve", "scalar_tensor_tensor", dict(out=orr.t[:], in0=orr.t[:], scalar=sm.t[:, 3:4],
                                                                 in1=fnw.t[:], op0=ALU.mult, op1=ALU.mult),
                             reads=[orr, sm, fnw], writes=[orr])
                        r1 = b0 + c * 128
                        S.dma("sp", dict(out=yout[r1:r1 + 128, :], in_=orr.t[:]), reads=[orr])
                    S.barrier()
                    S.final_wait("sp")
                    flush()
    top.close()
    return nc


CFG_FULL = dict(NP=1024, NS=2048, BLK1=512, BLK2=512, BLK3=512, NW=3)


def _host_inputs(cfg, x_prompt, x_sample, norm_w, w_in, b_gate, dw_w, dw_b, ln_g, ln_b, sconv_w, sconv_b,
                 dt_bias, a_log, d_skip, ssm_norm_w, w_branch, w_out, final_norm_w):
    NP, NS, BLK1 = cfg["NP"], cfg["NS"], cfg["BLK1"]
    f = lambda a: np.ascontiguousarray(np.asarray(a, dtype=np.float32))
    xp = f(x_prompt)[0]
    xs = f(x_sample)[0]
    xP = np.zeros((xp.shape[0] + 2 * PAD, D), np.float32)
    xP[PAD:-PAD] = xp
    xS = np.zeros((xs.shape[0] + 2 * PAD, D), np.float32)
    xS[PAD:-PAD] = xs
    pp = np.zeros((128, NPP), np.float32)
    pp[:, PP_DWW:PP_DWW + 496] = f(dw_w)[0].reshape(31, 16, 128).transpose(2, 1, 0).reshape(128, 496)
    pp[:, PP_DWB:PP_DWB + 16] = f(dw_b)[0].reshape(16, 128).T
    pp[:, PP_LNG:PP_LNG + 16] = f(ln_g)[0].reshape(16, 128).T
    pp[:, PP_LNB:PP_LNB + 16] = f(ln_b)[0].reshape(16, 128).T
    pp[:, PP_SCW:PP_SCW + 240] = f(sconv_w)[0].reshape(5, 48, 128).transpose(2, 1, 0).reshape(128, 240)
    pp[:, PP_SCB:PP_SCB + 48] = f(sconv_b)[0].reshape(48, 128).T
    pp[:, PP_BG:PP_BG + 32] = f(b_gate)[0].reshape(32, 128).T
    pp[:, PP_SNW:PP_SNW + 32] = f(ssm_norm_w)[0].reshape(32, 128).T
    pb = np.zeros((1, NPB), np.float32)
    pb[0, PB_NW:PB_NW + D] = f(norm_w)[0]
    pb[0, PB_FNW:PB_FNW + D] = f(final_norm_w)
    pb[0, 4096 + PB_DSK:4096 + PB_DSK + 64] = f(d_skip)[0]
    pb[0, 4096 + PB_DTB:4096 + PB_DTB + 128] = f(dt_bias)[0].reshape(128)
    pb[0, 4096 + PB_ALOG:4096 + PB_ALOG + 128] = f(a_log)[0].reshape(128)
    consts = np.zeros((128, 512), np.float32)
    i = np.arange(128)
    consts[:, 0:128] = np.eye(128)
    consts[:, 128:256] = (i[:, None] <= i[None, :])
    consts[:, 256:384] = (i[:, None] >= i[None, :])
    consts[:, 384:512] = 1.0
    common = {"w_in": f(w_in)[0], "w_branch": f(w_branch)[0], "w_out": f(w_out)[0],
              "pp": pp, "pb": pb, "consts": consts}
    nb1p, nb1s = NCORES * NP // BLK1, NCORES * NS // BLK1
    in_maps = []
    for k in range(NCORES):
        m = dict(common)
        m["xoP"] = np.ascontiguousarray(xP[k * NP:(k + 1) * NP + 2 * PAD])
        m["xoS"] = np.ascontiguousarray(xS[k * NS:(k + 1) * NS + 2 * PAD])
        rows, mrow = [], []
        for (xpad, nb, nown) in ((xP, nb1p, NP // BLK1), (xS, nb1s, NS // BLK1)):
            for j in range(nb):
                if k * nown <= j < (k + 1) * nown:
                    continue
                r0 = PAD + j * BLK1 - 2
                rows.append(xpad[r0:r0 + BLK1 + 4])
                mf = 1.0 if j < k * nown else 0.0
                mrow += [mf, 1 - mf, 1 - mf, mf]
        m["xoth"] = np.ascontiguousarray(np.stack(rows, axis=0))
        msk = np.ascontiguousarray(np.broadcast_to(np.asarray(mrow, np.float32)[None, :], (128, len(mrow))))
        m["masks"] = msk
        in_maps.append(m)
    return in_maps


_NC_CACHE = {}


def run(cfg, **inputs):
    key = tuple(sorted(cfg.items()))
    if key not in _NC_CACHE:
        _NC_CACHE[key] = build(cfg)
    nc = _NC_CACHE[key]
    in_maps = _host_inputs(cfg, **inputs)
    res = run_bass_kernel_spmd(nc, in_maps, core_ids=list(range(NCORES)))
    yp = np.concatenate([np.asarray(r["yP"], dtype=np.float32) for r in res.results], axis=0)[None]
    ys = np.concatenate([np.asarray(r["yS"], dtype=np.float32) for r in res.results], axis=0)[None]
    return yp, ys


def kernel(**inputs):
    return run(CFG_FULL, **inputs)
```

```python
import numpy as np
from contextlib import ExitStack
import concourse.bass as bass
import concourse.mybir as mybir
from concourse.bass_utils import run_bass_kernel_spmd

F32 = mybir.dt.float32
BF16 = mybir.dt.bfloat16
AF = mybir.ActivationFunctionType
ALU = mybir.AluOpType

D = 2048
KD = 16
CONV_K = 31
SSM_DIM = 4096
NH = 64
HD = 64
NG = 8
DS = 128
OFF_CV = 0
OFF_CG = 2048
OFF_CS = 4096
OFF_Z = 6144
OFF_X = 10240
OFF_B = OFF_X + 4096
OFF_C = OFF_B + 1024
OFF_DT = 16384
OFF_GATE = 16512
IN_DIM = 20608
EPS = 1e-5
PAD = 16
NCORES = 8

PP_DWW = 0
PP_DWB = 496
PP_LNG = 512
PP_LNB = 528
PP_SCW = 544
PP_SCB = 784
PP_BG = 832
PP_SNW = 864
NPP = 896
PB_NW = 0
PB_FNW = 2048
PB_DSK = 0
PB_DTB = 64
PB_ALOG = 192
NPB = 4416


class Buf:
    __slots__ = ("w", "r")

    def __init__(self):
        self.w = None
        self.r = {}


class T:
    def __init__(self, t):
        self.t = t
        self.b = Buf()


class Sched:
    def __init__(self):
        self.ops = {n: [] for n in ("pe", "act", "dve", "pool", "sp")}
        self.cnt = {n: 0 for n in self.ops}
        self.seen = {n: {} for n in self.ops}
        self.dslots = {"sp": 8, "pool": 6}
        self.dcnt = {"sp": 0, "pool": 0}
        self.semkeys = list(self.ops.keys())
        for q, n in self.dslots.items():
            for i in range(n):
                self.semkeys.append(f"{q}_d{i}")
        self.semval = {k: 0 for k in self.semkeys}

    def _collect(self, eng, reads, writes):
        deps = {}

        def need(d):
            if d is None:
                return
            k, v = d
            if k == "pe" and eng == "pe":
                return
            if deps.get(k, 0) < v:
                deps[k] = v

        for b in reads:
            need(b.w)
        for b in writes:
            need(b.w)
            for d in b.r.values():
                need(d)
        return deps

    def _waits(self, eng, deps):
        seen = self.seen[eng]
        for k, v in deps.items():
            if seen.get(k, 0) < v:
                self.ops[eng].append(("wait", k, v))
                seen[k] = v

    def _mark(self, eng, d, reads, writes):
        for b in reads:
            o = b.r.get(d[0])
            if o is None or o[1] < d[1]:
                b.r[d[0]] = d
        for b in writes:
            b.w = d
            b.r = {}

    def op(self, eng, name, kw, reads=(), writes=(), inc=True):
        fn = (name, kw)
        reads = [x.b if isinstance(x, T) else x for x in reads]
        writes = [x.b if isinstance(x, T) else x for x in writes]
        self._waits(eng, self._collect(eng, reads, writes))
        if inc:
            self.cnt[eng] += 1
            self.ops[eng].append(("ins", fn, eng, 1))
            d = (eng, self.cnt[eng])
            self.semval[eng] = self.cnt[eng]
        else:
            self.ops[eng].append(("ins", fn, None, 0))
            d = (eng, self.cnt[eng] + 1)
        self._mark(eng, d, reads, writes)

    def dma(self, eng, kw, reads=(), writes=()):
        fn = ("dma_start", kw)
        reads = [x.b if isinstance(x, T) else x for x in reads]
        writes = [x.b if isinstance(x, T) else x for x in writes]
        i = self.dcnt[eng]
        self.dcnt[eng] += 1
        ns = self.dslots[eng]
        key = f"{eng}_d{i % ns}"
        deps = self._collect(eng, reads, writes)
        prev = 16 * (i // ns)
        if prev > 0 and deps.get(key, 0) < prev:
            deps[key] = prev
        self._waits(eng, deps)
        val = 16 * (i // ns + 1)
        self.ops[eng].append(("ins", fn, key, 16))
        self.semval[key] = val
        self._mark(eng, (key, val), reads, writes)

    def barrier(self):
        for eng in self.ops:
            deps = {k: v for k, v in self.semval.items() if v > 0 and k != eng}
            self._waits(eng, deps)

    def final_wait(self, eng):
        deps = {k: v for k, v in self.semval.items() if v > 0 and k != eng}
        self._waits(eng, deps)


def build(cfg):
    NP, NS = cfg["NP"], cfg["NS"]
    BLK1, BLK2, BLK3 = cfg["BLK1"], cfg["BLK2"], cfg["BLK3"]
    LP, LS = NCORES * NP, NCORES * NS
    NB1P, NB1S = LP // BLK1, LS // BLK1
    NCH_OWN = (NP + NS) // 128

    nc = bass.Bass("TRN2", target_bir_lowering=False)
    dram = {}

    def din(name, shape):
        dram[name] = nc.dram_tensor(name, list(shape), F32, kind="ExternalInput").ap()

    NBO_P = NB1P - NP // BLK1
    NBO_S = NB1S - NS // BLK1
    NM = 4 * (NBO_P + NBO_S)
    din("xoth", (NBO_P + NBO_S, BLK1 + 4, D))
    din("xoP", (NP + 2 * PAD, D))
    din("xoS", (NS + 2 * PAD, D))
    din("w_in", (D, IN_DIM))
    din("w_branch", (6144, D))
    din("w_out", (D, D))
    din("pp", (128, NPP))
    din("pb", (1, NPB))
    din("consts", (128, 512))
    din("masks", (128, NM))
    yP = nc.dram_tensor("yP", [NP, D], F32, kind="ExternalOutput").ap()
    yS = nc.dram_tensor("yS", [NS, D], F32, kind="ExternalOutput").ap()
    acc_scr = nc.dram_tensor("acc_scr", [4, 128, SSM_DIM], F32).ap()
    yf_scr = nc.dram_tensor("yf_scr", [NCH_OWN, 128, SSM_DIM], F32).ap()
    scrbuf = {"acc": [Buf() for _ in range(4)], "yf": [Buf() for _ in range(NCH_OWN)]}

    S = Sched()
    top = ExitStack()
    sems = {}
    for k in S.semkeys:
        sems[k] = top.enter_context(nc.semaphore(k))

    uid = [0]

    def sb(es, name, shape, dt=F32):
        uid[0] += 1
        return T(es.enter_context(nc.sbuf_tensor(f"{name}_{uid[0]}", list(shape), dt)))

    pools = {}
    rr = {}

    def mkpsum(es, **spec):
        assert sum(spec.values()) <= 8
        pools.clear()
        for kind, n in spec.items():
            if kind == "tp":
                pools[kind] = [T(es.enter_context(nc.psum_tensor(f"ps{kind}{i}_{uid[0]}", [128, 1024], BF16))) for i in range(n)]
            else:
                pools[kind] = [T(es.enter_context(nc.psum_tensor(f"ps{kind}{i}_{uid[0]}", [128, 512], F32))) for i in range(n)]
            uid[0] += 1
            rr[kind] = 0

    def psum(kind):
        lst = pools[kind]
        t = lst[rr[kind] % len(lst)]
        rr[kind] += 1
        return t

    class Ring:
        def __init__(self, items):
            self.items = items
            self.i = 0

        def next(self):
            t = self.items[self.i % len(self.items)]
            self.i += 1
            return t

    cst = sb(top, "cst", [128, 512])
    cstb = sb(top, "cstb", [128, 512], BF16)
    ppt = sb(top, "ppt", [128, NPP])
    pbt = sb(top, "pbt", [128, NPB - 4096])
    mskt = sb(top, "mskt", [128, NM])
    abc = sb(top, "abc", [128, 128])
    sch = sb(top, "sch", [128, 288])
    W = Ring([sb(top, f"W{i}", [128, KD, 512], BF16) for i in range(cfg.get("NW", 3))])

    IDF = lambda: cst.t[:, 0:128]
    TRIF = lambda: cst.t[:, 128:256]
    TRIB = lambda: cst.t[:, 256:384]
    ONES = lambda: cst.t[:, 384:512]
    IDB = lambda: cstb.t[:, 0:128]

    def flush(es_phase=None):
        with nc.Block() as blk:
            def rep(eng):
                def body(h):
                    for e in S.ops[eng]:
                        if e[0] == "wait":
                            h.wait_ge(sems[e[1]], e[2])
                        else:
                            ins = getattr(h, e[1][0])(**e[1][1])
                            if e[2] is not None:
                                ins.then_inc(sems[e[2]], e[3])
                    S.ops[eng] = []
                return body
            blk.tensor(rep("pe"))
            blk.scalar(rep("act"))
            blk.vector(rep("dve"))
            blk.gpsimd(rep("pool"))
            blk.sync(rep("sp"))

    S.dma("sp", dict(out=cst.t[:], in_=dram["consts"]), writes=[cst])
    S.dma("sp", dict(out=ppt.t[:], in_=dram["pp"]), writes=[ppt])
    S.dma("sp", dict(out=pbt.t[:], in_=dram["pb"][:, 4096:NPB].partition_broadcast(128)), writes=[pbt])
    S.dma("sp", dict(out=mskt.t[:], in_=dram["masks"]), writes=[mskt])
    S.op("dve", "tensor_copy", dict(out=cstb.t[:], in_=cst.t[:]), reads=[cst], writes=[cstb])
    S.op("act", "activation", dict(out=abc.t[:], in_=pbt.t[:, PB_ALOG:PB_ALOG + 128], func=AF.Exp),
         reads=[pbt], writes=[abc])
    S.op("dve", "tensor_scalar", dict(out=abc.t[:], in0=abc.t[:], scalar1=-1.0, scalar2=None, op0=ALU.mult),
         reads=[abc], writes=[abc])
    S.op("dve", "tensor_scalar", dict(out=sch.t[:], in0=ppt.t[:, PP_SCW:PP_SCW + 288], scalar1=0.5, scalar2=None, op0=ALU.mult),
         reads=[ppt], writes=[sch])

    def load_w(mat, r0, c0, ncols, coff=0, wt=None):
        if wt is None:
            wt = W.next()
        src = mat[r0:r0 + 2048, c0:c0 + ncols].rearrange("(k p) c -> p k c", p=128)
        S.dma("pool", dict(out=wt.t[:, :, coff:coff + ncols], in_=src), writes=[wt])
        return wt

    def load_norm_T(es, src, nrows, hT, st):
        xt_r, xn_r, sm_r = st["xt"], st["xn"], st["sm"]
        for r0 in range(0, nrows, 128):
            R = min(128, nrows - r0)
            xt = xt_r.next()
            xn = xn_r.next()
            sm = sm_r.next()
            S.dma("sp", dict(out=xt.t[0:R, :], in_=src[r0:r0 + R, :]), writes=[xt])
            S.op("act", "activation", dict(
                out=xn.t[0:R, :], in_=xt.t[0:R, :], func=AF.Square, accum_out=sm.t[0:R, 0:1]),
                reads=[xt], writes=[xn, sm])
            S.op("dve", "tensor_scalar", dict(
                out=sm.t[0:R, 1:2], in0=sm.t[0:R, 0:1], scalar1=1.0 / D, scalar2=EPS, op0=ALU.mult, op1=ALU.add),
                reads=[sm], writes=[sm])
            S.op("act", "activation", dict(out=sm.t[0:R, 2:3], in_=sm.t[0:R, 1:2], func=AF.Sqrt),
                 reads=[sm], writes=[sm])
            S.op("dve", "reciprocal", dict(out=sm.t[0:R, 3:4], in_=sm.t[0:R, 2:3]),
                 reads=[sm], writes=[sm])
            S.op("dve", "scalar_tensor_tensor", dict(
                out=xn.t[0:R, :], in0=xt.t[0:R, :], scalar=sm.t[0:R, 3:4], in1=st["nw"].t[0:R, :],
                op0=ALU.mult, op1=ALU.mult), reads=[xt, sm, st["nw"]], writes=[xn])
            for half in range(2):
                tp = psum("tp")
                for j in range(8):
                    k = half * 8 + j
                    S.op("pe", "transpose", dict(
                        out=tp.t[:, j * 128:j * 128 + R], in_=xn.t[0:R, k * 128:(k + 1) * 128],
                        identity=cstb.t[0:R, 0:R]), reads=[xn, cstb], writes=[tp], inc=(j == 7))
                eng = "act" if half == 0 else "dve"
                src_v = tp.t[:].rearrange("p (j t) -> p j t", j=8)[:, :, 0:R]
                dst_v = hT.t[:, half * 8:half * 8 + 8, r0:r0 + R]
                if eng == "act":
                    S.op("act", "activation", dict(out=dst_v, in_=src_v, func=AF.Copy),
                         reads=[tp], writes=[hT])
                else:
                    S.op("dve", "tensor_copy", dict(out=dst_v, in_=src_v),
                         reads=[tp], writes=[hT])

    def proj_fm(wt, co, hT, t0, n):
        ps = psum("mm")
        for k in range(KD):
            S.op("pe", "matmul", dict(out=ps.t[:, 0:n], lhsT=wt.t[:, k, co:co + 128], rhs=hT.t[:, k, t0:t0 + n],
                start=(k == 0), stop=(k == KD - 1)), reads=[wt, hT], writes=[ps], inc=(k == KD - 1))
        return ps

    def proj_tm(wt, co, ncols, hT, t0, ps, po):
        for k in range(KD):
            S.op("pe", "matmul", dict(out=ps.t[:, po:po + ncols], lhsT=hT.t[:, k, t0:t0 + 128], rhs=wt.t[:, k, co:co + ncols],
                start=(k == 0), stop=(k == KD - 1)), reads=[wt, hT], writes=[ps], inc=(k == KD - 1))

    def subblocks(t0, n):
        k = (n + 511) // 512
        out = []
        for i in range(k):
            a = (n * i) // k
            b = (n * (i + 1)) // k
            out.append((t0 + a, b - a))
        return out

    def conv5_silu(es_bufs, wt, co, hT, tlo, Tn, pidx, outT, out_off=0):
        pre = es_bufs["pre"].next()
        acc = es_bufs["acc"].next()
        for (t0, n) in subblocks(tlo - 2, Tn + 4):
            ps = proj_fm(wt, co, hT, t0, n)
            o = t0 - (tlo - 2)
            S.op("act", "activation", dict(out=pre.t[:, o:o + n], in_=ps.t[:, 0:n], func=AF.Copy),
                 reads=[ps], writes=[pre])
        wbase = pidx * 5
        S.op("dve", "tensor_scalar", dict(out=acc.t[:, 0:Tn], in0=pre.t[:, 0:Tn], scalar1=sch.t[:, wbase:wbase + 1],
                                          scalar2=sch.t[:, 240 + pidx:240 + pidx + 1], op0=ALU.mult, op1=ALU.add),
             reads=[pre, sch], writes=[acc])
        acc2 = es_bufs["acc"].next()
        S.op("dve", "tensor_scalar", dict(out=acc2.t[:, 0:Tn], in0=pre.t[:, 1:1 + Tn], scalar1=sch.t[:, wbase + 1:wbase + 2],
                                          scalar2=None, op0=ALU.mult), reads=[pre, sch], writes=[acc2])
        for k in range(2, 5):
            tg = acc if k % 2 == 0 else acc2
            S.op("dve", "scalar_tensor_tensor", dict(
                out=tg.t[:, 0:Tn], in0=pre.t[:, k:k + Tn], scalar=sch.t[:, wbase + k:wbase + k + 1],
                in1=tg.t[:, 0:Tn], op0=ALU.mult, op1=ALU.add), reads=[pre, sch, tg], writes=[tg])
        S.op("dve", "tensor_tensor", dict(out=acc.t[:, 0:Tn], in0=acc.t[:, 0:Tn], in1=acc2.t[:, 0:Tn], op=ALU.add),
             reads=[acc, acc2], writes=[acc])
        th = pre
        S.op("act", "activation", dict(out=th.t[:, 0:Tn], in_=acc.t[:, 0:Tn], func=AF.Tanh), reads=[acc], writes=[th])
        S.op("dve", "scalar_tensor_tensor", dict(out=outT.t[:, out_off:out_off + Tn], in0=th.t[:, 0:Tn], scalar=1.0,
                                                 in1=acc.t[:, 0:Tn], op0=ALU.add, op1=ALU.mult),
             reads=[th, acc], writes=[outT])

    def dt_block(hT, tlo, nch, db, want_T):
        wt = load_w(dram["w_in"], 0, OFF_DT, 128)
        for c4 in range(0, nch, 4):
            n4 = min(4, nch - c4)
            ps = psum("sm")
            for i in range(n4):
                proj_tm(wt, 0, 128, hT, tlo + (c4 + i) * 128, ps, i * 128)
            sl = lambda t: t.t[:, c4:c4 + n4, :]
            psv = ps.t[:, 0:n4 * 128].rearrange("p (c f) -> p c f", c=n4)
            bias_v = pbt.t[:, PB_DTB:PB_DTB + 128].unsqueeze(1).to_broadcast([128, n4, 128])
            a_v = abc.t[:].unsqueeze(1).to_broadcast([128, n4, 128])
            S.op("dve", "tensor_tensor", dict(out=sl(db["A"]), in0=psv, in1=bias_v, op=ALU.add),
                 reads=[ps, pbt], writes=[db["A"]])
            S.op("dve", "scalar_tensor_tensor", dict(out=sl(db["dt"]), in0=sl(db["A"]), scalar=-1.0, in1=sl(db["A"]),
                                                     op0=ALU.mult, op1=ALU.max),
                 reads=[db["A"]], writes=[db["dt"]])
            S.op("act", "activation", dict(out=sl(db["dt"]), in_=sl(db["dt"]), func=AF.Exp, scale=-1.0),
                 reads=[db["dt"]], writes=[db["dt"]])
            S.op("act", "activation", dict(out=sl(db["dt"]), in_=sl(db["dt"]), func=AF.Ln, bias=1.0),
                 reads=[db["dt"]], writes=[db["dt"]])
            S.op("dve", "scalar_tensor_tensor", dict(out=sl(db["dt"]), in0=sl(db["A"]), scalar=0.0, in1=sl(db["dt"]),
                                                         op0=ALU.max, op1=ALU.add),
                 reads=[db["A"], db["dt"]], writes=[db["dt"]])
            S.op("dve", "tensor_tensor", dict(out=sl(db["A"]), in0=sl(db["dt"]), in1=a_v, op=ALU.mult),
                 reads=[db["dt"], abc], writes=[db["A"]])
        for c4 in range(0, nch, 4):
            n4 = min(4, nch - c4)
            ps = psum("sm")
            ps2 = psum("mm")
            for i in range(n4):
                c = c4 + i
                S.op("pe", "matmul", dict(out=ps.t[:, i * 128:i * 128 + 64], lhsT=TRIF(), rhs=db["A"].t[:, c, 0:64],
                                                        start=True, stop=True), reads=[cst, db["A"]], writes=[ps], inc=False)
                S.op("pe", "matmul", dict(out=ps.t[:, i * 128 + 64:i * 128 + 128], lhsT=TRIB(),
                                                        rhs=db["A"].t[:, c, 64:128], start=True, stop=True),
                     reads=[cst, db["A"]], writes=[ps], inc=False)
                S.op("pe", "matmul", dict(out=ps2.t[:, i * 128:i * 128 + 128], lhsT=ONES(), rhs=db["A"].t[:, c, :],
                                                        start=True, stop=True), reads=[cst, db["A"]], writes=[ps2],
                     inc=(i == n4 - 1))
            S.op("act", "activation", dict(
                out=db["cs"].t[:, c4:c4 + n4, :], in_=ps.t[:, 0:n4 * 128].rearrange("p (c f) -> p c f", c=n4), func=AF.Copy),
                reads=[ps], writes=[db["cs"]])
            S.op("dve", "tensor_copy", dict(
                out=db["tot"].t[:, c4:c4 + n4, :], in_=ps2.t[:, 0:n4 * 128].rearrange("p (c f) -> p c f", c=n4)),
                reads=[ps2], writes=[db["tot"]])
            if want_T:
                ps3 = psum("ck")
                for i in range(n4):
                    c = c4 + i
                    S.op("pe", "matmul", dict(out=ps3.t[0:64, i * 128:(i + 1) * 128], lhsT=db["A"].t[:, c, 0:64],
                                                            rhs=TRIF(), start=True, stop=True),
                         reads=[cst, db["A"]], writes=[ps3], inc=False)
                    S.op("pe", "matmul", dict(out=ps3.t[64:128, i * 128:(i + 1) * 128],
                                                            lhsT=db["A"].t[:, c, 64:128], rhs=TRIB(), start=True, stop=True),
                         reads=[cst, db["A"]], writes=[ps3], inc=(i == n4 - 1))
                S.op("act", "activation", dict(
                    out=db["csT"].t[:, c4:c4 + n4, :], in_=ps3.t[:, 0:n4 * 128].rearrange("p (c f) -> p c f", c=n4),
                    func=AF.Copy), reads=[ps3], writes=[db["csT"]])
        if want_T:
            S.op("act", "activation", dict(out=db["ecs"].t[:], in_=db["cs"].t[:], func=AF.Exp),
                 reads=[db["cs"]], writes=[db["ecs"]])

    def tok_major(srcT, col0, nblk, c, dst, dcol0):
        tp = psum("tp")
        for i in range(nblk):
            S.op("pe", "transpose", dict(out=tp.t[:, i * 128:(i + 1) * 128],
                                                  in_=srcT[i].t[:, col0 + c * 128:col0 + (c + 1) * 128], identity=IDB()),
                 reads=[srcT[i], cstb], writes=[tp], inc=(i == nblk - 1))
        S.op("act", "activation", dict(out=dst.t[:, c, dcol0:dcol0 + nblk * 128], in_=tp.t[:, 0:nblk * 128], func=AF.Copy),
             reads=[tp], writes=[dst])

    def interleave(ga, gb, ratio=2):
        if not cfg.get("IL", 1):
            for g_ in (gb, ga):
                if g_ is not None:
                    for _ in g_:
                        pass
            return
        alive_a, alive_b = ga is not None, gb is not None
        while alive_a or alive_b:
            if alive_a:
                try:
                    next(ga)
                except StopIteration:
                    alive_a = False
            for _ in range(ratio):
                if alive_b:
                    try:
                        next(gb)
                    except StopIteration:
                        alive_b = False

    def nw_load(es, st, off):
        st["nw"] = sb(es, "nwbc", [128, D])
        S.dma("sp", dict(out=st["nw"].t[:], in_=dram["pb"][:, off:off + D].partition_broadcast(128)), writes=[st["nw"]])

    def group_set(es, nch, tag, need_C, need_xT=True):
        Tn = nch * 128
        pack = sb(es, f"gpack{tag}", [128, 7 * Tn], BF16)
        d = {"pack": pack,
             "xg_tok": T(pack.t[:, 0:4 * Tn].rearrange("p (c f) -> p c f", c=nch)),
             "Bg_tok": T(pack.t[:, 4 * Tn:5 * Tn].rearrange("p (c f) -> p c f", c=nch)),
             "BgT": T(pack.t[:, 5 * Tn:6 * Tn]),
             "CgT": T(pack.t[:, 6 * Tn:7 * Tn])}
        if need_xT:
            d["xgT"] = [sb(es, f"xgT{tag}{i}", [128, Tn], BF16) for i in range(4)]
        return d

    def conv_gen(cbufs, gs, g, hT, tlo, nch, need_C):
        Tn = nch * 128
        wx_ = load_w(dram["w_in"], 0, OFF_X + g * 512, 512)
        wbc = load_w(dram["w_in"], 0, OFF_B + g * 128, 128)
        if need_C:
            load_w(dram["w_in"], 0, OFF_C + g * 128, 128, coff=128, wt=wbc)
        for i in range(4):
            conv5_silu(cbufs, wx_, i * 128, hT, tlo, Tn, g * 4 + i, gs["xgT"][i])
            yield
        conv5_silu(cbufs, wbc, 0, hT, tlo, Tn, 32 + g, gs["BgT"])
        yield
        if need_C:
            conv5_silu(cbufs, wbc, 128, hT, tlo, Tn, 40 + g, gs["CgT"])
            yield
        for c in range(nch):
            tok_major(gs["xgT"], 0, 4, c, gs["xg_tok"], 0)
            tok_major([gs["BgT"]], 0, 1, c, gs["Bg_tok"], 0)
            if c % 2 == 1:
                yield

    NCH1 = BLK1 // 128
    TH1 = BLK1 + 4
    with ExitStack() as es:
        mkpsum(es, mm=3, tp=2, sm=1, ck=2)
        hT = sb(es, "hT1", [128, KD, TH1], BF16)
        st = {"xt": Ring([sb(es, f"xt{i}", [128, D]) for i in range(2)]),
              "xn": Ring([sb(es, f"xn{i}", [128, D], BF16) for i in range(2)]),
              "sm": Ring([sb(es, f"sm{i}", [128, 8]) for i in range(2)])}
        nw_load(es, st, PB_NW)
        cb = {"pre": Ring([sb(es, f"pre{i}", [128, TH1]) for i in range(2)]),
              "acc": Ring([sb(es, f"cacc{i}", [128, BLK1]) for i in range(2)])}
        db = {k: sb(es, "db_" + k, [128, NCH1, 128]) for k in ("dt", "A", "cs", "tot")}
        Et = sb(es, "Et", [128, NCH1, 128])
        wt_ = sb(es, "wt_", [128, NCH1, 128])
        Dt = sb(es, "Dt", [128, 128])
        aF = sb(es, "aF", [128, 64])
        cB = sb(es, "cB", [128, 64])
        pcum = sb(es, "pcum", [128, 64])
        tmpd = sb(es, "tmpd", [128, 64])
        gsets = [group_set(es, NCH1, f"a{i}", False) for i in range(2)]
        wx = Ring([sb(es, f"wx{i}", [128, 512], BF16) for i in range(4)])
        tmpL = Ring([sb(es, f"tmpL{i}", [128, 512]) for i in range(2)])
        RF = sb(es, "RF", [128, SSM_DIM])
        RB = sb(es, "RB", [128, SSM_DIM])
        v3 = lambda ap: ap.rearrange("p (h d) -> p h d", h=8)

        def tail_gen(gs, g, mfa, cBt, aFt):
            xg_tok, Bg_tok = gs["xg_tok"], gs["Bg_tok"]
            Lf = psum("ck")
            Lb = psum("ck")
            pend = None
            for c in range(NCH1 + 1):
                cur = None
                if c < NCH1:
                    cur = []
                    for di in (0, 1):
                        wxt = wx.next()
                        cur.append(wxt)
                        S.op("dve", "tensor_tensor", dict(
                            out=v3(wxt.t[:]), in0=v3(xg_tok.t[:, c, :]),
                            in1=wt_.t[:, c, di * 64 + g * 8:di * 64 + g * 8 + 8].unsqueeze(2).to_broadcast([128, 8, 64]),
                            op=ALU.mult), reads=[xg_tok, wt_], writes=[wxt])
                if pend is not None:
                    pc, pw = pend
                    for di, Lps in ((0, Lf), (1, Lb)):
                        S.op("pe", "matmul", dict(out=Lps.t[:, :], lhsT=Bg_tok.t[:, pc, :], rhs=pw[di].t[:],
                                                  start=(pc == 0), stop=(pc == NCH1 - 1)),
                             reads=[Bg_tok, pw[di]], writes=[Lps], inc=True)
                pend = (c, cur) if cur is not None else None
                yield
            gsl = slice(g * 512, (g + 1) * 512)
            S.op("dve", "tensor_tensor", dict(
                out=v3(RF.t[:, gsl]), in0=v3(RF.t[:, gsl]),
                in1=aFt.t[:, g * 8:g * 8 + 8].unsqueeze(2).to_broadcast([128, 8, 64]), op=ALU.mult),
                reads=[RF, aFt], writes=[RF])
            S.op("dve", "scalar_tensor_tensor", dict(
                out=RF.t[:, gsl], in0=Lf.t[:, :], scalar=mfa, in1=RF.t[:, gsl], op0=ALU.mult, op1=ALU.add),
                reads=[Lf, RF, mskt], writes=[RF])
            tl = tmpL.next()
            S.op("dve", "tensor_tensor", dict(
                out=v3(tl.t[:]), in0=v3(Lb.t[:, :]),
                in1=cBt.t[:, g * 8:g * 8 + 8].unsqueeze(2).to_broadcast([128, 8, 64]), op=ALU.mult),
                reads=[Lb, cBt], writes=[tl])
            S.op("dve", "tensor_tensor", dict(out=RB.t[:, gsl], in0=RB.t[:, gsl], in1=tl.t[:], op=ALU.add),
                 reads=[RB, tl], writes=[RB])

        blk_i = 0
        for si, nblk in enumerate((NBO_P, NBO_S) if cfg.get("P1", 1) else ()):
            S.op("dve", "memset", dict(ap=RF.t[:], constant=0.0), writes=[RF])
            S.op("dve", "memset", dict(ap=RB.t[:], constant=0.0), writes=[RB])
            S.op("dve", "memset", dict(ap=pcum.t[:], constant=1.0), writes=[pcum])
            for j in range(nblk):
                mc = 4 * blk_i
                mf = mskt.t[:, mc:mc + 1]
                omf = mskt.t[:, mc + 1:mc + 2]
                mb = mskt.t[:, mc + 2:mc + 3]
                omb = mskt.t[:, mc + 3:mc + 4]
                load_norm_T(es, dram["xoth"][blk_i], TH1, hT, st)
                blk_i += 1
                dt_block(hT, 2, NCH1, db, want_T=False)
                for c in range(NCH1):
                    ps = psum("sm")
                    for c2 in range(c, NCH1):
                        S.op("pe", "matmul", dict(out=ps.t[:, 0:64], lhsT=ONES(), rhs=db["A"].t[:, c2, 0:64],
                                                  start=(c2 == c), stop=(c2 == NCH1 - 1)),
                             reads=[cst, db["A"]], writes=[ps], inc=False)
                    for c2 in range(0, c + 1):
                        S.op("pe", "matmul", dict(out=ps.t[:, 64:128], lhsT=ONES(), rhs=db["A"].t[:, c2, 64:128],
                                                  start=(c2 == 0), stop=(c2 == c)),
                             reads=[cst, db["A"]], writes=[ps], inc=(c2 == c))
                    S.op("act", "activation", dict(out=Et.t[:, c, :], in_=ps.t[:, 0:128], func=AF.Copy),
                         reads=[ps], writes=[Et])
                S.op("dve", "tensor_tensor", dict(out=wt_.t[:], in0=Et.t[:], in1=db["cs"].t[:], op=ALU.subtract),
                     reads=[Et, db["cs"]], writes=[wt_])
                S.op("act", "activation", dict(out=wt_.t[:], in_=wt_.t[:], func=AF.Exp), reads=[wt_], writes=[wt_])
                S.op("dve", "tensor_tensor", dict(out=wt_.t[:], in0=wt_.t[:], in1=db["dt"].t[:], op=ALU.mult),
                     reads=[wt_, db["dt"]], writes=[wt_])
                S.op("act", "activation", dict(out=Dt.t[:, 0:64], in_=Et.t[:, 0, 0:64], func=AF.Exp),
                     reads=[Et], writes=[Dt])
                S.op("act", "activation", dict(out=Dt.t[:, 64:128], in_=Et.t[:, NCH1 - 1, 64:128], func=AF.Exp),
                     reads=[Et], writes=[Dt])
                S.op("dve", "tensor_scalar", dict(out=aF.t[:], in0=Dt.t[:, 0:64], scalar1=mf, scalar2=omf,
                                                  op0=ALU.mult, op1=ALU.add), reads=[Dt, mskt], writes=[aF])
                S.op("dve", "tensor_scalar", dict(out=cB.t[:], in0=pcum.t[:], scalar1=mb, scalar2=None, op0=ALU.mult),
                     reads=[pcum, mskt], writes=[cB])
                S.op("dve", "tensor_scalar", dict(out=tmpd.t[:], in0=Dt.t[:, 64:128], scalar1=mb, scalar2=omb,
                                                  op0=ALU.mult, op1=ALU.add), reads=[Dt, mskt], writes=[tmpd])
                S.op("dve", "tensor_tensor", dict(out=pcum.t[:], in0=pcum.t[:], in1=tmpd.t[:], op=ALU.mult),
                     reads=[pcum, tmpd], writes=[pcum])
                prev = None
                for g in range(NG):
                    gs = gsets[g % 2]
                    interleave(conv_gen(cb, gs, g, hT, 2, NCH1, False), prev, ratio=1)
                    prev = tail_gen(gs, g, mf, cB, aF)
                interleave(None, prev)
            S.dma("sp", dict(out=acc_scr[2 * si], in_=RF.t[:]), reads=[RF], writes=[scrbuf["acc"][2 * si]])
            S.dma("sp", dict(out=acc_scr[2 * si + 1], in_=RB.t[:]), reads=[RB], writes=[scrbuf["acc"][2 * si + 1]])
        S.barrier()
        flush()

    def chunk_gen(bufs, gs, g, di, nch, db, carry, chunk_cb):
        xg_tok, Bg_tok, BgT, CgT = gs["xg_tok"], gs["Bg_tok"], gs["BgT"], gs["CgT"]
        Sbf = bufs["Sbf"]
        S.op("act", "activation", dict(out=Sbf.t[:], in_=carry.t[:], func=AF.Copy), reads=[carry], writes=[Sbf])
        order = range(nch) if di == 0 else range(nch - 1, -1, -1)
        mask = TRIF if di == 0 else TRIB
        hb = di * 64 + g * 8
        v3 = lambda ap: ap.rearrange("p (h d) -> p h d", h=8)
        for c in order:
            cs_ = slice(c * 128, (c + 1) * 128)
            ps = psum("sm")
            S.op("pe", "matmul", dict(out=ps.t[:, 0:128], lhsT=BgT.t[:, cs_], rhs=CgT.t[:, cs_], start=True, stop=True),
                 reads=[BgT, CgT], writes=[ps])
            psss = []
            for hq in range(2):
                pss = psum("sel")
                psss.append(pss)
                for hh in range(4):
                    hi = hq * 4 + hh
                    S.op("pe", "matmul", dict(out=pss.t[:, hh * 128:(hh + 1) * 128],
                                              lhsT=cst.t[:, hb + hi:hb + hi + 1].to_broadcast([128, 128]),
                                              rhs=db["csT"].t[:, c, :], start=True, stop=True),
                         reads=[cst, db["csT"]], writes=[pss], inc=(hh == 3))
            yield
            CBm = bufs["CBm"].next()
            S.op("dve", "tensor_tensor", dict(out=CBm.t[:], in0=ps.t[:, 0:128], in1=mask(), op=ALU.mult),
                 reads=[ps, cst], writes=[CBm])
            ncs = bufs["ncs"].next()
            S.op("dve", "tensor_scalar", dict(out=ncs.t[:, 0:8], in0=db["cs"].t[:, c, hb:hb + 8], scalar1=-1.0, scalar2=None,
                                              op0=ALU.mult), reads=[db["cs"]], writes=[ncs])
            wv = bufs["wv"].next()
            S.op("dve", "tensor_tensor", dict(out=wv.t[:, 0:8], in0=db["tot"].t[:, c, hb:hb + 8],
                                              in1=db["cs"].t[:, c, hb:hb + 8], op=ALU.subtract),
                 reads=[db["tot"], db["cs"]], writes=[wv])
            S.op("act", "activation", dict(out=wv.t[:, 0:8], in_=wv.t[:, 0:8], func=AF.Exp), reads=[wv], writes=[wv])
            S.op("dve", "tensor_tensor", dict(out=wv.t[:, 0:8], in0=wv.t[:, 0:8], in1=db["dt"].t[:, c, hb:hb + 8], op=ALU.mult),
                 reads=[wv, db["dt"]], writes=[wv])
            S.op("act", "activation", dict(out=wv.t[:, 8:16], in_=db["tot"].t[:, c, hb:hb + 8], func=AF.Exp),
                 reads=[db["tot"]], writes=[wv])
            segs_ = []
            for hi in range(8):
                sg = bufs["seg"].next()
                segs_.append(sg)
                S.op("act", "activation", dict(out=sg.t[:], in_=psss[hi // 4].t[:, (hi % 4) * 128:(hi % 4 + 1) * 128],
                                               func=AF.Exp, bias=ncs.t[:, hi:hi + 1]),
                     reads=[psss[hi // 4], ncs], writes=[sg])
            xdt = bufs["wx"].next()
            S.op("dve", "tensor_tensor", dict(
                out=v3(xdt.t[:]), in0=v3(xg_tok.t[:, c, :]),
                in1=db["dt"].t[:, c, hb:hb + 8].unsqueeze(2).to_broadcast([128, 8, 64]), op=ALU.mult),
                reads=[xg_tok, db["dt"]], writes=[xdt])
            wxt = bufs["wx"].next()
            S.op("dve", "tensor_tensor", dict(
                out=v3(wxt.t[:]), in0=v3(xg_tok.t[:, c, :]), in1=wv.t[:, 0:8].unsqueeze(2).to_broadcast([128, 8, 64]),
                op=ALU.mult), reads=[xg_tok, wv], writes=[wxt])
            Mts = []
            for hi in range(8):
                Mt = bufs["M"].next()
                Mts.append(Mt)
                S.op("dve", "scalar_tensor_tensor", dict(out=Mt.t[:], in0=segs_[hi].t[:], scalar=1.0, in1=CBm.t[:],
                                                         op0=ALU.min, op1=ALU.mult),
                     reads=[segs_[hi], CBm], writes=[Mt])
            yield
            Yps = psum("ck")
            for hi in range(8):
                S.op("pe", "matmul", dict(out=Yps.t[:, hi * 64:(hi + 1) * 64], lhsT=Mts[hi].t[:],
                                          rhs=xdt.t[:, hi * 64:(hi + 1) * 64], start=True, stop=True),
                     reads=[Mts[hi], xdt], writes=[Yps], inc=(hi == 7))
            CSps = psum("ck")
            S.op("pe", "matmul", dict(out=CSps.t[:, :], lhsT=CgT.t[:, cs_], rhs=Sbf.t[:], start=True, stop=True),
                 reads=[CgT, Sbf], writes=[CSps])
            Lps = psum("sm")
            S.op("pe", "matmul", dict(out=Lps.t[:, :], lhsT=Bg_tok.t[:, c, :], rhs=wxt.t[:], start=True, stop=True),
                 reads=[Bg_tok, wxt], writes=[Lps])
            yield
            S.op("dve", "tensor_tensor", dict(
                out=v3(carry.t[:]), in0=v3(carry.t[:]), in1=wv.t[:, 8:16].unsqueeze(2).to_broadcast([128, 8, 64]),
                op=ALU.mult), reads=[carry, wv], writes=[carry])
            S.op("dve", "tensor_tensor", dict(out=carry.t[:], in0=carry.t[:], in1=Lps.t[:, :], op=ALU.add),
                 reads=[carry, Lps], writes=[carry])
            S.op("act", "activation", dict(out=Sbf.t[:], in_=carry.t[:], func=AF.Copy), reads=[carry], writes=[Sbf])
            yield from chunk_cb(c, Yps, CSps, hb)

    def ssd_bufs(es, nch, tag, nbuf=2):
        Tn = nch * 128
        return {
            "cb": {"pre": Ring([sb(es, f"pre{tag}{i}", [128, Tn + 4]) for i in range(nbuf)]),
                   "acc": Ring([sb(es, f"cacc{tag}{i}", [128, Tn]) for i in range(nbuf)])},
            "Sbf": sb(es, f"Sbf{tag}", [128, 512], BF16),
            "CBm": Ring([sb(es, f"CBm{tag}{i}", [128, 128]) for i in range(2)]),
            "ncs": Ring([sb(es, f"ncs{tag}{i}", [128, 8]) for i in range(2)]),
            "seg": Ring([sb(es, f"seg{tag}{i}", [128, 128]) for i in range(8)]),
            "M": Ring([sb(es, f"M{tag}{i}", [128, 128], BF16) for i in range(8)]),
            "wv": Ring([sb(es, f"wv{tag}{i}", [128, 16]) for i in range(2)]),
            "wx": Ring([sb(es, f"wx{tag}{i}", [128, 512], BF16) for i in range(4)]),
        }

    gsp = nc.dram_tensor("gsp", [(NP + NS) // BLK2 * NG, 128, 7 * BLK2], BF16).ap()
    gsp_buf = [Buf() for _ in range((NP + NS) // BLK2 * NG)]
    assert BLK2 == BLK3

    def ssd_block(bufs, gsets, di, hT, tlo, nch, db, carry_g, mk_cb, blk_idx, reload):
        prev = None
        for g in range(NG):
            gs = gsets[g % 2]
            parts = [gs["xg_tok"], gs["Bg_tok"], gs["BgT"], gs["CgT"]]
            idx = blk_idx * NG + g
            if reload:
                S.dma("sp", dict(out=gs["pack"].t[:], in_=gsp[idx]), reads=[gsp_buf[idx]], writes=[gs["pack"]] + parts)
                interleave(None, prev)
            else:
                interleave(conv_gen(bufs["cb"], gs, g, hT, tlo, nch, True), prev, ratio=2)
                S.dma("sp", dict(out=gsp[idx], in_=gs["pack"].t[:]), reads=[gs["pack"]] + parts, writes=[gsp_buf[idx]])
            prev = chunk_gen(bufs, gs, g, di, nch, db, carry_g[g], mk_cb(g, gs))
        interleave(None, prev)

    segs = (("xoP", NP, yP, 0, 0), ("xoS", NS, yS, NP // 128, 1))

    NCH2 = BLK2 // 128
    TH2 = BLK2 + 4
    with ExitStack() as es:
        mkpsum(es, mm=2, tp=1, sm=1, ck=2, sel=2)
        hT = sb(es, "hT2", [128, KD, TH2], BF16)
        st = {"xt": Ring([sb(es, f"xt2{i}", [128, D]) for i in range(2)]),
              "xn": Ring([sb(es, f"xn2{i}", [128, D], BF16) for i in range(2)]),
              "sm": Ring([sb(es, f"sm2{i}", [128, 8]) for i in range(2)])}
        nw_load(es, st, PB_NW)
        db = {k: sb(es, "db2_" + k, [128, NCH2, 128]) for k in ("dt", "A", "cs", "tot", "csT", "ecs")}
        bufs = ssd_bufs(es, NCH2, "p2")
        gsets = [group_set(es, NCH2, f"b{i}", True) for i in range(2)]
        carries = sb(es, "carries2", [128, SSM_DIM])
        carry_g = [T(carries.t[:, g * 512:(g + 1) * 512]) for g in range(NG)]
        yst = Ring([sb(es, f"yst{i}", [128, 512]) for i in range(3)])
        for (xname, ntok, yout, chbase, sidx) in (segs if cfg.get("P2", 1) else ()):
            S.dma("sp", dict(out=carries.t[:], in_=acc_scr[2 * sidx]),
                  reads=[scrbuf["acc"][2 * sidx]], writes=[carries] + carry_g)
            for b0 in range(0, ntok, BLK2):
                row0 = PAD + b0 - 2
                load_norm_T(es, dram[xname][row0:row0 + TH2, :], TH2, hT, st)
                dt_block(hT, 2, NCH2, db, want_T=True)

                def mk_cb(g, gs, b0=b0, chbase=chbase):
                    def cbk(c, Yps, CSps, hb):
                        y = yst.next()
                        S.op("act", "activation", dict(out=y.t[:], in_=Yps.t[:, :], func=AF.Copy), reads=[Yps], writes=[y])
                        for hi in range(8):
                            S.op("dve", "scalar_tensor_tensor", dict(
                                out=y.t[:, hi * 64:(hi + 1) * 64], in0=CSps.t[:, hi * 64:(hi + 1) * 64],
                                scalar=db["ecs"].t[:, c, hb + hi:hb + hi + 1], in1=y.t[:, hi * 64:(hi + 1) * 64],
                                op0=ALU.mult, op1=ALU.add), reads=[CSps, db["ecs"], y], writes=[y])
                        ch = chbase + b0 // 128 + c
                        S.dma("sp", dict(out=yf_scr[ch][:, g * 512:(g + 1) * 512], in_=y.t[:]),
                              reads=[y], writes=[scrbuf["yf"][ch]])
                        yield
                    return cbk
                ssd_block(bufs, gsets, 0, hT, 2, NCH2, db, carry_g, mk_cb, chbase * 128 // BLK2 + b0 // BLK2, False)
        S.barrier()
        flush()

    NCH3 = BLK3 // 128
    TH3 = BLK3 + 2 * PAD
    onesb = lambda: cstb.t[:, 384:512]
    with ExitStack() as es:
        hT = sb(es, "hT3", [128, KD, TH3], BF16)
        carries = sb(es, "carries3", [128, SSM_DIM])
        carry_g = [T(carries.t[:, g * 512:(g + 1) * 512]) for g in range(NG)]
        ysT = [sb(es, f"ysT{i}", [128, BLK3], BF16) for i in range(32)]
        ycT = [sb(es, f"ycT{i}", [128, BLK3], BF16) for i in range(16)]

        for (xname, ntok, yout, chbase, sidx) in (segs if cfg.get("P3", 1) else ()):
            S.dma("sp", dict(out=carries.t[:], in_=acc_scr[2 * sidx + 1]),
                  reads=[scrbuf["acc"][2 * sidx + 1]], writes=[carries] + carry_g)
            for b0 in range(ntok - BLK3, -1, -BLK3):
                with ExitStack() as e2:
                    mkpsum(e2, tp=2)
                    st = {"xt": Ring([sb(e2, f"xt3{i}", [128, D]) for i in range(2)]),
                          "xn": Ring([sb(e2, f"xn3{i}", [128, D], BF16) for i in range(2)]),
                          "sm": Ring([sb(e2, f"sm3{i}", [128, 8]) for i in range(2)])}
                    nw_load(e2, st, PB_NW)
                    load_norm_T(e2, dram[xname][b0:b0 + TH3, :], TH3, hT, st)
                    S.barrier()
                    flush()
                with ExitStack() as e2:
                    db = {k: sb(e2, "db3_" + k, [128, NCH3, 128]) for k in ("dt", "A", "cs", "tot", "csT", "ecs")}
                    mkpsum(e2, mm=2, tp=1, sm=1, ck=2, sel=2)
                    bufs = ssd_bufs(e2, NCH3, "p3", nbuf=2)
                    gsets = [group_set(e2, NCH3, f"c{i}", True, need_xT=False) for i in range(2)]
                    yb = Ring([sb(e2, f"yb{i}", [128, 512]) for i in range(1)])
                    yfl = Ring([sb(e2, f"yfl{i}", [128, 512]) for i in range(1)])
                    zs = Ring([sb(e2, f"zs{i}", [128, 512]) for i in range(1)])
                    ysn = Ring([sb(e2, f"ysn{i}", [128, 512], BF16) for i in range(2)])
                    sq = sb(e2, "sqj", [128, 512])
                    sm3 = Ring([sb(e2, f"sm3b{i}", [128, 8]) for i in range(2)])
                    dt_block(hT, PAD, NCH3, db, want_T=True)
                    def mk_cb(g, gs, b0=b0, chbase=chbase):
                      wz = load_w(dram["w_in"], 0, OFF_Z + g * 512, 512)

                      def cbk(c, Yps, CSps, hb):
                        if True:
                            y = yb.next()
                            yf = yfl.next()
                            ch = chbase + b0 // 128 + c
                            S.dma("sp", dict(out=yf.t[:], in_=yf_scr[ch][:, g * 512:(g + 1) * 512]),
                                  reads=[scrbuf["yf"][ch]], writes=[yf])
                            S.op("act", "activation", dict(out=y.t[:], in_=Yps.t[:, :], func=AF.Copy), reads=[Yps], writes=[y])
                            for hi in range(8):
                                S.op("dve", "scalar_tensor_tensor", dict(
                                    out=y.t[:, hi * 64:(hi + 1) * 64], in0=CSps.t[:, hi * 64:(hi + 1) * 64],
                                    scalar=db["ecs"].t[:, c, hb + hi:hb + hi + 1], in1=y.t[:, hi * 64:(hi + 1) * 64],
                                    op0=ALU.mult, op1=ALU.add), reads=[CSps, db["ecs"], y], writes=[y])
                            S.op("dve", "tensor_tensor", dict(out=y.t[:], in0=y.t[:], in1=yf.t[:], op=ALU.add),
                                 reads=[y, yf], writes=[y])
                            v3 = lambda ap: ap.rearrange("p (h d) -> p h d", h=8)
                            S.op("dve", "tensor_tensor", dict(
                                out=v3(yf.t[:]), in0=v3(gs["xg_tok"].t[:, c, :]),
                                in1=pbt.t[:, PB_DSK + g * 8:PB_DSK + g * 8 + 8].unsqueeze(2).to_broadcast([128, 8, 64]),
                                op=ALU.mult), reads=[gs["xg_tok"], pbt], writes=[yf])
                            S.op("dve", "tensor_tensor", dict(out=y.t[:], in0=y.t[:], in1=yf.t[:], op=ALU.add),
                                 reads=[y, yf], writes=[y])
                            yield
                            zp = psum("mm")
                            proj_tm(wz, 0, 512, hT, PAD + c * 128, zp, 0)
                            yield
                            z = zs.next()
                            S.op("act", "activation", dict(out=z.t[:], in_=zp.t[:, :], func=AF.Tanh, scale=0.5), reads=[zp], writes=[z])
                            S.op("dve", "scalar_tensor_tensor", dict(out=z.t[:], in0=z.t[:], scalar=1.0, in1=zp.t[:, :],
                                                                     op0=ALU.add, op1=ALU.mult), reads=[z, zp], writes=[z])
                            S.op("dve", "scalar_tensor_tensor", dict(out=y.t[:], in0=y.t[:], scalar=0.5, in1=z.t[:],
                                                                     op0=ALU.mult, op1=ALU.mult), reads=[y, z], writes=[y])
                            sm = sm3.next()
                            S.op("act", "activation", dict(out=sq.t[:], in_=y.t[:], func=AF.Square, accum_out=sm.t[:, 0:1]),
                                 reads=[y], writes=[sq, sm])
                            S.op("dve", "tensor_scalar", dict(out=sm.t[:, 1:2], in0=sm.t[:, 0:1], scalar1=1.0 / 512, scalar2=EPS,
                                                              op0=ALU.mult, op1=ALU.add), reads=[sm], writes=[sm])
                            S.op("act", "activation", dict(out=sm.t[:, 2:3], in_=sm.t[:, 1:2], func=AF.Sqrt), reads=[sm], writes=[sm])
                            S.op("dve", "reciprocal", dict(out=sm.t[:, 3:4], in_=sm.t[:, 2:3]), reads=[sm], writes=[sm])
                            yn = ysn.next()
                            S.op("dve", "tensor_scalar", dict(out=yn.t[:], in0=y.t[:], scalar1=sm.t[:, 3:4], scalar2=None,
                                                              op0=ALU.mult), reads=[y, sm], writes=[yn])
                            tp = psum("tp")
                            for i in range(4):
                                S.op("pe", "transpose", dict(out=tp.t[:, i * 128:(i + 1) * 128],
                                                             in_=yn.t[:, i * 128:(i + 1) * 128], identity=IDB()),
                                     reads=[yn, cstb], writes=[tp], inc=(i == 3))
                            for i in range(4):
                                cbi = g * 4 + i
                                S.op("dve", "tensor_scalar", dict(
                                    out=ysT[cbi].t[:, c * 128:(c + 1) * 128], in0=tp.t[:, i * 128:(i + 1) * 128],
                                    scalar1=ppt.t[:, PP_SNW + cbi:PP_SNW + cbi + 1], scalar2=None, op0=ALU.mult),
                                    reads=[tp, ppt], writes=[ysT[cbi]])
                            yield
                      return cbk
                    ssd_block(bufs, gsets, 1, hT, PAD, NCH3, db, carry_g, mk_cb, chbase * 128 // BLK3 + b0 // BLK3, True)
                    S.barrier()
                    flush()

                with ExitStack() as e2:
                    mkpsum(e2, mm=4, sm=2)
                    cpre = Ring([sb(e2, f"cpre{i}", [128, TH3]) for i in range(2)])
                    csig = Ring([sb(e2, f"csig{i}", [128, TH3]) for i in range(2)])
                    cacc = Ring([sb(e2, f"cacc3{i}", [128, BLK3]) for i in range(3)])
                    cacc2 = Ring([sb(e2, f"cacc4{i}", [128, BLK3]) for i in range(2)])
                    stat = sb(e2, "stat", [128, 2, BLK3])
                    usq = Ring([sb(e2, f"usq{i}", [128, BLK3], BF16) for i in range(2)])
                    gt = Ring([sb(e2, f"gt{i}", [128, BLK3]) for i in range(2)])
                    stp = [psum("sm"), psum("sm")]
                    for cbi in range(16):
                        if cbi % 4 == 0:
                            wcv = load_w(dram["w_in"], 0, OFF_CV + cbi * 128, 512)
                            wcg = load_w(dram["w_in"], 0, OFF_CG + cbi * 128, 512)
                        co = (cbi % 4) * 128
                        pre = cpre.next()
                        sg_ = csig.next()
                        for (t0, n) in subblocks(0, TH3):
                            ps = proj_fm(wcg, co, hT, t0, n)
                            S.op("act", "activation", dict(out=sg_.t[:, t0:t0 + n], in_=ps.t[:, 0:n], func=AF.Sigmoid),
                                 reads=[ps], writes=[sg_])
                            ps2 = proj_fm(wcv, co, hT, t0, n)
                            S.op("dve", "tensor_tensor", dict(out=pre.t[:, t0:t0 + n], in0=ps2.t[:, 0:n],
                                                              in1=sg_.t[:, t0:t0 + n], op=ALU.mult),
                                 reads=[ps2, sg_], writes=[pre])
                        acc = cacc.next()
                        acc2 = cacc2.next()
                        wb0 = PP_DWW + cbi * 31
                        S.op("dve", "tensor_scalar", dict(out=acc.t[:], in0=pre.t[:, 1:1 + BLK3], scalar1=ppt.t[:, wb0:wb0 + 1],
                                                          scalar2=ppt.t[:, PP_DWB + cbi:PP_DWB + cbi + 1], op0=ALU.mult,
                                                          op1=ALU.add), reads=[pre, ppt], writes=[acc])
                        S.op("dve", "tensor_scalar", dict(out=acc2.t[:], in0=pre.t[:, 2:2 + BLK3], scalar1=ppt.t[:, wb0 + 1:wb0 + 2],
                                                          scalar2=None, op0=ALU.mult), reads=[pre, ppt], writes=[acc2])
                        for k in range(2, CONV_K):
                            tg = acc if k % 2 == 0 else acc2
                            S.op("dve", "scalar_tensor_tensor", dict(
                                out=tg.t[:], in0=pre.t[:, 1 + k:1 + k + BLK3], scalar=ppt.t[:, wb0 + k:wb0 + k + 1],
                                in1=tg.t[:], op0=ALU.mult, op1=ALU.add), reads=[pre, ppt, tg], writes=[tg])
                        S.op("dve", "tensor_tensor", dict(out=acc.t[:], in0=acc.t[:], in1=acc2.t[:], op=ALU.add),
                             reads=[acc, acc2], writes=[acc])
                        S.op("act", "activation", dict(out=ycT[cbi].t[:], in_=acc.t[:], func=AF.Copy),
                             reads=[acc], writes=[ycT[cbi]])
                        us = usq.next()
                        S.op("act", "activation", dict(out=us.t[:], in_=ycT[cbi].t[:], func=AF.Square),
                             reads=[ycT[cbi]], writes=[us])
                        S.op("pe", "matmul", dict(out=stp[0].t[:, 0:BLK3], lhsT=onesb(), rhs=ycT[cbi].t[:],
                                                  start=(cbi == 0), stop=(cbi == 15)),
                             reads=[cstb, ycT[cbi]], writes=[stp[0]])
                        S.op("pe", "matmul", dict(out=stp[1].t[:, 0:BLK3], lhsT=onesb(), rhs=us.t[:],
                                                  start=(cbi == 0), stop=(cbi == 15)),
                             reads=[cstb, us], writes=[stp[1]])
                    S.op("dve", "tensor_scalar", dict(out=stat.t[:, 0, :], in0=stp[0].t[:, 0:BLK3], scalar1=1.0 / D, scalar2=None,
                                                      op0=ALU.mult), reads=[stp[0]], writes=[stat])
                    S.op("dve", "tensor_scalar", dict(out=stat.t[:, 1, :], in0=stp[1].t[:, 0:BLK3], scalar1=1.0 / D, scalar2=EPS,
                                                      op0=ALU.mult, op1=ALU.add), reads=[stp[1]], writes=[stat])
                    tmpm = cacc.next()
                    S.op("dve", "tensor_tensor", dict(out=tmpm.t[:], in0=stat.t[:, 0, :], in1=stat.t[:, 0, :], op=ALU.mult),
                         reads=[stat], writes=[tmpm])
                    S.op("dve", "tensor_tensor", dict(out=stat.t[:, 1, :], in0=stat.t[:, 1, :], in1=tmpm.t[:], op=ALU.subtract),
                         reads=[stat, tmpm], writes=[stat])
                    S.op("act", "activation", dict(out=stat.t[:, 1, :], in_=stat.t[:, 1, :], func=AF.Sqrt), reads=[stat], writes=[stat])
                    S.op("dve", "reciprocal", dict(out=stat.t[:, 1, :], in_=stat.t[:, 1, :]), reads=[stat], writes=[stat])
                    for cbi in range(16):
                        if cbi % 4 == 0:
                            wcs = load_w(dram["w_in"], 0, OFF_CS + cbi * 128, 512)
                        co = (cbi % 4) * 128
                        a_ = cacc.next()
                        S.op("dve", "tensor_tensor", dict(out=a_.t[:], in0=ycT[cbi].t[:], in1=stat.t[:, 0, :], op=ALU.subtract),
                             reads=[ycT[cbi], stat], writes=[a_])
                        S.op("dve", "tensor_tensor", dict(out=a_.t[:], in0=a_.t[:], in1=stat.t[:, 1, :], op=ALU.mult),
                             reads=[a_, stat], writes=[a_])
                        S.op("act", "activation", dict(out=a_.t[:], in_=a_.t[:], func=AF.Silu,
                                                       scale=ppt.t[:, PP_LNG + cbi:PP_LNG + cbi + 1],
                                                       bias=ppt.t[:, PP_LNB + cbi:PP_LNB + cbi + 1]),
                             reads=[a_, ppt], writes=[a_])
                        for (t0, n) in subblocks(0, BLK3):
                            ps = proj_fm(wcs, co, hT, PAD + t0, n)
                            g_ = gt.next()
                            S.op("act", "activation", dict(out=g_.t[:, 0:n], in_=ps.t[:, 0:n], func=AF.Silu),
                                 reads=[ps], writes=[g_])
                            S.op("dve", "tensor_tensor", dict(out=ycT[cbi].t[:, t0:t0 + n], in0=a_.t[:, t0:t0 + n],
                                                              in1=g_.t[:, 0:n], op=ALU.mult),
                                 reads=[a_, g_], writes=[ycT[cbi]])
                    S.barrier()
                    flush()

                with ExitStack() as e2:
                    mkpsum(e2, mm=6)
                    fnw = sb(e2, "fnw", [128, D])
                    S.dma("sp", dict(out=fnw.t[:], in_=dram["pb"][:, PB_FNW:PB_FNW + D].partition_broadcast(128)), writes=[fnw])
                    mT = [sb(e2, f"mT{i}", [128, BLK3], BF16) for i in range(16)]
                    o1s = [sb(e2, f"o1s{i}", [128, BLK3]) for i in range(4)]
                    gt = Ring([sb(e2, f"gtc{i}", [128, BLK3]) for i in range(2)])
                    ores = Ring([sb(e2, f"ores{i}", [128, D]) for i in range(1)])
                    xres = Ring([sb(e2, f"xres{i}", [128, D]) for i in range(2)])
                    sm3 = Ring([sb(e2, f"sm3c{i}", [128, 8]) for i in range(2)])
                    for dq in range(4):
                        wgc = load_w(dram["w_in"], 0, OFF_GATE + dq * 512, 512)
                        wb0_ = load_w(dram["w_branch"], 0, dq * 512, 512)
                        for dj in range(4):
                            dbi = dq * 4 + dj
                            co = dj * 128
                            for (t0, n) in subblocks(0, BLK3):
                                poc = psum("mm")
                                for k in range(16):
                                    S.op("pe", "matmul", dict(out=poc.t[:, 0:n], lhsT=wb0_.t[:, k, co:co + 128],
                                                              rhs=ycT[k].t[:, t0:t0 + n], start=(k == 0), stop=(k == 15)),
                                         reads=[wb0_, ycT[k]], writes=[poc], inc=(k == 15))
                                pgc = proj_fm(wgc, co, hT, PAD + t0, n)
                                g1 = gt.next()
                                S.op("act", "activation", dict(out=g1.t[:, 0:n], in_=pgc.t[:, 0:n], func=AF.Sigmoid,
                                                               bias=ppt.t[:, PP_BG + dbi:PP_BG + dbi + 1]),
                                     reads=[pgc, ppt], writes=[g1])
                                S.op("dve", "tensor_tensor", dict(out=o1s[dj].t[:, t0:t0 + n], in0=poc.t[:, 0:n],
                                                                  in1=g1.t[:, 0:n], op=ALU.mult),
                                     reads=[poc, g1], writes=[o1s[dj]])
                        wgs = load_w(dram["w_in"], 0, OFF_GATE + 2048 + dq * 512, 512)
                        wb1_ = load_w(dram["w_branch"], 2048, dq * 512, 512)
                        wb2_ = load_w(dram["w_branch"], 4096, dq * 512, 512)
                        wbs = [wb1_, wb2_]
                        for dj in range(4):
                            dbi = dq * 4 + dj
                            co = dj * 128
                            for (t0, n) in subblocks(0, BLK3):
                                pos = psum("mm")
                                for k in range(32):
                                    S.op("pe", "matmul", dict(out=pos.t[:, 0:n], lhsT=wbs[k // 16].t[:, k % 16, co:co + 128],
                                                              rhs=ysT[k].t[:, t0:t0 + n], start=(k == 0), stop=(k == 31)),
                                         reads=[wbs[k // 16], ysT[k]], writes=[pos], inc=(k == 31))
                                pgs = proj_fm(wgs, co, hT, PAD + t0, n)
                                g2 = gt.next()
                                S.op("act", "activation", dict(out=g2.t[:, 0:n], in_=pgs.t[:, 0:n], func=AF.Sigmoid,
                                                               bias=ppt.t[:, PP_BG + 16 + dbi:PP_BG + 16 + dbi + 1]),
                                     reads=[pgs, ppt], writes=[g2])
                                S.op("dve", "tensor_tensor", dict(out=g2.t[:, 0:n], in0=pos.t[:, 0:n], in1=g2.t[:, 0:n],
                                                                  op=ALU.mult), reads=[pos, g2], writes=[g2])
                                S.op("dve", "tensor_tensor", dict(out=mT[dbi].t[:, t0:t0 + n], in0=o1s[dj].t[:, t0:t0 + n],
                                                                  in1=g2.t[:, 0:n], op=ALU.add),
                                     reads=[o1s[dj], g2], writes=[mT[dbi]])
                    wos = None
                    for c in range(NCH3):
                        xr = xres.next()
                        r0 = PAD + b0 + c * 128
                        S.dma("sp", dict(out=xr.t[:], in_=dram[xname][r0:r0 + 128, :]), writes=[xr])
                        orr = ores.next()
                        for eq in range(4):
                            wo = load_w(dram["w_out"], 0, eq * 512, 512)
                            po = psum("mm")
                            for k in range(16):
                                S.op("pe", "matmul", dict(out=po.t[:, :], lhsT=mT[k].t[:, c * 128:(c + 1) * 128],
                                                          rhs=wo.t[:, k, :], start=(k == 0), stop=(k == 15)),
                                     reads=[wo, mT[k]], writes=[po], inc=(k == 15))
                            S.op("dve", "tensor_tensor", dict(out=orr.t[:, eq * 512:(eq + 1) * 512], in0=po.t[:, :],
                                                              in1=xr.t[:, eq * 512:(eq + 1) * 512], op=ALU.add),
                                 reads=[po, xr], writes=[orr])
                        sm = sm3.next()
                        S.op("act", "activation", dict(out=xr.t[:], in_=orr.t[:], func=AF.Square, accum_out=sm.t[:, 0:1]),
                             reads=[orr], writes=[xr, sm])
                        S.op("dve", "tensor_scalar", dict(out=sm.t[:, 1:2], in0=sm.t[:, 0:1], scalar1=1.0 / D, scalar2=EPS,
                                                          op0=ALU.mult, op1=ALU.add), reads=[sm], writes=[sm])
                        S.op("act", "activation", dict(out=sm.t[:, 2:3], in_=sm.t[:, 1:2], func=AF.Sqrt), reads=[sm], writes=[sm])
                        S.op("dve", "reciprocal", dict(out=sm.t[:, 3:4], in_=sm.t[:, 2:3]), reads=[sm], writes=[sm])
                        S.op("dve", "scalar_tensor_tensor", dict(out=orr.t[:], in0=orr.t[:], scalar=sm.t[:, 3:4],
                                                                 in1=fnw.t[:], op0=ALU.mult, op1=ALU.mult),
                             reads=[orr, sm, fnw], writes=[orr])
                        r1 = b0 + c * 128
                        S.dma("sp", dict(out=yout[r1:r1 + 128, :], in_=orr.t[:]), reads=[orr])
                    S.barrier()
                    S.final_wait("sp")
                    flush()
    top.close()
    return nc


CFG_FULL = dict(NP=1024, NS=2048, BLK1=512, BLK2=512, BLK3=512, NW=3)


def _host_inputs(cfg, x_prompt, x_sample, norm_w, w_in, b_gate, dw_w, dw_b, ln_g, ln_b, sconv_w, sconv_b,
                 dt_bias, a_log, d_skip, ssm_norm_w, w_branch, w_out, final_norm_w):
    NP, NS, BLK1 = cfg["NP"], cfg["NS"], cfg["BLK1"]
    f = lambda a: np.ascontiguousarray(np.asarray(a, dtype=np.float32))
    xp = f(x_prompt)[0]
    xs = f(x_sample)[0]
    xP = np.zeros((xp.shape[0] + 2 * PAD, D), np.float32)
    xP[PAD:-PAD] = xp
    xS = np.zeros((xs.shape[0] + 2 * PAD, D), np.float32)
    xS[PAD:-PAD] = xs
    pp = np.zeros((128, NPP), np.float32)
    pp[:, PP_DWW:PP_DWW + 496] = f(dw_w)[0].reshape(31, 16, 128).transpose(2, 1, 0).reshape(128, 496)
    pp[:, PP_DWB:PP_DWB + 16] = f(dw_b)[0].reshape(16, 128).T
    pp[:, PP_LNG:PP_LNG + 16] = f(ln_g)[0].reshape(16, 128).T
    pp[:, PP_LNB:PP_LNB + 16] = f(ln_b)[0].reshape(16, 128).T
    pp[:, PP_SCW:PP_SCW + 240] = f(sconv_w)[0].reshape(5, 48, 128).transpose(2, 1, 0).reshape(128, 240)
    pp[:, PP_SCB:PP_SCB + 48] = f(sconv_b)[0].reshape(48, 128).T
    pp[:, PP_BG:PP_BG + 32] = f(b_gate)[0].reshape(32, 128).T
    pp[:, PP_SNW:PP_SNW + 32] = f(ssm_norm_w)[0].reshape(32, 128).T
    pb = np.zeros((1, NPB), np.float32)
    pb[0, PB_NW:PB_NW + D] = f(norm_w)[0]
    pb[0, PB_FNW:PB_FNW + D] = f(final_norm_w)
    pb[0, 4096 + PB_DSK:4096 + PB_DSK + 64] = f(d_skip)[0]
    pb[0, 4096 + PB_DTB:4096 + PB_DTB + 128] = f(dt_bias)[0].reshape(128)
    pb[0, 4096 + PB_ALOG:4096 + PB_ALOG + 128] = f(a_log)[0].reshape(128)
    consts = np.zeros((128, 512), np.float32)
    i = np.arange(128)
    consts[:, 0:128] = np.eye(128)
    consts[:, 128:256] = (i[:, None] <= i[None, :])
    consts[:, 256:384] = (i[:, None] >= i[None, :])
    consts[:, 384:512] = 1.0
    common = {"w_in": f(w_in)[0], "w_branch": f(w_branch)[0], "w_out": f(w_out)[0],
              "pp": pp, "pb": pb, "consts": consts}
    nb1p, nb1s = NCORES * NP // BLK1, NCORES * NS // BLK1
    in_maps = []
    for k in range(NCORES):
        m = dict(common)
        m["xoP"] = np.ascontiguousarray(xP[k * NP:(k + 1) * NP + 2 * PAD])
        m["xoS"] = np.ascontiguousarray(xS[k * NS:(k + 1) * NS + 2 * PAD])
        rows, mrow = [], []
        for (xpad, nb, nown) in ((xP, nb1p, NP // BLK1), (xS, nb1s, NS // BLK1)):
            for j in range(nb):
                if k * nown <= j < (k + 1) * nown:
                    continue
                r0 = PAD + j * BLK1 - 2
                rows.append(xpad[r0:r0 + BLK1 + 4])
                mf = 1.0 if j < k * nown else 0.0
                mrow += [mf, 1 - mf, 1 - mf, mf]
        m["xoth"] = np.ascontiguousarray(np.stack(rows, axis=0))
        msk = np.ascontiguousarray(np.broadcast_to(np.asarray(mrow, np.float32)[None, :], (128, len(mrow))))
        m["masks"] = msk
        in_maps.append(m)
    return in_maps


_NC_CACHE = {}


def run(cfg, **inputs):
    key = tuple(sorted(cfg.items()))
    if key not in _NC_CACHE:
        _NC_CACHE[key] = build(cfg)
    nc = _NC_CACHE[key]
    in_maps = _host_inputs(cfg, **inputs)
    res = run_bass_kernel_spmd(nc, in_maps, core_ids=list(range(NCORES)))
    yp = np.concatenate([np.asarray(r["yP"], dtype=np.float32) for r in res.results], axis=0)[None]
    ys = np.concatenate([np.asarray(r["yS"], dtype=np.float32) for r in res.results], axis=0)[None]
    return yp, ys


def kernel(**inputs):
    return run(CFG_FULL, **inputs)
```
